# Optimizing a Trainium2 kernel written in Bass

```python
import jax, jax.numpy as jnp
from jax import lax
import numpy as np

D_MODEL = 1024
BATCH = 8
SEQ = 4096
DEPTH = 4

CTX_LEN = 256
GRID_W = 64
N_MOD = 9
D_FF = 2816
NORM_EPS = 1e-6
RWKV_HEAD = 64
D_RWKV = 3 * D_MODEL // 4
RWKV_HEADS = D_RWKV // RWKV_HEAD
DECAY_RANK = 64
ICLR_RANK = 64
GATE_RANK = 128
DECAY_SCALE = 0.606531
GN_EPS = 64e-5
D_FOURIER = D_MODEL // 4
FOURIER_GROUPS = 4
FOURIER_GROUP = D_FOURIER // FOURIER_GROUPS
D_A_IN = 3 * D_RWKV + DECAY_RANK + ICLR_RANK + GATE_RANK
D_EVEN_IN = D_A_IN + D_FOURIER
D_MIX_EVEN = D_RWKV + D_FOURIER
D_CONV = 3 * D_MODEL // 4
CONV_WIDTH = 31
POOL_WIDTHS = (2, 4, 8, 16)
D_POOL = D_MODEL // 4
POOL_GROUP = D_POOL // len(POOL_WIDTHS)
D_ODD_IN = 2 * D_CONV + D_POOL
D_MIX_ODD = D_CONV + D_POOL

kernel_name = 'hybrid_rwkv7_fnet_conformer_pool_dit'


def rmsnorm(x, g):
    xf = x.astype(jnp.float32)
    y = xf * lax.rsqrt(jnp.mean(xf * xf, axis=-1, keepdims=True) + NORM_EPS)
    return y.astype(x.dtype) * g


def modulate(h, g, shift, scale):
    return rmsnorm(h, g) * (1 + scale) + shift


def swiglu(y, w_gu, w_down):
    gate, up = jnp.split(y @ w_gu, 2, axis=-1)
    return (jax.nn.silu(gate) * up) @ w_down


def half_ffn(h, m, off, g, w_gu, w_down):
    y = modulate(h, g, m[off], m[off + 1])
    return h + 0.5 * m[off + 2] * swiglu(y, w_gu, w_down)


def to_column_major(u):
    b, n = u.shape[0], u.shape[1]
    rows = n // GRID_W
    return u.reshape(b, rows, GRID_W, -1).transpose(0, 2, 1, 3).reshape(b, n, -1)


def from_column_major(u):
    b, n = u.shape[0], u.shape[1]
    rows = n // GRID_W
    return u.reshape(b, GRID_W, rows, -1).transpose(0, 2, 1, 3).reshape(b, n, -1)


def token_shift(z, mu):
    zp = jnp.pad(z, ((0, 0), (1, 1), (0, 0)))
    return z + mu * (0.5 * (zp[:, :-2] + zp[:, 2:]) - z)


def fourier_mix(u):
    b, n, _ = u.shape
    uf = u.astype(jnp.float32).reshape(b, n, FOURIER_GROUPS, FOURIER_GROUP)
    out = jnp.fft.fft2(uf, axes=(1, 3), norm='ortho').real
    return out.reshape(b, n, D_FOURIER).astype(u.dtype)


def rwkv_features(zA, w0, w_up, a0, a_up, g_up, k_k, k_a):
    b, n, _ = zA.shape
    heads = lambda u: u.reshape(b, n, RWKV_HEADS, RWKV_HEAD)
    s = [D_RWKV, 2 * D_RWKV, 3 * D_RWKV, 3 * D_RWKV + DECAY_RANK, 3 * D_RWKV + DECAY_RANK + ICLR_RANK]
    r, k, v, wd, ad, gd = jnp.split(zA, s, axis=-1)
    g = jax.nn.sigmoid(gd) @ g_up
    kk = heads(k * k_k).astype(jnp.float32)
    kk = kk / jnp.maximum(jnp.sqrt(jnp.sum(kk * kk, axis=-1, keepdims=True)), 1e-12)
    dirs = []
    for d in range(2):
        w = jnp.exp(-DECAY_SCALE * jax.nn.sigmoid(w0[d] + jnp.tanh(wd) @ w_up[d]))
        a = jax.nn.sigmoid(a0[d] + ad @ a_up[d])
        kd = k * (1 + (a - 1) * k_a)
        dirs.append((heads(w), heads(kd), kk * heads(a).astype(jnp.float32)))
    return heads(r), heads(v), kk, g, dirs


def wkv_scan(S0, r, w, k, v, kk, bb, reverse):
    tm = lambda u: jnp.moveaxis(u.astype(jnp.float32), 1, 0)
    xs = (tm(r), tm(w), tm(k), tm(v), tm(kk), tm(bb))

    def step(S, inp):
        rt, wt, kt, vt, kkt, bt = inp
        sk = jnp.einsum('bhij,bhj->bhi', S, kkt)
        S = S * wt[:, :, None, :] - sk[..., None] * bt[:, :, None, :] + vt[..., None] * kt[:, :, None, :]
        return S, jnp.einsum('bhij,bhj->bhi', S, rt)

    S, ys = lax.scan(step, S0, xs, reverse=reverse)
    return S, jnp.moveaxis(ys, 0, 1)


def rwkv_output(y, r, kds, v, g, r_k, gn_w, gn_b):
    b, n = y.shape[0], y.shape[1]
    mu = jnp.mean(y, axis=-1, keepdims=True)
    var = jnp.mean(jnp.square(y - mu), axis=-1, keepdims=True)
    yn = ((y - mu) * lax.rsqrt(var + GN_EPS)).reshape(b, n, D_RWKV) * gn_w + gn_b
    rk = r_k.reshape(RWKV_HEADS, RWKV_HEAD)
    coef = jnp.sum(r * kds[0] * rk, axis=-1, keepdims=True) + jnp.sum(r * kds[1] * rk, axis=-1, keepdims=True)
    bonus = (coef * v).reshape(b, n, D_RWKV)
    return (yn.astype(g.dtype) + bonus) * g


def even_mixer(y_ctx, y_lat, col_major, w_in, mu, w0, w_up, a0, a_up, g_up, k_k, k_a, r_k, gn_w, gn_b, w_out):
    z_ctx = y_ctx @ w_in
    z_lat = y_lat @ w_in
    f_ctx = fourier_mix(z_ctx[..., D_A_IN:])
    f_lat = fourier_mix(z_lat[..., D_A_IN:])
    zA_ctx = z_ctx[..., :D_A_IN]
    zA_lat = z_lat[..., :D_A_IN]
    if col_major:
        zA_lat = to_column_major(zA_lat)
    zA_ctx = token_shift(zA_ctx, mu)
    zA_lat = token_shift(zA_lat, mu)
    fargs = (w0, w_up, a0, a_up, g_up, k_k, k_a)
    r_c, v_c, kk_c, g_c, dirs_c = rwkv_features(zA_ctx, *fargs)
    r_l, v_l, kk_l, g_l, dirs_l = rwkv_features(zA_lat, *fargs)
    S0 = jnp.zeros((y_ctx.shape[0], RWKV_HEADS, RWKV_HEAD, RWKV_HEAD), jnp.float32)
    ys_c, ys_l = [], []
    for d in range(2):
        w_c, kd_c, b_c = dirs_c[d]
        w_l, kd_l, b_l = dirs_l[d]
        S_c, yc = wkv_scan(S0, r_c, w_c, kd_c, v_c, kk_c, b_c, reverse=(d == 1))
        _, yl = wkv_scan(S_c, r_l, w_l, kd_l, v_l, kk_l, b_l, reverse=(d == 1))
        ys_c.append(yc)
        ys_l.append(yl)
    o_ctx = rwkv_output(ys_c[0] + ys_c[1], r_c, (dirs_c[0][1], dirs_c[1][1]), v_c, g_c, r_k, gn_w, gn_b)
    o_lat = rwkv_output(ys_l[0] + ys_l[1], r_l, (dirs_l[0][1], dirs_l[1][1]), v_l, g_l, r_k, gn_w, gn_b)
    if col_major:
        o_lat = from_column_major(o_lat)
    out_ctx = jnp.concatenate([o_ctx, f_ctx], axis=-1) @ w_out
    out_lat = jnp.concatenate([o_lat, f_lat], axis=-1) @ w_out
    return out_ctx, out_lat


def centered_pool_minus_self(u, width):
    n = u.shape[1]
    uf = u.astype(jnp.float32)
    cs = jnp.pad(jnp.cumsum(uf, axis=1), ((0, 0), (1, 0), (0, 0)))
    t = jnp.arange(n)
    lo = jnp.maximum(t - width // 2, 0)
    hi = jnp.minimum(t + (width - 1 - width // 2), n - 1)
    total = jnp.take(cs, hi + 1, axis=1) - jnp.take(cs, lo, axis=1)
    mean = total / (hi - lo + 1).astype(jnp.float32)[None, :, None]
    return (mean - uf).astype(u.dtype)


def odd_mixer(y, w_in, conv_w, conv_b, cnorm_g, pool_w, pool_scale, w_out):
    b, n, _ = y.shape
    z = y @ w_in
    ga, gb, q = jnp.split(z, [D_CONV, 2 * D_CONV], axis=-1)
    u = ga * jax.nn.sigmoid(gb)
    u = lax.conv_general_dilated(u, conv_w[:, None, :], window_strides=(1,),
                                 padding=[(CONV_WIDTH // 2, CONV_WIDTH // 2)],
                                 dimension_numbers=('NWC', 'WIO', 'NWC'),
                                 feature_group_count=D_CONV) + conv_b
    u = jax.nn.silu(rmsnorm(u, cnorm_g))
    qg = jnp.split(q, len(POOL_WIDTHS), axis=-1)
    p = jnp.stack([centered_pool_minus_self(qi, wd) for qi, wd in zip(qg, POOL_WIDTHS)], axis=2)
    p = jnp.einsum('bngc,gcd->bngd', p, pool_w).reshape(b, n, D_POOL) * pool_scale
    return jnp.concatenate([u, p], axis=-1) @ w_out


def setup_inputs(seed: int = 0) -> dict:
    key = jax.random.key(seed)
    ks = iter(jax.random.split(key, 40))
    nrm = lambda shape, scale: scale * jax.random.normal(next(ks), shape, jnp.float32)
    ne, no = (DEPTH + 1) // 2, DEPTH // 2
    D = D_MODEL
    return {
        'x': nrm((BATCH, SEQ, D), 1.0),
        'c': nrm((BATCH, D), 1.0),
        'ctx': nrm((BATCH, CTX_LEN, D), 1.0),
        'c_ctx': nrm((D,), 1.0),
        'ada_w': nrm((DEPTH, D, N_MOD * D), 0.5 * D ** -0.5),
        'ada_b': nrm((DEPTH, N_MOD * D), 0.02),
        'norm_g': 1.0 + nrm((DEPTH, 3, D), 0.05),
        'ffn1_w_gu': nrm((DEPTH, D, 2 * D_FF), D ** -0.5),
        'ffn1_w_down': nrm((DEPTH, D_FF, D), D_FF ** -0.5),
        'ffn2_w_gu': nrm((DEPTH, D, 2 * D_FF), D ** -0.5),
        'ffn2_w_down': nrm((DEPTH, D_FF, D), D_FF ** -0.5),
        'e_w_in': nrm((ne, D, D_EVEN_IN), D ** -0.5),
        'e_mu': jax.random.uniform(next(ks), (ne, D_A_IN), jnp.float32),
        'e_w0': -1.0 + nrm((ne, 2, D_RWKV), 0.5),
        'e_w_up': nrm((ne, 2, DECAY_RANK, D_RWKV), DECAY_RANK ** -0.5),
        'e_a0': nrm((ne, 2, D_RWKV), 0.5),
        'e_a_up': nrm((ne, 2, ICLR_RANK, D_RWKV), ICLR_RANK ** -0.5),
        'e_g_up': nrm((ne, GATE_RANK, D_RWKV), GATE_RANK ** -0.5),
        'e_k_k': 1.0 + nrm((ne, D_RWKV), 0.1),
        'e_k_a': 1.0 + nrm((ne, D_RWKV), 0.1),
        'e_r_k': nrm((ne, D_RWKV), 0.1),
        'e_gn_w': 1.0 + nrm((ne, D_RWKV), 0.05),
        'e_gn_b': nrm((ne, D_RWKV), 0.02),
        'e_w_out': nrm((ne, D_MIX_EVEN, D), D_MIX_EVEN ** -0.5),
        'o_w_in': nrm((no, D, D_ODD_IN), D ** -0.5),
        'o_conv_w': nrm((no, CONV_WIDTH, D_CONV), CONV_WIDTH ** -0.5),
        'o_conv_b': nrm((no, D_CONV), 0.02),
        'o_cnorm_g': 1.0 + nrm((no, D_CONV), 0.05),
        'o_pool_w': nrm((no, len(POOL_WIDTHS), POOL_GROUP, POOL_GROUP), POOL_GROUP ** -0.5),
        'o_pool_scale': 1.0 + nrm((no, D_POOL), 0.1),
        'o_w_out': nrm((no, D_MIX_ODD, D), D_MIX_ODD ** -0.5),
        'final_g': 1.0 + nrm((D,), 0.05),
    }


def reference(x, c, ctx, c_ctx, ada_w, ada_b, norm_g, ffn1_w_gu, ffn1_w_down, ffn2_w_gu, ffn2_w_down,
              e_w_in, e_mu, e_w0, e_w_up, e_a0, e_a_up, e_g_up, e_k_k, e_k_a, e_r_k, e_gn_w, e_gn_b, e_w_out,
              o_w_in, o_conv_w, o_conv_b, o_cnorm_g, o_pool_w, o_pool_scale, o_w_out, final_g):
    h_lat, h_ctx = x, ctx
    for i in range(DEPTH):
        j = i // 2
        with_ctx = not (i == DEPTH - 1 and i % 2 == 1)
        m_lat = jnp.split((jax.nn.silu(c) @ ada_w[i] + ada_b[i])[:, None, :], N_MOD, axis=-1)
        if with_ctx:
            m_ctx = jnp.split((jax.nn.silu(c_ctx) @ ada_w[i] + ada_b[i])[None, None, :], N_MOD, axis=-1)
            h_ctx = half_ffn(h_ctx, m_ctx, 0, norm_g[i, 0], ffn1_w_gu[i], ffn1_w_down[i])
        h_lat = half_ffn(h_lat, m_lat, 0, norm_g[i, 0], ffn1_w_gu[i], ffn1_w_down[i])
        y_lat = modulate(h_lat, norm_g[i, 1], m_lat[3], m_lat[4])
        if i % 2 == 0:
            y_ctx = modulate(h_ctx, norm_g[i, 1], m_ctx[3], m_ctx[4])
            o_ctx, o_lat = even_mixer(y_ctx, y_lat, j % 2 == 1, e_w_in[j], e_mu[j], e_w0[j], e_w_up[j],
                                      e_a0[j], e_a_up[j], e_g_up[j], e_k_k[j], e_k_a[j], e_r_k[j],
                                      e_gn_w[j], e_gn_b[j], e_w_out[j])
            h_ctx = h_ctx + m_ctx[5] * o_ctx
            h_lat = h_lat + m_lat[5] * o_lat
        else:
            odd_args = (o_w_in[j], o_conv_w[j], o_conv_b[j], o_cnorm_g[j], o_pool_w[j], o_pool_scale[j], o_w_out[j])
            h_lat = h_lat + m_lat[5] * odd_mixer(y_lat, *odd_args)
            if with_ctx:
                y_ctx = modulate(h_ctx, norm_g[i, 1], m_ctx[3], m_ctx[4])
                h_ctx = h_ctx + m_ctx[5] * odd_mixer(y_ctx, *odd_args)
        if with_ctx:
            h_ctx = half_ffn(h_ctx, m_ctx, 6, norm_g[i, 2], ffn2_w_gu[i], ffn2_w_down[i])
        h_lat = half_ffn(h_lat, m_lat, 6, norm_g[i, 2], ffn2_w_gu[i], ffn2_w_down[i])
    return rmsnorm(h_lat, final_g)
```

```python
import numpy as np
from contextlib import ExitStack
import concourse.bass as bass
import concourse.mybir as mybir
from concourse.bass_utils import run_bass_kernel_spmd

F32 = mybir.dt.float32
BF16 = mybir.dt.bfloat16
ALU = mybir.AluOpType
AF = mybir.ActivationFunctionType
AX = mybir.AxisListType

D = 1024
SEQ = 4096
CTX = 256
NT = SEQ + CTX
DEPTH = 4
DFF = 2816
NMOD = 9
EPS = 1e-6

EPOCH = 30000
NDMA = 8
DMA_EPOCH = 1800


class Buf:
    __slots__ = ("name", "w", "r")

    def __init__(self, name):
        self.name = name
        self.w = None
        self.r = {}


class Prog:
    ENG = ("pe", "act", "dve", "pool", "sp")

    def __init__(self, nc):
        self.nc = nc
        self.q = {e: [] for e in self.ENG}
        self.cnt = {e: 0 for e in self.ENG}
        self.dcnt = {e: 0 for e in self.ENG}
        self.waited = {e: {} for e in self.ENG}
        self.semkeys = []
        self.semset = set()
        self.sems = {}
        self.nbuf = 0

    def buf(self, name=None):
        self.nbuf += 1
        return Buf(name or f"b{self.nbuf}")

    def bufs(self, n, name="b"):
        return [self.buf(f"{name}{i}") for i in range(n)]

    def _key(self, k):
        if k not in self.semset:
            self.semset.add(k)
            self.semkeys.append(k)
        return k

    def _collect(self, eng, reads, writes, is_dma):
        need = {}

        def add(tok, raw):
            teng, tdma, key, val = tok
            if (not is_dma) and (not tdma) and teng == eng and not raw and eng == "pe":
                return
            if self.waited[eng].get(key, 0) >= val:
                return
            if need.get(key, 0) < val:
                need[key] = val

        for b in reads:
            if b.w is not None:
                add(b.w, True)
        for b in writes:
            if b.w is not None:
                add(b.w, False)
            for t in b.r.values():
                add(t, False)
        return need

    def _register(self, tok, reads, writes):
        teng, tdma, key, val = tok
        rk = key if tdma else teng
        for b in reads:
            b.r[rk] = tok
        for b in writes:
            b.w = tok
            b.r = {}

    def op(self, eng, fn, r=(), w=()):
        need = self._collect(eng, r, w, False)
        for k, v in need.items():
            self.waited[eng][k] = v
        self.cnt[eng] += 1
        k = self.cnt[eng] - 1
        key = self._key((eng, k // EPOCH))
        val = k % EPOCH + 1
        tok = (eng, False, key, val)
        self.waited[eng][key] = max(self.waited[eng].get(key, 0), 0)
        self._register(tok, r, w)
        waits = list(need.items())
        sems = self.sems

        def run(e):
            for wk, wv in waits:
                e.wait_ge(sems[wk], wv)
            fn(e).then_inc(sems[key], 1)

        self.q[eng].append(run)
        return tok

    def dma(self, eng, out, in_, r=(), w=(), **kw):
        need = self._collect(eng, r, w, True)
        i = self.dcnt[eng]
        self.dcnt[eng] += 1
        st = i // (NDMA * DMA_EPOCH)
        j = i % (NDMA * DMA_EPOCH)
        slot = j % NDMA
        rnd = j // NDMA
        key = self._key(("dma", eng, st, slot))
        if rnd > 0 and self.waited[eng].get(key, 0) < 16 * rnd:
            need[key] = max(need.get(key, 0), 16 * rnd)
        for k, v in need.items():
            self.waited[eng][k] = v
        val = 16 * (rnd + 1)
        tok = (eng, True, key, val)
        self._register(tok, r, w)
        waits = list(need.items())
        sems = self.sems

        def run(e):
            for wk, wv in waits:
                e.wait_ge(sems[wk], wv)
            e.dma_start(out=out, in_=in_, **kw).then_inc(sems[key], 16)

        self.q[eng].append(run)
        return tok

    def finish(self, eng, toks):
        waits = []
        for t in toks:
            waits.append((t[2], t[3]))
        sems = self.sems

        def run(e):
            for wk, wv in waits:
                e.wait_ge(sems[wk], wv)

        self.q[eng].append(run)

    def emit(self, stack):
        nc = self.nc
        for i, k in enumerate(self.semkeys):
            self.sems[k] = stack.enter_context(nc.semaphore(f"s{i}"))
        block = stack.enter_context(nc.Block())
        q = self.q

        @block.tensor
        def _(e):
            for f in q["pe"]:
                f(e)

        @block.scalar
        def _(e):
            for f in q["act"]:
                f(e)

        @block.vector
        def _(e):
            for f in q["dve"]:
                f(e)

        @block.gpsimd
        def _(e):
            for f in q["pool"]:
                f(e)

        @block.sync
        def _(e):
            for f in q["sp"]:
                f(e)


def _prog_barrier(self):
    toks = []
    for e in self.ENG:
        if self.cnt[e] > 0:
            k = self.cnt[e] - 1
            toks.append(((e, k // EPOCH), k % EPOCH + 1))
        n = self.dcnt[e]
        if n > 0:
            for back in range(min(n, NDMA)):
                i = n - 1 - back
                st = i // (NDMA * DMA_EPOCH)
                j = i % (NDMA * DMA_EPOCH)
                toks.append((("dma", e, st, j % NDMA), 16 * (j // NDMA + 1)))
    sems = self.sems
    for e in self.ENG:
        waits = []
        for key, val in toks:
            if self.waited[e].get(key, 0) >= val:
                continue
            self.waited[e][key] = val
            waits.append((key, val))

        def run(eng, waits=waits):
            for wk, wv in waits:
                eng.wait_ge(sems[wk], wv)

        self.q[e].append(run)


Prog.barrier = _prog_barrier


TT = 256
NTILES = NT // TT
DIN_E = 2816
DIN_O = 1792


_UNIQ = [0]


def mk_sb(nc, st):
    def sb(name, shape, dt=F32):
        _UNIQ[0] += 1
        return st.enter_context(nc.sbuf_tensor(f"{name}_{_UNIQ[0]}", list(shape), dt))
    return sb


class KC:
    pass


def declare_io(nc, debug):
    C = KC()
    C.nc = nc
    di = lambda name, shape: nc.dram_tensor(name, list(shape), F32, kind="ExternalInput").ap()
    C.xT = di("xT", [D, NT])
    C.cv = di("cv", [128, 16])
    C.ada_w = di("ada_w", [DEPTH, D, NMOD * D])
    C.ada_b = di("ada_b", [128, DEPTH * 72])
    C.norm_g = di("norm_g", [128, DEPTH * 3 * 8])
    C.final_g = di("final_g", [128, 8])
    C.ident = di("ident", [128, 128])
    C.ffn_gu = [di("ffn1_w_gu", [DEPTH, D, 2 * DFF]), di("ffn2_w_gu", [DEPTH, D, 2 * DFF])]
    C.ffn_dn = [di("ffn1_w_down", [DEPTH, DFF, D]), di("ffn2_w_down", [DEPTH, DFF, D])]
    C.outT = nc.dram_tensor("outT", [D, SEQ], F32, kind="ExternalOutput").ap()
    C.hT = nc.dram_tensor("hT", [D, NT], F32).ap()
    if debug:
        C.dbg = nc.dram_tensor("dbg", [D, NT], F32, kind="ExternalOutput").ap()
    return C


def fm(ap2d, t0, n):
    return ap2d.rearrange("(c p) t -> p c t", p=128)[:, :, t0:t0 + n]


def setup_persistent(C, P, st):
    nc = C.nc
    sb = mk_sb(nc, st)
    C.ps = [st.enter_context(nc.psum_tensor(f"ps{i}", [128, 512], F32)) for i in range(8)]
    C.bps = P.bufs(8, "ps")
    C.ident_sb = sb("ident_sb", [128, 128])
    C.ones_bf = sb("ones_bf", [128, 128], BF16)
    C.mod = sb("mod", [128, DEPTH, 72, 2])
    C.mA = sb("mA", [128, DEPTH, 3, 2, 8])
    C.mB = sb("mB", [128, DEPTH, 3, 2, 8])
    C.mG = sb("mG", [128, DEPTH, 3, 2, 8])
    C.g_sb = sb("g_sb", [128, DEPTH, 3, 8])
    C.fg_sb = sb("fg_sb", [128, 8])
    C.eps_sb = sb("eps_sb", [128, 1])
    C.b_const = P.buf("const")
    C.b_mod = P.buf("mod")
    C.hbuf = P.bufs(NTILES, "hT")
    C.xbuf = P.bufs(NTILES, "xT")
    C.obuf = P.bufs(NTILES, "outT")
    P.dma("sp", C.ident_sb[:], C.ident[:, :], w=[C.b_const])
    P.dma("sp", C.g_sb[:].rearrange("p a b c -> p (a b c)"), C.norm_g[:, :], w=[C.b_const])
    P.dma("sp", C.fg_sb[:], C.final_g[:, :], w=[C.b_const])
    P.op("dve", lambda e: e.memset(C.ones_bf[:], 1.0), w=[C.b_const])
    P.op("dve", lambda e: e.memset(C.eps_sb[:], EPS), w=[C.b_const])


def stage_ada(C, P):
    nc = C.nc
    with ExitStack() as st:
        sb = mk_sb(nc, st)
        cvs = sb("cvs", [128, 8, 2])
        cs = sb("cs", [128, 8, 2])
        adab = sb("adab", [128, DEPTH, 72])
        slabs = [sb(f"slab{i}", [128, 8, 1152]) for i in range(2)]
        b_cv, b_cs, b_ab = P.bufs(3, "ada")
        b_slab = P.bufs(2, "slab")
        P.dma("sp", cvs[:].rearrange("p a b -> p (a b)"), C.cv[:, :], w=[b_cv])
        P.dma("sp", adab[:].rearrange("p a b -> p (a b)"), C.ada_b[:, :], w=[b_ab])
        P.op("act", lambda e: e.activation(out=cs[:], in_=cvs[:], func=AF.Silu), r=[b_cv], w=[b_cs])
        ps0 = C.ps[0]
        n = 0
        for L in range(DEPTH):
            wv = C.ada_w[L].rearrange("(kc p) n -> p kc n", p=128)
            for s in range(8):
                slab = slabs[n % 2]
                bs = b_slab[n % 2]
                n += 1
                P.dma("sp", slab[:], wv[:, :, s * 1152:(s + 1) * 1152], w=[bs])
                for qq in range(9):
                    q = s * 9 + qq
                    for kc in range(8):
                        P.op("pe", lambda e, slab=slab, qq=qq, q=q, kc=kc: e.matmul(
                            ps0[:, 2 * q:2 * q + 2], lhsT=slab[:, kc, qq * 128:(qq + 1) * 128], rhs=cs[:, kc, :],
                            start=(kc == 0), stop=(kc == 7)), r=[bs, b_cs], w=[C.bps[0]])
            P.op("dve", lambda e, L=L: e.tensor_tensor(
                out=C.mod[:, L, :, :], in0=ps0[:, 0:144].rearrange("p (q v) -> p q v", v=2),
                in1=adab[:, L, :].unsqueeze(2).to_broadcast([128, 72, 2]), op=ALU.add),
                r=[C.bps[0], b_ab], w=[C.b_mod])
        for L in range(DEPTH):
            for s in range(3):
                for v in range(2):
                    P.op("dve", lambda e, L=L, s=s, v=v: e.scalar_tensor_tensor(
                        out=C.mA[:, L, s, v, :], in0=C.mod[:, L, (3 * s + 1) * 8:(3 * s + 1) * 8 + 8, v], scalar=1.0,
                        in1=C.g_sb[:, L, s, :], op0=ALU.add, op1=ALU.mult), r=[C.b_mod, C.b_const], w=[C.b_mod])
                    P.op("dve", lambda e, L=L, s=s, v=v: e.tensor_copy(
                        out=C.mB[:, L, s, v, :], in_=C.mod[:, L, (3 * s) * 8:(3 * s) * 8 + 8, v]),
                        r=[C.b_mod], w=[C.b_mod])
                    P.op("dve", lambda e, L=L, s=s, v=v: e.tensor_scalar(
                        out=C.mG[:, L, s, v, :], in0=C.mod[:, L, (3 * s + 2) * 8:(3 * s + 2) * 8 + 8, v],
                        scalar1=(1.0 if s == 1 else 0.5), scalar2=None, op0=ALU.mult), r=[C.b_mod], w=[C.b_mod])
        P.barrier()


def emit_norm(C, P, h, b_h, T, sqbuf, b_sq, rstd, b_rstd, psb):
    ps = C.ps[psb]
    P.op("act", lambda e: e.activation(out=sqbuf[:, :, :T], in_=h[:, :, :T], func=AF.Square), r=b_h, w=[b_sq])
    for c in range(8):
        P.op("pe", lambda e, c=c: e.matmul(ps[:, :T], lhsT=C.ones_bf[:], rhs=sqbuf[:, c, :T], start=(c == 0), stop=(c == 7)),
             r=[b_sq, C.b_const], w=[C.bps[psb]])
    P.op("act", lambda e: e.activation(out=rstd[:, :T], in_=ps[:, :T], func=AF.Sqrt, bias=C.eps_sb[:, 0:1], scale=1.0 / D),
         r=[C.bps[psb], C.b_const], w=[b_rstd])
    P.op("dve", lambda e: e.reciprocal(out=rstd[:, :T], in_=rstd[:, :T]), r=[b_rstd], w=[b_rstd])


def emit_modulate(C, P, h, b_h, y, b_y, rstd, b_rstd, tmps, b_tmps, L, s, v, T):
    for c in range(8):
        tmp = tmps[c % 2]
        bt = b_tmps[c % 2]
        P.op("dve", lambda e, c=c, tmp=tmp: e.scalar_tensor_tensor(
            out=tmp[:, :T], in0=h[:, c, :T], scalar=C.mA[:, L, s, v, c:c + 1], in1=rstd[:, :T],
            op0=ALU.mult, op1=ALU.mult), r=[b_h[c], b_rstd, C.b_mod], w=[bt])
        P.op("act", lambda e, c=c, tmp=tmp: e.activation(
            out=y[:, c, :T], in_=tmp[:, :T], func=AF.Identity, bias=C.mB[:, L, s, v, c:c + 1], scale=1.0),
            r=[bt, C.b_mod], w=[b_y])


def stage_ffn(C, P, L, s, src, srcbuf, tiles):
    nc = C.nc
    wgu_d = C.ffn_gu[s // 2][L]
    wdn_d = C.ffn_dn[s // 2][L]
    T = TT
    with ExitStack() as st:
        sb = mk_sb(nc, st)
        wgu = sb("wgu", [128, 8, 2 * DFF], BF16)
        wdn = sb("wdn", [128, 22, D], BF16)
        hs = [sb(f"h{i}", [128, 8, T]) for i in range(2)]
        ys = [sb(f"y{i}", [128, 8, T], BF16) for i in range(2)]
        actb = sb("actb", [128, 22, T], BF16)
        sq = sb("sq", [128, 8, T], BF16)
        rstd = sb("rstd", [128, T])
        tmps = [sb(f"tmp{i}", [128, T]) for i in range(2)]
        sgs = [sb(f"sg{i}", [128, T]) for i in range(2)]
        b_wgu = P.bufs(8, "wgu")
        b_wdn = P.bufs(22, "wdn")
        b_hs = [P.bufs(8, f"h{i}_") for i in range(2)]
        b_ys = P.bufs(2, "y")
        b_act = P.bufs(22, "act")
        b_sq, b_rstd = P.bufs(2, "nrm")
        b_tmps = P.bufs(2, "tmp")
        b_sgs = P.bufs(2, "sg")
        for kc in range(8):
            P.dma("pool", wgu[:, kc, :], wgu_d[kc * 128:(kc + 1) * 128, :], w=[b_wgu[kc]])
        for f in range(22):
            P.dma("pool", wdn[:, f, :], wdn_d[f * 128:(f + 1) * 128, :], w=[b_wdn[f]])

        def prologue(i):
            ti = tiles[i]
            h, bh, y, by = hs[i % 2], b_hs[i % 2], ys[i % 2], b_ys[i % 2]
            v = 1 if ti == 0 else 0
            P.dma("sp", h[:], fm(src, ti * T, T), r=[srcbuf[ti]], w=bh)
            emit_norm(C, P, h, bh, T, sq, b_sq, rstd, b_rstd, 0)
            emit_modulate(C, P, h, bh, y, by, rstd, b_rstd, tmps, b_tmps, L, s, v, T)

        prologue(0)
        for i, ti in enumerate(tiles):
            h, bh, y, by = hs[i % 2], b_hs[i % 2], ys[i % 2], b_ys[i % 2]
            v = 1 if ti == 0 else 0
            for f in range(22):
                pgb = 1 + 2 * (f % 2)
                pub = 2 + 2 * (f % 2)
                pg, pu = C.ps[pgb], C.ps[pub]
                for kc in range(8):
                    P.op("pe", lambda e, f=f, kc=kc, pg=pg, y=y: e.matmul(
                        pg[:, :T], lhsT=wgu[:, kc, f * 128:(f + 1) * 128], rhs=y[:, kc, :],
                        start=(kc == 0), stop=(kc == 7)), r=[b_wgu[kc], by], w=[C.bps[pgb]])
                for kc in range(8):
                    P.op("pe", lambda e, f=f, kc=kc, pu=pu, y=y: e.matmul(
                        pu[:, :T], lhsT=wgu[:, kc, DFF + f * 128:DFF + (f + 1) * 128], rhs=y[:, kc, :],
                        start=(kc == 0), stop=(kc == 7)), r=[b_wgu[kc], by], w=[C.bps[pub]])
                sg, bsg = sgs[f % 2], b_sgs[f % 2]
                P.op("act", lambda e, sg=sg, pg=pg: e.activation(out=sg[:], in_=pg[:, :T], func=AF.Silu),
                     r=[C.bps[pgb]], w=[bsg])
                P.op("dve", lambda e, f=f, sg=sg, pu=pu: e.tensor_tensor(
                    out=actb[:, f, :], in0=sg[:], in1=pu[:, :T], op=ALU.mult), r=[bsg, C.bps[pub]], w=[b_act[f]])
                if f == 10 and i + 1 < len(tiles):
                    prologue(i + 1)
            for dc in range(8):
                pob = 5 + dc % 3
                po = C.ps[pob]
                for f in range(22):
                    P.op("pe", lambda e, f=f, dc=dc, po=po: e.matmul(
                        po[:, :T], lhsT=wdn[:, f, dc * 128:(dc + 1) * 128], rhs=actb[:, f, :],
                        start=(f == 0), stop=(f == 21)), r=[b_wdn[f], b_act[f]], w=[C.bps[pob]])
                P.op("dve", lambda e, dc=dc, po=po, h=h, v=v: e.scalar_tensor_tensor(
                    out=h[:, dc, :], in0=po[:, :T], scalar=C.mG[:, L, s, v, dc:dc + 1], in1=h[:, dc, :],
                    op0=ALU.mult, op1=ALU.add), r=[C.bps[pob], bh[dc], C.b_mod], w=[bh[dc]])
            P.dma("sp", fm(C.hT, ti * T, T), h[:], r=bh, w=[C.hbuf[ti]])
        P.barrier()


def stage_final(C, P, tiles):
    nc = C.nc
    T = TT
    toks = []
    with ExitStack() as st:
        sb = mk_sb(nc, st)
        hs = [sb(f"fh{i}", [128, 8, T]) for i in range(2)]
        os_ = [sb(f"fo{i}", [128, 8, T]) for i in range(2)]
        sq = sb("fsq", [128, 8, T], BF16)
        rstd = sb("frstd", [128, T])
        b_hs = [P.bufs(8, f"fh{i}_") for i in range(2)]
        b_os = P.bufs(2, "fo")
        b_sq, b_rstd = P.bufs(2, "fn")
        for i, ti in enumerate(tiles):
            h, bh, o, bo = hs[i % 2], b_hs[i % 2], os_[i % 2], b_os[i % 2]
            P.dma("sp", h[:], fm(C.hT, ti * T, T), r=[C.hbuf[ti]], w=bh)
            emit_norm(C, P, h, bh, T, sq, b_sq, rstd, b_rstd, 0)
            for c in range(8):
                P.op("dve", lambda e, c=c, h=h, o=o: e.scalar_tensor_tensor(
                    out=o[:, c, :], in0=h[:, c, :], scalar=C.fg_sb[:, c:c + 1], in1=rstd[:],
                    op0=ALU.mult, op1=ALU.mult), r=[bh[c], b_rstd, C.b_const], w=[bo])
            toks.append(P.dma("sp", fm(C.outT, (ti - 1) * T, T), o[:], r=[bo], w=[C.obuf[ti]]))
        P.barrier()
    return toks


def build_program(stages, debug=False):
    nc = bass.Bass("TRN2", target_bir_lowering=False)
    C = declare_io(nc, debug)
    declare_odd(C)
    declare_even(C)
    P = Prog(nc)
    all_tiles = list(range(NTILES))
    lat_tiles = list(range(1, NTILES))
    with ExitStack() as st:
        setup_persistent(C, P, st)
        src, srcbuf = C.xT, C.xbuf
        toks = []
        for sg in stages:
            kind = sg[0]
            if kind == "ada":
                stage_ada(C, P)
            elif kind == "ffn":
                _, L, s, with_ctx = sg
                stage_ffn(C, P, L, s, src, srcbuf, all_tiles if with_ctx else lat_tiles)
                src, srcbuf = C.hT, C.hbuf
            elif kind == "odd":
                _, L, with_ctx = sg
                stage_odd(C, P, L, all_tiles if with_ctx else lat_tiles)
            elif kind == "even":
                _, L = sg
                zbuf = P.bufs(NT // 128, "ztok")
                fbuf = P.bufs(NTILES, "fT")
                stage_even_A(C, P, L, all_tiles, zbuf, fbuf)
                stage_rwkv(C, P, L)
                stage_even_C(C, P, L, all_tiles)
            elif kind == "evenA":
                _, L = sg
                zbuf = P.bufs(NT // 128, "ztok")
                fbuf = P.bufs(NTILES, "fT")
                stage_even_A(C, P, L, all_tiles, zbuf, fbuf)
            elif kind == "final":
                toks += stage_final(C, P, lat_tiles)
            else:
                raise ValueError(kind)
        if debug:
            for ti in range(NTILES):
                toks.append(P.dma("sp", C.dbg[:, ti * TT:(ti + 1) * TT], C.hT[:, ti * TT:(ti + 1) * TT], r=[C.hbuf[ti]], w=[P.buf()]))
        P.finish("sp", toks)
        P.emit(st)
    return nc


def chunked(v, n=128):
    v = np.asarray(v, np.float32)
    lead = v.shape[:-1]
    k = v.shape[-1] // n
    return np.ascontiguousarray(np.moveaxis(v.reshape(lead + (k, n)), -1, 0))


def prep_inputs(inp, ncores=8):
    f = lambda a: np.ascontiguousarray(np.asarray(a, np.float32))
    shared = {
        "ada_w": f(inp["ada_w"]),
        "ada_b": chunked(inp["ada_b"]).reshape(128, -1),
        "norm_g": chunked(inp["norm_g"]).reshape(128, -1),
        "final_g": chunked(inp["final_g"]).reshape(128, -1),
        "ident": np.eye(128, dtype=np.float32),
        "ffn1_w_gu": f(inp["ffn1_w_gu"]), "ffn2_w_gu": f(inp["ffn2_w_gu"]),
        "ffn1_w_down": f(inp["ffn1_w_down"]), "ffn2_w_down": f(inp["ffn2_w_down"]),
    }
    prep_odd(inp, shared)
    prep_even(inp, shared)
    maps = []
    cc = chunked(inp["c_ctx"])
    for b in range(ncores):
        m = dict(shared)
        m["xT"] = np.ascontiguousarray(np.concatenate([np.asarray(inp["ctx"][b]).T, np.asarray(inp["x"][b]).T], axis=1), dtype=np.float32)
        cb = chunked(inp["c"][b])
        m["cv"] = np.ascontiguousarray(np.stack([cb, cc], axis=-1).reshape(128, 16))
        maps.append(m)
    return maps


CONVW = 31
SEQS = {0: (0, CTX), 1: (CTX, NT)}


def seq_of_tile(ti):
    return 0 if ti == 0 else 1


def declare_odd(C):
    nc = C.nc
    di = lambda name, shape: nc.dram_tensor(name, list(shape), F32, kind="ExternalInput").ap()
    C.o_w_in = di("o_w_in", [2, D, DIN_O])
    C.o_w_out = di("o_w_out", [2, D, D])
    C.o_cw = di("o_cw", [128, 2 * 6 * CONVW])
    C.o_small = di("o_small", [128, 2 * 14])
    C.o_pwbd = di("o_pwbd", [128, 2 * 2 * 128])
    C.o_mask = di("o_mask", [128, 2 * 16])
    C.o_invc = di("o_invc", [256, NT])
    C.uT = nc.dram_tensor("uT", [768, NT], F32).ap()
    C.qT = nc.dram_tensor("qT", [256, NT], F32).ap()


def stage_odd(C, P, L, tiles):
    nc = C.nc
    j = L // 2
    T = TT
    ubuf = P.bufs(NTILES, "uT")
    qbuf = P.bufs(NTILES, "qT")
    with ExitStack() as st:
        sb = mk_sb(nc, st)
        win = sb("o_win", [128, 8, DIN_O], BF16)
        wout = sb("o_wout", [128, 8, D], BF16)
        cw = sb("o_cw", [128, 6, CONVW])
        small = sb("o_small", [128, 14])
        pwbd = sb("o_pwbd", [128, 2, 128])
        mask = sb("o_mask", [128, 2, 16])
        b_win = P.bufs(8, "owin")
        b_wout = P.bufs(8, "owout")
        b_oc = P.buf("oconst")
        for kc in range(8):
            P.dma("pool", win[:, kc, :], C.o_w_in[j, kc * 128:(kc + 1) * 128, :], w=[b_win[kc]])
        for kc in range(8):
            P.dma("pool", wout[:, kc, :], C.o_w_out[j, kc * 128:(kc + 1) * 128, :], w=[b_wout[kc]])
        P.dma("sp", cw[:].rearrange("p a b -> p (a b)"), C.o_cw[:, j * 186:(j + 1) * 186], w=[b_oc])
        P.dma("sp", small[:], C.o_small[:, j * 14:(j + 1) * 14], w=[b_oc])
        P.dma("sp", pwbd[:].rearrange("p a b -> p (a b)"), C.o_pwbd[:, j * 256:(j + 1) * 256], w=[b_oc])
        P.dma("sp", mask[:].rearrange("p a b -> p (a b)"), C.o_mask[:, :], w=[b_oc])
        hs = [sb(f"oh{i}", [128, 8, T]) for i in range(2)]
        ys = [sb(f"oy{i}", [128, 8, T], BF16) for i in range(2)]
        us = [sb(f"ou{i}", [128, 6, T]) for i in range(2)]
        qs = [sb(f"oq{i}", [128, 2, T]) for i in range(2)]
        sq = sb("osq", [128, 8, T], BF16)
        rstd = sb("orstd", [128, T])
        tmps = [sb(f"otmp{i}", [128, T]) for i in range(2)]
        sgs = [sb(f"osg{i}", [128, T]) for i in range(2)]
        b_hs = [P.bufs(8, f"oh{i}_") for i in range(2)]
        b_ys = P.bufs(2, "oy")
        b_us = P.bufs(2, "ou")
        b_qs = P.bufs(2, "oq")
        b_sq, b_rstd = P.bufs(2, "onrm")
        b_tmps = P.bufs(2, "otmp")
        b_sgs = P.bufs(2, "osg")

        def prologue(i):
            ti = tiles[i]
            h, bh, y, by = hs[i % 2], b_hs[i % 2], ys[i % 2], b_ys[i % 2]
            v = 1 if ti == 0 else 0
            P.dma("sp", h[:], fm(C.hT, ti * T, T), r=[C.hbuf[ti]], w=bh)
            emit_norm(C, P, h, bh, T, sq, b_sq, rstd, b_rstd, 0)
            emit_modulate(C, P, h, bh, y, by, rstd, b_rstd, tmps, b_tmps, L, 1, v, T)

        prologue(0)
        for i, ti in enumerate(tiles):
            y, by = ys[i % 2], b_ys[i % 2]
            u, bu, q, bq = us[i % 2], b_us[i % 2], qs[i % 2], b_qs[i % 2]
            if i + 1 < len(tiles):
                prologue(i + 1)
            for oc in range(6):
                pab = 1 + 2 * (oc % 2)
                pbb = 2 + 2 * (oc % 2)
                pa, pb = C.ps[pab], C.ps[pbb]
                for kc in range(8):
                    P.op("pe", lambda e, oc=oc, kc=kc, pa=pa, y=y: e.matmul(
                        pa[:, :T], lhsT=win[:, kc, oc * 128:(oc + 1) * 128], rhs=y[:, kc, :],
                        start=(kc == 0), stop=(kc == 7)), r=[b_win[kc], by], w=[C.bps[pab]])
                for kc in range(8):
                    P.op("pe", lambda e, oc=oc, kc=kc, pb=pb, y=y: e.matmul(
                        pb[:, :T], lhsT=win[:, kc, 768 + oc * 128:768 + (oc + 1) * 128], rhs=y[:, kc, :],
                        start=(kc == 0), stop=(kc == 7)), r=[b_win[kc], by], w=[C.bps[pbb]])
                sg, bsg = sgs[oc % 2], b_sgs[oc % 2]
                P.op("act", lambda e, sg=sg, pb=pb: e.activation(out=sg[:], in_=pb[:, :T], func=AF.Sigmoid),
                     r=[C.bps[pbb]], w=[bsg])
                P.op("dve", lambda e, oc=oc, sg=sg, pa=pa, u=u: e.tensor_tensor(
                    out=u[:, oc, :], in0=sg[:], in1=pa[:, :T], op=ALU.mult), r=[bsg, C.bps[pab]], w=[bu])
            for qc in range(2):
                pqb = 5 + qc
                pq = C.ps[pqb]
                for kc in range(8):
                    P.op("pe", lambda e, qc=qc, kc=kc, pq=pq, y=y: e.matmul(
                        pq[:, :T], lhsT=win[:, kc, 1536 + qc * 128:1536 + (qc + 1) * 128], rhs=y[:, kc, :],
                        start=(kc == 0), stop=(kc == 7)), r=[b_win[kc], by], w=[C.bps[pqb]])
                P.op("act", lambda e, qc=qc, pq=pq, q=q: e.activation(out=q[:, qc, :], in_=pq[:, :T], func=AF.Copy),
                     r=[C.bps[pqb]], w=[bq])
            P.dma("pool", C.uT.rearrange("(c p) t -> p c t", p=128)[:, :, ti * T:(ti + 1) * T], u[:], r=[bu], w=[ubuf[ti]])
            P.dma("pool", C.qT.rearrange("(c p) t -> p c t", p=128)[:, :, ti * T:(ti + 1) * T], q[:], r=[bq], w=[qbuf[ti]])
        HU = T + 30
        HQ = T + 16
        uh = [sb(f"ouh{i}", [128, 6, HU]) for i in range(2)]
        qh = [sb(f"oqh{i}", [128, 2, HQ]) for i in range(2)]
        ic = [sb(f"oic{i}", [128, 2, T]) for i in range(2)]
        cv = [sb(f"ocv{i}", [128, 6, T]) for i in range(2)]
        pl = [sb(f"opl{i}", [128, 2, T]) for i in range(2)]
        cat = [sb(f"ocat{i}", [128, 8, T], BF16) for i in range(2)]
        sq6 = sb("osq6", [128, 6, T], BF16)
        b_uh = P.bufs(2, "ouh")
        b_qh = P.bufs(2, "oqh")
        b_ic = P.bufs(2, "oic")
        b_cv = [P.bufs(6, f"ocv{i}_") for i in range(2)]
        b_pl = [P.bufs(2, f"opl{i}_") for i in range(2)]
        b_cat = [P.bufs(8, f"ocat{i}_") for i in range(2)]

        def loadsB(i):
            ti = tiles[i]
            s0, s1 = SEQS[seq_of_tile(ti)]
            t0 = ti * T
            k = i % 2
            for (buf, bb, dram, nch, hl, hr, tb) in ((uh[k], b_uh[k], C.uT, 6, 15, 15, ubuf), (qh[k], b_qh[k], C.qT, 2, 8, 7, qbuf)):
                lo, hi = max(t0 - hl, s0), min(t0 + T + hr, s1)
                if lo > t0 - hl or hi < t0 + T + hr:
                    P.op("pool", lambda e, buf=buf: e.memset(buf[:], 0.0), w=[bb])
                deps = [tb[x] for x in (ti - 1, ti, ti + 1) if 0 <= x < NTILES and seq_of_tile(x) == seq_of_tile(ti)]
                P.dma("sp", buf[:, :, lo - (t0 - hl):hi - (t0 - hl)],
                      dram.rearrange("(c p) t -> p c t", p=128)[:, :, lo:hi], r=deps, w=[bb])
            P.dma("sp", ic[k][:], C.o_invc.rearrange("(c p) t -> p c t", p=128)[:, :, t0:t0 + T], w=[b_ic[k]])
            P.dma("sp", hs[k][:], fm(C.hT, t0, T), r=[C.hbuf[ti]], w=b_hs[k])

        loadsB(0)
        for i, ti in enumerate(tiles):
            k = i % 2
            v = 1 if ti == 0 else 0
            if i + 1 < len(tiles):
                loadsB(i + 1)
            u_h, q_h, icv, conv, pool, ct, h, bh = uh[k], qh[k], ic[k], cv[k], pl[k], cat[k], hs[k], b_hs[k]
            for c in range(6):
                P.op("dve", lambda e, c=c, conv=conv, u_h=u_h: e.tensor_scalar(
                    out=conv[:, c, :], in0=u_h[:, c, 0:T], scalar1=cw[:, c, 0:1], scalar2=small[:, c:c + 1],
                    op0=ALU.mult, op1=ALU.add), r=[b_uh[k], b_oc], w=[b_cv[k][c]])
                for kk in range(1, CONVW):
                    P.op("dve", lambda e, c=c, kk=kk, conv=conv, u_h=u_h: e.scalar_tensor_tensor(
                        out=conv[:, c, :], in0=u_h[:, c, kk:kk + T], scalar=cw[:, c, kk:kk + 1], in1=conv[:, c, :],
                        op0=ALU.mult, op1=ALU.add), r=[b_uh[k], b_oc, b_cv[k][c]], w=[b_cv[k][c]])
            P.op("act", lambda e, conv=conv: e.activation(out=sq6[:], in_=conv[:], func=AF.Square), r=b_cv[k], w=[b_sq])
            for c in range(6):
                P.op("pe", lambda e, c=c: e.matmul(C.ps[0][:, :T], lhsT=C.ones_bf[:], rhs=sq6[:, c, :], start=(c == 0), stop=(c == 5)),
                     r=[b_sq, C.b_const], w=[C.bps[0]])
            P.op("act", lambda e: e.activation(out=rstd[:], in_=C.ps[0][:, :T], func=AF.Sqrt, bias=C.eps_sb[:, 0:1], scale=1.0 / 768),
                 r=[C.bps[0], C.b_const], w=[b_rstd])
            P.op("dve", lambda e: e.reciprocal(out=rstd[:], in_=rstd[:]), r=[b_rstd], w=[b_rstd])
            for c in range(6):
                tmp, bt = tmps[c % 2], b_tmps[c % 2]
                P.op("dve", lambda e, c=c, tmp=tmp, conv=conv: e.scalar_tensor_tensor(
                    out=tmp[:], in0=conv[:, c, :], scalar=small[:, 6 + c:7 + c], in1=rstd[:], op0=ALU.mult, op1=ALU.mult),
                    r=[b_cv[k][c], b_rstd, b_oc], w=[bt])
                P.op("act", lambda e, c=c, tmp=tmp, ct=ct: e.activation(out=ct[:, c, :], in_=tmp[:], func=AF.Silu),
                     r=[bt], w=[b_cat[k][c]])
            for c in range(2):
                koffs = range(6, 10) if c == 0 else range(0, 16)
                first = True
                for kk in koffs:
                    if first:
                        P.op("dve", lambda e, c=c, kk=kk, pool=pool, q_h=q_h: e.tensor_scalar(
                            out=pool[:, c, :], in0=q_h[:, c, kk:kk + T], scalar1=mask[:, c, kk:kk + 1], scalar2=None, op0=ALU.mult),
                            r=[b_qh[k], b_oc], w=[b_pl[k][c]])
                        first = False
                    else:
                        P.op("dve", lambda e, c=c, kk=kk, pool=pool, q_h=q_h: e.scalar_tensor_tensor(
                            out=pool[:, c, :], in0=q_h[:, c, kk:kk + T], scalar=mask[:, c, kk:kk + 1], in1=pool[:, c, :],
                            op0=ALU.mult, op1=ALU.add), r=[b_qh[k], b_oc, b_pl[k][c]], w=[b_pl[k][c]])
                P.op("dve", lambda e, c=c, pool=pool, icv=icv: e.tensor_tensor(
                    out=pool[:, c, :], in0=pool[:, c, :], in1=icv[:, c, :], op=ALU.mult), r=[b_pl[k][c], b_ic[k]], w=[b_pl[k][c]])
                P.op("dve", lambda e, c=c, pool=pool, q_h=q_h: e.tensor_tensor(
                    out=pool[:, c, :], in0=pool[:, c, :], in1=q_h[:, c, 8:8 + T], op=ALU.subtract), r=[b_pl[k][c], b_qh[k]], w=[b_pl[k][c]])
                ppb = 1 + c
                P.op("pe", lambda e, c=c, pool=pool, ppb=ppb: e.matmul(C.ps[ppb][:, :T], lhsT=pwbd[:, c, :], rhs=pool[:, c, :], start=True, stop=True),
                     r=[b_pl[k][c], b_oc], w=[C.bps[ppb]])
                P.op("act", lambda e, c=c, ct=ct, ppb=ppb: e.activation(
                    out=ct[:, 6 + c, :], in_=C.ps[ppb][:, :T], func=AF.Identity, scale=small[:, 12 + c:13 + c]),
                    r=[C.bps[ppb], b_oc], w=[b_cat[k][6 + c]])
            for dc in range(8):
                pob = 3 + dc % 4
                po = C.ps[pob]
                for kc in range(8):
                    P.op("pe", lambda e, dc=dc, kc=kc, po=po, ct=ct: e.matmul(
                        po[:, :T], lhsT=wout[:, kc, dc * 128:(dc + 1) * 128], rhs=ct[:, kc, :],
                        start=(kc == 0), stop=(kc == 7)), r=[b_wout[kc], b_cat[k][kc]], w=[C.bps[pob]])
                P.op("dve", lambda e, dc=dc, po=po, h=h, v=v: e.scalar_tensor_tensor(
                    out=h[:, dc, :], in0=po[:, :T], scalar=C.mG[:, L, 1, v, dc:dc + 1], in1=h[:, dc, :],
                    op0=ALU.mult, op1=ALU.add), r=[C.bps[pob], bh[dc], C.b_mod], w=[bh[dc]])
            P.dma("pool", fm(C.hT, ti * T, T), h[:], r=bh, w=[C.hbuf[ti]])
        P.barrier()


def prep_odd(inp, shared):
    cw = np.asarray(inp["o_conv_w"], np.float32)
    a = cw.transpose(0, 2, 1).reshape(2, 6, 128, CONVW)
    shared["o_cw"] = np.ascontiguousarray(a.transpose(2, 0, 1, 3).reshape(128, -1))
    sm = np.concatenate([chunked(inp["o_conv_b"]), chunked(inp["o_cnorm_g"]), chunked(inp["o_pool_scale"])], axis=-1)
    shared["o_small"] = np.ascontiguousarray(sm.reshape(128, -1))
    pw = np.asarray(inp["o_pool_w"], np.float32)
    bd = np.zeros((128, 2, 2, 128), np.float32)
    for j in range(2):
        for c in range(2):
            for gi in range(2):
                bd[gi * 64:(gi + 1) * 64, j, c, gi * 64:(gi + 1) * 64] = pw[j, 2 * c + gi]
    shared["o_pwbd"] = np.ascontiguousarray(bd.reshape(128, -1))
    widths = (2, 4, 8, 16)
    mask = np.zeros((128, 2, 16), np.float32)
    invc = np.zeros((256, NT), np.float32)
    for g, wd in enumerate(widths):
        c, gi = g // 2, g % 2
        for kk in range(16):
            off = kk - 8
            if -(wd // 2) <= off <= wd - 1 - wd // 2:
                mask[gi * 64:(gi + 1) * 64, c, kk] = 1.0
        for (s0, n) in ((0, CTX), (CTX, SEQ)):
            t = np.arange(n)
            lo = np.maximum(t - wd // 2, 0)
            hi = np.minimum(t + (wd - 1 - wd // 2), n - 1)
            invc[g * 64:(g + 1) * 64, s0:s0 + n] = (1.0 / (hi - lo + 1).astype(np.float32))[None, :]
    shared["o_mask"] = np.ascontiguousarray(mask.reshape(128, -1))
    shared["o_invc"] = invc
    shared["o_w_in"] = np.ascontiguousarray(np.asarray(inp["o_w_in"], np.float32))
    shared["o_w_out"] = np.ascontiguousarray(np.asarray(inp["o_w_out"], np.float32))


import ml_dtypes

DA = 2560
DR = 768
NH = 12
SC = 128
DECAY = 0.606531
GN_EPS = 64e-5
TABW = 9 * DR
RWKV_LIMIT = 0


def declare_even(C):
    nc = C.nc
    di = lambda name, shape, dt=F32: nc.dram_tensor(name, list(shape), dt, kind="ExternalInput").ap()
    C.e_w_in = di("e_w_in", [2, D, DIN_E])
    C.e_w_out = di("e_w_out", [2, D, D])
    C.e_mu = di("e_mu", [128, 2 * DA])
    C.e_tab = di("e_tab", [128, 2 * TABW])
    C.e_w_up = di("e_w_up", [2, 2, 64, DR])
    C.e_a_up = di("e_a_up", [2, 2, 64, DR])
    C.e_g_up = di("e_g_up", [2, 128, DR])
    C.e_masks = di("e_masks", [128, 2 * 768])
    C.cosT = di("cosT", [SEQ, SEQ], BF16)
    C.sinT = di("sinT", [SEQ, SEQ], BF16)
    C.cosc = di("cosc", [CTX, CTX], BF16)
    C.sinc = di("sinc", [CTX, CTX], BF16)
    C.cdft = di("cdft", [128, 4 * 128])
    C.z_tok = nc.dram_tensor("z_tok", [NT, DIN_E], F32).ap()
    C.fT = nc.dram_tensor("fT", [256, NT], F32).ap()
    C.yf_tok = nc.dram_tensor("yf_tok", [NT, DR], F32).ap()
    C.o_tok = nc.dram_tensor("o_tok", [NT, DR], F32).ap()


def stage_even_A(C, P, L, tiles, zbuf, fbuf):
    nc = C.nc
    j = L // 2
    T = TT
    with ExitStack() as st:
        sb = mk_sb(nc, st)
        win = sb("e_win", [128, 8, DIN_E], BF16)
        b_win = P.bufs(8, "ewin")
        for kc in range(8):
            P.dma("pool", win[:, kc, :], C.e_w_in[j, kc * 128:(kc + 1) * 128, :], w=[b_win[kc]])
        hs = [sb(f"eh{i}", [128, 8, T]) for i in range(2)]
        ys = [sb(f"ey{i}", [128, 8, T], BF16) for i in range(2)]
        zt = [sb(f"ezt{i}", [128, DIN_E]) for i in range(2)]
        sq = sb("esq", [128, 8, T], BF16)
        rstd = sb("erstd", [128, T])
        tmps = [sb(f"etmp{i}", [128, T]) for i in range(2)]
        b_hs = [P.bufs(8, f"eh{i}_") for i in range(2)]
        b_ys = P.bufs(2, "ey")
        b_zt = P.bufs(2, "ezt")
        b_sq, b_rstd = P.bufs(2, "enrm")
        b_tmps = P.bufs(2, "etmp")

        def prologue(i):
            ti = tiles[i]
            h, bh, y, by = hs[i % 2], b_hs[i % 2], ys[i % 2], b_ys[i % 2]
            v = 1 if ti == 0 else 0
            P.dma("sp", h[:], fm(C.hT, ti * T, T), r=[C.hbuf[ti]], w=bh)
            emit_norm(C, P, h, bh, T, sq, b_sq, rstd, b_rstd, 0)
            emit_modulate(C, P, h, bh, y, by, rstd, b_rstd, tmps, b_tmps, L, 1, v, T)

        colblocks = [(c0, min(512, DIN_E - c0)) for c0 in range(0, DIN_E, 512)]
        prologue(0)
        nz = 0
        nps = 0
        for i, ti in enumerate(tiles):
            y, by = ys[i % 2], b_ys[i % 2]
            if i + 1 < len(tiles):
                prologue(i + 1)
            for tb in range(T // 128):
                z, bz = zt[nz % 2], b_zt[nz % 2]
                nz += 1
                for (c0, wd) in colblocks:
                    pb = 1 + nps % 7
                    nps += 1
                    ps = C.ps[pb]
                    for kc in range(8):
                        P.op("pe", lambda e, kc=kc, ps=ps, y=y, tb=tb, c0=c0, wd=wd: e.matmul(
                            ps[:, :wd], lhsT=y[:, kc, tb * 128:(tb + 1) * 128], rhs=win[:, kc, c0:c0 + wd],
                            start=(kc == 0), stop=(kc == 7)), r=[b_win[kc], by], w=[C.bps[pb]])
                    if nps % 2 == 0:
                        P.op("act", lambda e, ps=ps, z=z, c0=c0, wd=wd: e.activation(out=z[:, c0:c0 + wd], in_=ps[:, :wd], func=AF.Copy),
                             r=[C.bps[pb]], w=[bz])
                    else:
                        P.op("dve", lambda e, ps=ps, z=z, c0=c0, wd=wd: e.tensor_copy(out=z[:, c0:c0 + wd], in_=ps[:, :wd]),
                             r=[C.bps[pb]], w=[bz])
                r0 = ti * T + tb * 128
                P.dma("pool", C.z_tok[r0:r0 + 128, :], z[:], r=[bz], w=[zbuf[r0 // 128]])
        P.barrier()
    with ExitStack() as st:
        sb = mk_sb(nc, st)
        cd = sb("cdft", [128, 4, 128], BF16)
        b_cd = P.buf("cdft")
        P.dma("pool", cd[:].rearrange("p a b -> p (a b)"), C.cdft[:, :], w=[b_cd])
        seqs = []
        if 0 in tiles:
            seqs.append((0, CTX, C.cosc, C.sinc, 2, 3, CTX))
        seqs.append((CTX, SEQ, C.cosT, C.sinT, 0, 1, 512))
        zf = sb("zf", [128, SEQ // 128, 256], BF16)
        cosb = [sb(f"cosb{i}", [128, SEQ // 128, 512], BF16) for i in range(2)]
        sinb = [sb(f"sinb{i}", [128, SEQ // 128, 512], BF16) for i in range(2)]
        pq = [sb(f"pq{i}", [128, 2, 512], BF16) for i in range(2)]
        fo = [sb(f"fo{i}", [128, 512]) for i in range(2)]
        b_zf = P.buf("zf")
        b_cos = P.bufs(2, "cosb")
        b_sin = P.bufs(2, "sinb")
        b_pq = P.bufs(2, "pq")
        b_fo = P.bufs(2, "fo")
        nk = 0
        nf = 0
        for (s0, N, ctab, stab, ic, isn, KW) in seqs:
            nch = N // 128
            P.dma("pool", zf[:, :nch, :], C.z_tok[s0:s0 + N, DA:DA + 256].rearrange("(nc p) c -> p nc c", p=128),
                  r=[zbuf[(s0 // 128) + x] for x in range(nch)], w=[b_zf])
            for kb in range(N // KW):
                cb, sbb = cosb[nk % 2], sinb[nk % 2]
                bc, bs = b_cos[nk % 2], b_sin[nk % 2]
                nk += 1
                P.dma("sp", cb[:, :nch, :KW], ctab[:, kb * KW:(kb + 1) * KW].rearrange("(nc p) k -> p nc k", p=128), w=[bc])
                P.dma("sp", sbb[:, :nch, :KW], stab[:, kb * KW:(kb + 1) * KW].rearrange("(nc p) k -> p nc k", p=128), w=[bs])
                for hh in range(2):
                    pp, bpq = pq[nf % 2], b_pq[nf % 2]
                    f, bf = fo[nf % 2], b_fo[nf % 2]
                    nf += 1
                    for n_ in range(nch):
                        P.op("pe", lambda e, n_=n_, cb=cb, hh=hh, KW=KW, nch=nch: e.matmul(
                            C.ps[1][:, :KW], lhsT=zf[:, n_, hh * 128:(hh + 1) * 128], rhs=cb[:, n_, :KW],
                            start=(n_ == 0), stop=(n_ == nch - 1)), r=[b_zf, bc], w=[C.bps[1]])
                    for n_ in range(nch):
                        P.op("pe", lambda e, n_=n_, sbb=sbb, hh=hh, KW=KW, nch=nch: e.matmul(
                            C.ps[2][:, :KW], lhsT=zf[:, n_, hh * 128:(hh + 1) * 128], rhs=sbb[:, n_, :KW],
                            start=(n_ == 0), stop=(n_ == nch - 1)), r=[b_zf, bs], w=[C.bps[2]])
                    P.op("act", lambda e, pp=pp, KW=KW: e.activation(out=pp[:, 0, :KW], in_=C.ps[1][:, :KW], func=AF.Copy),
                         r=[C.bps[1]], w=[bpq])
                    P.op("dve", lambda e, pp=pp, KW=KW: e.tensor_copy(out=pp[:, 1, :KW], in_=C.ps[2][:, :KW]),
                         r=[C.bps[2]], w=[bpq])
                    pfb = 3 + nf % 2
                    P.op("pe", lambda e, pp=pp, KW=KW, pfb=pfb, ic=ic: e.matmul(
                        C.ps[pfb][:, :KW], lhsT=cd[:, ic, :], rhs=pp[:, 0, :KW], start=True, stop=False),
                        r=[b_cd, bpq], w=[C.bps[pfb]])
                    P.op("pe", lambda e, pp=pp, KW=KW, pfb=pfb, isn=isn: e.matmul(
                        C.ps[pfb][:, :KW], lhsT=cd[:, isn, :], rhs=pp[:, 1, :KW], start=False, stop=True),
                        r=[b_cd, bpq], w=[C.bps[pfb]])
                    P.op("act", lambda e, f=f, KW=KW, pfb=pfb: e.activation(out=f[:, :KW], in_=C.ps[pfb][:, :KW], func=AF.Copy),
                         r=[C.bps[pfb]], w=[bf])
                    c0 = s0 + kb * KW
                    P.dma("pool", C.fT[hh * 128:(hh + 1) * 128, c0:c0 + KW], f[:, :KW], r=[bf],
                          w=[fbuf[(c0 // TT) + x] for x in range(max(1, KW // TT))])
        P.barrier()


def stage_rwkv(C, P, L):
    nc = C.nc
    j = L // 2
    colmajor = (j % 2 == 1)
    NSC = NT // SC
    ybuf = P.bufs(NSC, "yf")

    def OP(eng, method, r, w, *args, **kw):
        return P.op(eng, lambda e: getattr(e, method)(*args, **kw), r=r, w=w)

    def row_runs(seq, sc, delta):
        N = CTX if seq == 0 else SEQ
        lo = sc * SC + delta
        hi = lo + SC
        a, b = max(lo, 0), min(hi, N)
        runs = []
        if seq == 0 or not colmajor:
            base = 0 if seq == 0 else CTX
            runs.append((a - lo, b - lo, ("lin", base + a, base + b)))
        else:
            pos = a
            while pos < b:
                w_ = pos // 64
                e_ = min(b, (w_ + 1) * 64)
                runs.append((pos - lo, e_ - lo, ("cm", w_, pos % 64, pos % 64 + (e_ - pos))))
                pos = e_
        return runs, (a > lo or b < hi)

    def dram_rows(tens, spec, c0, c1):
        if spec[0] == "lin":
            return tens[spec[1]:spec[2], c0:c1]
        _, w_, r0, r1 = spec
        return tens[CTX:NT, c0:c1].rearrange("(r w) c -> w r c", w=64)[w_, r0:r1, :]

    with ExitStack() as st:
        sb = mk_sb(nc, st)
        mu = sb("r_mu", [128, DA])
        tab = sb("r_tab", [128, 9, DR])
        wa_up = sb("r_waup", [128, 2, DR])
        gup = sb("r_gup", [128, DR])
        masks = sb("r_masks", [128, 2, 768])
        ones = sb("r_ones", [128, 128])
        gneps = sb("r_gneps", [128, 1])
        zc = sb("r_zc", [128, DA])
        zp = sb("r_zp", [128, DA])
        zn = sb("r_zn", [128, DA])
        lrT = sb("r_lrT", [128, 128])
        sgT = sb("r_sgT", [128, 128])
        S = {i: sb(f"r_S{i}", [128, DR]) for i in (1, 2, 3, 4, 6, 7)}
        S8 = [sb(f"r_S8_{i}", [128, DR]) for i in range(2)]
        S8b = [sb(f"r_S8b_{i}", [128, DR], BF16) for i in range(2)]
        S5 = [sb("r_S5", [128, DR])] * 2
        NB = [sb(f"r_NB_{i}", [128, DR], BF16) for i in range(2)]
        KH = sb("r_KH", [128, DR], BF16)
        vb = sb("r_vb", [128, DR], BF16)
        FMb = sb("r_FMb", [128, 6, 128], BF16)
        FMk = sb("r_FMk", [128, 6, 128], BF16)
        FMkq = sb("r_FMkq", [128, 6, 128], BF16)
        FMr = [sb(f"r_FMr{i}", [128, 6, 128], BF16) for i in range(2)]
        XC = sb("r_XC", [128, NH, 2, 128], BF16)
        BD = sb("r_BD", [128, NH, 2, 128], BF16)
        Yt = sb("r_Yt", [128, NH, 128], BF16)
        Zt = sb("r_Zt", [128, NH, 128], BF16)
        Zf = sb("r_Zf", [128, NH, 128])
        KqT = sb("r_KqT", [128, 6, 128], BF16)
        BU = sb("r_BU", [128, DR], BF16)
        Hb = [sb(f"r_Hb{i}", [128, 6, 64], BF16) for i in range(2)]
        UY = sb("r_UY", [128, DR])
        Gs = sb("r_G", [128, 6, 64])
        Ylocs = sb("r_Yloc", [128, DR])
        Hs = [sb(f"r_H{i}", [128, 6, 64]) for i in range(2)]
        tmpH = sb("r_tmpH", [128, 6, 64])
        pL = [sb(f"r_pL{i}", [128, 8]) for i in range(2)]
        stt_ = sb("r_st", [128, 4, NH])
        yft = [sb(f"r_yf{i}", [128, DR]) for i in range(2)]
        BON = [sb(f"r_bon{i}", [128, DR]) for i in range(2)]
        Gt = [sb(f"r_g{i}", [128, DR]) for i in range(2)]

        b_k = P.buf("r_const")
        b_zc, b_zp, b_zn, b_lr, b_sg = P.bufs(5, "r_z")
        b_S = {i: P.buf(f"r_S{i}") for i in (1, 2, 3, 4, 6, 7)}
        b_S8 = P.bufs(2, "r_S8")
        b_S8b = P.bufs(2, "r_S8b")
        b_S5 = [P.buf("r_S5")] * 2
        b_NB = P.bufs(2, "r_NB")
        b_KH, b_vb = P.bufs(2, "r_khvb")
        b_Hb = P.bufs(2, "r_Hb")
        b_FMb, b_FMk, b_FMkq = P.bufs(3, "r_FM")
        b_FMr = P.bufs(2, "r_FMr")
        b_X = P.bufs(3, "r_X")
        b_nC = P.bufs(3, "r_nC")
        b_Bm = P.bufs(3, "r_Bm")
        b_Y = P.bufs(3, "r_Y")
        b_Z = P.bufs(3, "r_Z")
        b_Zf = P.bufs(3, "r_Zf")
        b_KqT, b_BU, b_UY, b_G, b_Yloc, b_tmpH = P.bufs(6, "r_m")
        b_pL = P.bufs(2, "r_pL")
        b_st = P.bufs(4, "r_st")
        b_yf = P.bufs(2, "r_yf")
        b_BON = P.bufs(2, "r_bon")
        b_Gt = P.bufs(2, "r_gt")
        b_H = P.bufs(2, "r_H")

        P.dma("sp", mu[:], C.e_mu[:, j * DA:(j + 1) * DA], w=[b_k])
        P.dma("sp", tab[:].rearrange("p a b -> p (a b)"), C.e_tab[:, j * TABW:(j + 1) * TABW], w=[b_k])
        for d in range(2):
            P.dma("sp", wa_up[0:64, d, :], C.e_w_up[j, d, :, :], w=[b_k])
            P.dma("sp", wa_up[64:128, d, :], C.e_a_up[j, d, :, :], w=[b_k])
        P.dma("sp", gup[:], C.e_g_up[j, :, :], w=[b_k])
        P.dma("sp", masks[:].rearrange("p a b -> p (a b)"), C.e_masks[:, :], w=[b_k])
        OP("dve", "memset", [], [b_k], ones[:], 1.0)
        OP("dve", "memset", [], [b_k], gneps[:], GN_EPS)

        psn = [0]

        def psget():
            b = psn[0] % 8
            psn[0] += 1
            return C.ps[b], C.bps[b]

        T_W0, T_A0, T_KK, T_KA, T_RK, T_GNW, T_GNB = 0, 2, 4, 5, 6, 7, 8
        r_, k_, v_ = zc[:, 0:DR], zc[:, DR:2 * DR], zc[:, 2 * DR:3 * DR]
        v3 = lambda ap: ap.rearrange("p (h n) -> p h n", n=64)
        bc = lambda kk_: stt_[:, kk_, :].unsqueeze(2).to_broadcast([128, NH, 64])
        vh = lambda h: vb[:, h * 64:(h + 1) * 64]

        def gidx(seq, sc):
            return sc if seq == 0 else 2 + sc

        def P0(d, seq, sc, pb):
            if d == 1:
                g = gidx(seq, sc)
                P.dma("sp", yft[pb][:], C.yf_tok[g * SC:(g + 1) * SC, :], r=[ybuf[g]], w=[b_yf[pb]])
            for (tile_, bt, delta) in ((zc, b_zc, 0), (zp, b_zp, -1), (zn, b_zn, 1)):
                runs, clipped = row_runs(seq, sc, delta)
                if clipped:
                    OP("pool", "memset", [], [bt], tile_[:], 0.0)
                for (p0, p1, spec) in runs:
                    P.dma("sp", tile_[p0:p1, :], dram_rows(C.z_tok, spec, 0, DA), w=[bt])

        def P1(d, seq, sc, pb):
            OP("dve", "tensor_tensor", [b_zp, b_zn], [b_zp], out=zp[:], in0=zp[:], in1=zn[:], op=ALU.add)
            OP("dve", "scalar_tensor_tensor", [b_zp, b_zc], [b_zp], out=zp[:], in0=zp[:], scalar=0.5, in1=zc[:], op0=ALU.mult, op1=ALU.subtract)
            OP("dve", "tensor_tensor", [b_zp, b_k], [b_zp], out=zp[:], in0=zp[:], in1=mu[:], op=ALU.mult)
            OP("dve", "tensor_tensor", [b_zc, b_zp], [b_zc], out=zc[:], in0=zc[:], in1=zp[:], op=ALU.add)
            pT, bT = psget()
            OP("pe", "transpose", [b_zc, C.b_const], [bT], pT[:, 0:128], zc[:, 2304:2432], C.ident_sb[:])
            OP("pe", "transpose", [b_zc, C.b_const], [bT], pT[:, 128:256], zc[:, 2432:2560], C.ident_sb[:])
            OP("act", "activation", [bT], [b_lr], out=lrT[0:64, :], in_=pT[0:64, 0:128], func=AF.Tanh)
            OP("act", "activation", [bT], [b_lr], out=lrT[64:128, :], in_=pT[64:128, 0:128], func=AF.Copy)
            if d == 1:
                OP("act", "activation", [bT], [b_sg], out=sgT[:], in_=pT[:, 128:256], func=AF.Sigmoid)

        def lowrank(dst, bdst, lhsT, rhs, rbufs, addtab):
            pa, ba = psget()
            pb_, bb = psget()
            OP("pe", "matmul", rbufs, [ba], pa[:, 0:512], lhsT=lhsT, rhs=rhs[:, 0:512], start=True, stop=True)
            OP("pe", "matmul", rbufs, [bb], pb_[:, 0:256], lhsT=lhsT, rhs=rhs[:, 512:768], start=True, stop=True)
            if addtab is None:
                OP("act", "activation", [ba], [bdst], out=dst[:, 0:512], in_=pa[:, 0:512], func=AF.Copy)
                OP("act", "activation", [bb], [bdst], out=dst[:, 512:768], in_=pb_[:, 0:256], func=AF.Copy)
            else:
                OP("dve", "tensor_tensor", [ba, b_k], [bdst], out=dst[:, 0:512], in0=pa[:, 0:512], in1=addtab[:, 0:512], op=ALU.add)
                OP("dve", "tensor_tensor", [bb, b_k], [bdst], out=dst[:, 512:768], in0=pb_[:, 0:256], in1=addtab[:, 512:768], op=ALU.add)

        def P2(d, seq, sc, pb):
            lowrank(S[1], b_S[1], lrT[0:64, :], wa_up[0:64, d, :], [b_lr, b_k], tab[:, T_W0 + d, :])
            OP("act", "activation", [b_S[1]], [b_S[1]], out=S[1][:], in_=S[1][:], func=AF.Sigmoid)
            OP("dve", "tensor_scalar", [b_S[1]], [b_S[1]], out=S[1][:], in0=S[1][:], scalar1=-DECAY, scalar2=None, op0=ALU.mult)
            lowrank(S[2], b_S[2], lrT[64:128, :], wa_up[64:128, d, :], [b_lr, b_k], tab[:, T_A0 + d, :])
            OP("act", "activation", [b_S[2]], [b_S[2]], out=S[2][:], in_=S[2][:], func=AF.Sigmoid)
            if d == 1:
                lowrank(Gt[pb], b_Gt[pb], sgT[:], gup[:], [b_sg, b_k], None)
            OP("dve", "tensor_tensor", [b_zc, b_k], [b_S[3]], out=S[3][:], in0=k_, in1=tab[:, T_KK, :], op=ALU.mult)
            OP("dve", "tensor_tensor", [b_S[3]], [b_S[7]], out=S[7][:], in0=S[3][:], in1=S[3][:], op=ALU.mult)
            OP("dve", "tensor_reduce", [b_S[7]], [b_st[0]], out=stt_[:, 0, :], in_=v3(S[7][:]), axis=AX.X, op=ALU.add)
            OP("act", "activation", [b_st[0]], [b_st[0]], out=stt_[:, 0, :], in_=stt_[:, 0, :], func=AF.Sqrt)
            OP("dve", "tensor_scalar", [b_st[0]], [b_st[0]], out=stt_[:, 0, :], in0=stt_[:, 0, :], scalar1=1e-12, scalar2=None, op0=ALU.max)
            OP("dve", "reciprocal", [b_st[0]], [b_st[0]], out=stt_[:, 0, :], in_=stt_[:, 0, :])
            OP("dve", "tensor_tensor", [b_S[3], b_st[0]], [b_S[3]], out=v3(S[3][:]), in0=v3(S[3][:]), in1=bc(0), op=ALU.mult)
            OP("dve", "scalar_tensor_tensor", [b_S[2], b_k], [b_S[4]], out=S[4][:], in0=S[2][:], scalar=-1.0, in1=tab[:, T_KA, :], op0=ALU.add, op1=ALU.mult)
            OP("dve", "tensor_tensor", [b_S[4], b_zc], [b_S[4]], out=S[4][:], in0=S[4][:], in1=k_, op=ALU.mult)
            OP("dve", "tensor_tensor", [b_S[4], b_zc], [b_S[4]], out=S[4][:], in0=S[4][:], in1=k_, op=ALU.add)
            OP("dve", "tensor_tensor", [b_S[3], b_S[2]], [b_S5[pb]], out=S5[pb][:], in0=S[3][:], in1=S[2][:], op=ALU.mult)

        def P3(d, seq, sc, pb):
            tri = masks[:, d, 640:768]
            pCa, bCa = psget()
            pCb, bCb = psget()
            OP("pe", "matmul", [b_k, b_S[1]], [bCa], pCa[:, 0:512], lhsT=tri, rhs=S[1][:, 0:512], start=True, stop=True)
            OP("pe", "matmul", [b_k, b_S[1]], [bCb], pCb[:, 0:256], lhsT=tri, rhs=S[1][:, 512:768], start=True, stop=True)
            pTa, bTa = psget()
            pTb, bTb = psget()
            OP("pe", "matmul", [b_k, b_S[1]], [bTa], pTa[:, 0:512], lhsT=ones[:], rhs=S[1][:, 0:512], start=True, stop=True)
            OP("pe", "matmul", [b_k, b_S[1]], [bTb], pTb[:, 0:256], lhsT=ones[:], rhs=S[1][:, 512:768], start=True, stop=True)
            pP, bP = psget()
            for pr in range(6):
                OP("pe", "matmul", [b_k, b_S[1]], [bP], pP[:, 2 * pr:2 * pr + 2], lhsT=S[1][:, pr * 128:(pr + 1) * 128], rhs=ones[:, 0:2], start=True, stop=True)
            OP("act", "activation", [bP], [b_pL[pb]], out=pL[pb][:, 0:6], in_=pP[:, 0:12].rearrange("p (a b) -> p a b", b=2)[:, :, 0], func=AF.Exp)
            OP("act", "activation", [bCa], [b_S[6]], out=S[6][:, 0:512], in_=pCa[:, 0:512], func=AF.Copy)
            OP("act", "activation", [bCb], [b_S[6]], out=S[6][:, 512:768], in_=pCb[:, 0:256], func=AF.Copy)
            OP("dve", "tensor_tensor", [b_S[6], b_S[1]], [b_S[7]], out=S[7][:], in0=S[6][:], in1=S[1][:], op=ALU.subtract)
            OP("act", "activation", [b_S[7]], [b_S[7]], out=S[7][:], in_=S[7][:], func=AF.Exp)
            OP("dve", "tensor_tensor", [b_S[3], b_S[7]], [b_S8[pb]], out=S8[pb][:], in0=S[3][:], in1=S[7][:], op=ALU.mult)
            OP("act", "activation", [b_S8[pb]], [b_S8b[pb]], out=S8b[pb][:], in_=S8[pb][:], func=AF.Copy)
            OP("act", "activation", [b_S[6]], [b_S[7]], out=S[7][:], in_=S[6][:], func=AF.Exp, scale=-1.0)
            OP("dve", "tensor_tensor", [b_S5[pb], b_S[7]], [b_S[2]], out=S[2][:], in0=S5[pb][:], in1=S[7][:], op=ALU.mult)
            OP("dve", "tensor_tensor", [b_S[4], b_S[7]], [b_S[3]], out=S[3][:], in0=S[4][:], in1=S[7][:], op=ALU.mult)
            OP("act", "activation", [b_S[6]], [b_S[7]], out=S[7][:], in_=S[6][:], func=AF.Exp)
            OP("dve", "tensor_tensor", [b_zc, b_S[7]], [b_S[1]], out=S[1][:], in0=r_, in1=S[7][:], op=ALU.mult)
            OP("dve", "tensor_tensor", [bTa, b_S[6]], [b_S[7]], out=S[7][:, 0:512], in0=pTa[:, 0:512], in1=S[6][:, 0:512], op=ALU.subtract)
            OP("dve", "tensor_tensor", [bTb, b_S[6]], [b_S[7]], out=S[7][:, 512:768], in0=pTb[:, 0:256], in1=S[6][:, 512:768], op=ALU.subtract)
            OP("act", "activation", [b_S[7]], [b_S[7]], out=S[7][:], in_=S[7][:], func=AF.Exp)
            OP("dve", "scalar_tensor_tensor", [b_S5[pb], b_S[7]], [b_NB[pb]], out=NB[pb][:], in0=S5[pb][:], scalar=-1.0, in1=S[7][:], op0=ALU.mult, op1=ALU.mult)
            OP("dve", "tensor_tensor", [b_S[4], b_S[7]], [b_KH], out=KH[:], in0=S[4][:], in1=S[7][:], op=ALU.mult)
            OP("act", "activation", [b_zc], [b_vb], out=vb[:], in_=v_, func=AF.Copy)

        def P4(d, seq, sc, pb):
            for (src, bsrc, dst, bdst) in ((S[2], b_S[2], FMb, b_FMb), (S[3], b_S[3], FMk, b_FMk),
                                           (S8[pb], b_S8[pb], FMkq, b_FMkq), (S[1], b_S[1], FMr[pb], b_FMr[pb])):
                pa, ba = psget()
                pb_, bb = psget()
                for pr in range(4):
                    OP("pe", "transpose", [bsrc, C.b_const], [ba], pa[:, pr * 128:(pr + 1) * 128], src[:, pr * 128:(pr + 1) * 128], C.ident_sb[:])
                for pr in range(4, 6):
                    OP("pe", "transpose", [bsrc, C.b_const], [bb], pb_[:, (pr - 4) * 128:(pr - 3) * 128], src[:, pr * 128:(pr + 1) * 128], C.ident_sb[:])
                OP("act", "activation", [ba], [bdst], out=dst[:, 0:4, :], in_=pa[:, 0:512].rearrange("p (a b) -> p a b", b=128), func=AF.Copy)
                OP("act", "activation", [bb], [bdst], out=dst[:, 4:6, :], in_=pb_[:, 0:256].rearrange("p (a b) -> p a b", b=128), func=AF.Copy)
            if d == 1:
                lowrank(S[1], b_S[1], lrT[64:128, :], wa_up[64:128, 0, :], [b_lr, b_k], tab[:, T_A0 + 0, :])
                OP("act", "activation", [b_S[1]], [b_S[1]], out=S[1][:], in_=S[1][:], func=AF.Sigmoid)
                OP("dve", "scalar_tensor_tensor", [b_S[1], b_k], [b_S[1]], out=S[1][:], in0=S[1][:], scalar=-1.0, in1=tab[:, T_KA, :], op0=ALU.add, op1=ALU.mult)
                OP("dve", "tensor_tensor", [b_S[1], b_zc], [b_S[1]], out=S[1][:], in0=S[1][:], in1=k_, op=ALU.mult)
                OP("dve", "tensor_tensor", [b_S[1], b_zc], [b_S[1]], out=S[1][:], in0=S[1][:], in1=k_, op=ALU.add)
                OP("dve", "tensor_tensor", [b_S[1], b_S[4]], [b_S[1]], out=S[1][:], in0=S[1][:], in1=S[4][:], op=ALU.add)
                OP("dve", "tensor_tensor", [b_zc, b_k], [b_S[2]], out=S[2][:], in0=r_, in1=tab[:, T_RK, :], op=ALU.mult)
                OP("dve", "tensor_tensor", [b_S[2], b_S[1]], [b_S[2]], out=S[2][:], in0=S[2][:], in1=S[1][:], op=ALU.mult)
                OP("dve", "tensor_reduce", [b_S[2]], [b_st[3]], out=stt_[:, 3, :], in_=v3(S[2][:]), axis=AX.X, op=ALU.add)
                OP("dve", "tensor_tensor", [b_zc, b_st[3]], [b_BON[pb]], out=v3(BON[pb][:]), in0=v3(v_), in1=bc(3), op=ALU.mult)

        XCv = XC[:].rearrange("p (pr m) a b -> p pr m (a b)", m=2)
        BDv = BD[:].rearrange("p (pr m) a b -> p pr m (a b)", m=2)
        Ytv = Yt[:].rearrange("p (pr m) b -> p pr m b", m=2)
        X = XC[:, :, 0, :]

        def per_head_768(lhs_fn, rhs_fn, rbufs_fn, dst, bdst, addsrc=None, baddsrc=None):
            pa, ba = psget()
            pb_, bb = psget()
            for h in range(NH):
                (pp, bp, c0) = (pa, ba, h * 64) if h < 8 else (pb_, bb, (h - 8) * 64)
                OP("pe", "matmul", rbufs_fn(h), [bp], pp[:, c0:c0 + 64], lhsT=lhs_fn(h), rhs=rhs_fn(h), start=True, stop=True)
            if addsrc is None:
                OP("act", "activation", [ba], [bdst], out=dst[:, 0:512], in_=pa[:, 0:512], func=AF.Copy)
                OP("act", "activation", [bb], [bdst], out=dst[:, 512:768], in_=pb_[:, 0:256], func=AF.Copy)
            else:
                OP("dve", "tensor_tensor", [ba, baddsrc], [bdst], out=dst[:, 0:512], in0=pa[:, 0:512], in1=addsrc[:, 0:512], op=ALU.add)
                OP("dve", "tensor_tensor", [bb, baddsrc], [bdst], out=dst[:, 512:768], in0=pb_[:, 0:256], in1=addsrc[:, 512:768], op=ALU.add)

        def E(d, seq, sc, pb):
            mXC = masks[:, d, 0:256]
            mBD = masks[:, d, 256:512]
            mM2 = masks[:, d, 512:640]
            FMkr_b = [b_FMkq, b_FMr[pb]]
            for hq in range(3):
                for (FMx, bFM, msk, dstv, wb) in ((FMb, b_FMb, mXC, XCv, [b_X[hq], b_nC[hq]]), (FMk, b_FMk, mBD, BDv, [b_Bm[hq]])):
                    for m in range(2):
                        mb = slice(m * 64, (m + 1) * 64)
                        pa, ba = psget()
                        for q in range(2):
                            hp = 2 * hq + q
                            OP("pe", "matmul", [bFM] + FMkr_b, [ba], pa[:, q * 256:q * 256 + 128], lhsT=FMx[mb, hp, :], rhs=FMkq[mb, hp, :], start=True, stop=True)
                            OP("pe", "matmul", [bFM] + FMkr_b, [ba], pa[:, q * 256 + 128:(q + 1) * 256], lhsT=FMx[mb, hp, :], rhs=FMr[pb][mb, hp, :], start=True, stop=True)
                        OP("dve", "tensor_tensor", [ba, b_k], wb, out=dstv[:, 2 * hq:2 * hq + 2, m, :],
                           in0=pa[:, 0:512].rearrange("p (h x) -> p h x", x=256), in1=msk.unsqueeze(1).to_broadcast([128, 2, 256]), op=ALU.mult)
                for m in range(2):
                    mb = slice(m * 64, (m + 1) * 64)
                    pa, ba = psget()
                    for q in range(2):
                        pr = 2 * hq + q
                        OP("pe", "matmul", [b_FMkq, b_FMb], [ba], pa[:, q * 128:(q + 1) * 128], lhsT=FMkq[mb, pr, :], rhs=FMb[mb, pr, :], start=True, stop=True)
                    OP("dve", "tensor_tensor", [ba, b_k], [b_Y[hq]], out=Ytv[:, 2 * hq:2 * hq + 2, m, :], in0=pa[:, 0:256].rearrange("p (h x) -> p h x", x=128),
                       in1=mM2.unsqueeze(1).to_broadcast([128, 2, 128]), op=ALU.mult)
            for g in range(3):
                g4 = slice(4 * g, 4 * g + 4)
                OP("act", "activation", [b_X[g]], [b_Zf[g]], out=Zf[:, g4, :], in_=X[:, g4, :], func=AF.Copy)
                OP("dve", "tensor_tensor", [b_Zf[g], C.b_const], [b_Zf[g]], out=Zf[:, g4, :], in0=Zf[:, g4, :],
                   in1=C.ident_sb[:].unsqueeze(1).to_broadcast([128, 4, 128]), op=ALU.add)
                OP("act", "activation", [b_Zf[g]], [b_Z[g]], out=Zt[:, g4, :], in_=Zf[:, g4, :], func=AF.Copy)
            pG, bG = psget()
            for h in range(NH):
                pr, m = h // 2, h % 2
                mb = slice(m * 64, (m + 1) * 64)
                OP("pe", "matmul", [b_KH, b_vb], [bG], pG[mb, pr * 64:(pr + 1) * 64], lhsT=KH[:, h * 64:(h + 1) * 64], rhs=vh(h), start=True, stop=True)
            OP("act", "activation", [bG], [b_G], out=Gs[:], in_=pG[:, 0:384].rearrange("p (a b) -> p a b", b=64), func=AF.Copy)
            per_head_768(lambda h: BD[:, h, 0, :], vh, lambda h: [b_Bm[h // 4], b_vb], BU, b_BU)
            per_head_768(lambda h: BD[:, h, 1, :], vh, lambda h: [b_Bm[h // 4], b_vb], Ylocs, b_Yloc)

        def I_level(lv):
            last = (lv == SC // 2)
            for g in range(3):
                g4 = slice(4 * g, 4 * g + 4)
                if not last:
                    pX, bX = psget()
                    for hh in range(4):
                        h = 4 * g + hh
                        OP("pe", "matmul", [b_Y[g], b_X[g]], [bX], pX[:, hh * 128:(hh + 1) * 128], lhsT=Yt[:, h, :], rhs=X[:, h, :], start=True, stop=True)
                pY, bY = psget()
                for hh in range(4):
                    h = 4 * g + hh
                    OP("pe", "matmul", [b_Y[g], b_X[g]], [bY], pY[:, hh * 128:(hh + 1) * 128], lhsT=X[:, h, :], rhs=Yt[:, h, :], start=True, stop=True)
                if not last:
                    OP("act", "activation", [bX], [b_X[g]], out=X[:, g4, :], in_=pX[:, 0:512].rearrange("p (h x) -> p h x", x=128), func=AF.Copy)
                OP("act", "activation", [bY], [b_Y[g]], out=Yt[:, g4, :], in_=pY[:, 0:512].rearrange("p (h x) -> p h x", x=128), func=AF.Copy)
                pZ, bZ = psget()
                for hh in range(4):
                    h = 4 * g + hh
                    OP("pe", "matmul", [b_Y[g], b_Z[g]], [bZ], pZ[:, hh * 128:(hh + 1) * 128], lhsT=Yt[:, h, :], rhs=Zt[:, h, :], start=True, stop=True)
                OP("dve", "tensor_tensor", [bZ, b_Zf[g]], [b_Zf[g]], out=Zf[:, g4, :], in0=pZ[:, 0:512].rearrange("p (h x) -> p h x", x=128),
                   in1=Zf[:, g4, :], op=ALU.add)
                OP("act", "activation", [b_Zf[g]], [b_Z[g]], out=Zt[:, g4, :], in_=Zf[:, g4, :], func=AF.Copy)

        def Lphase(d, seq, sc, pb, cur):
            pa, ba = psget()
            pb_, bb = psget()
            for h in range(NH):
                pr, m = h // 2, h % 2
                mb = slice(m * 64, (m + 1) * 64)
                (pp, bp, c0) = (pa, ba, pr * 128) if pr < 4 else (pb_, bb, (pr - 4) * 128)
                OP("pe", "matmul", [b_S8b[pb], b_Z[h // 4]], [bp], pp[mb, c0:c0 + 128], lhsT=S8b[pb][:, h * 64:(h + 1) * 64], rhs=Zt[:, h, :], start=True, stop=True)
            OP("act", "activation", [ba], [b_KqT], out=KqT[:, 0:4, :], in_=pa[:, 0:512].rearrange("p (a b) -> p a b", b=128), func=AF.Copy)
            OP("act", "activation", [bb], [b_KqT], out=KqT[:, 4:6, :], in_=pb_[:, 0:256].rearrange("p (a b) -> p a b", b=128), func=AF.Copy)
            per_head_768(lambda h: Zt[:, h, :], lambda h: BU[:, h * 64:(h + 1) * 64], lambda h: [b_Z[h // 4], b_BU], UY, b_UY)
            Hc, bHc, Hn, bHn = Hs[cur], b_H[cur], Hs[1 - cur], b_H[1 - cur]
            Hbc, bHbc, Hbn, bHbn = Hb[cur], b_Hb[cur], Hb[1 - cur], b_Hb[1 - cur]

            def ph_rowtiled(lhs_fn, rbufs, dst, bdst, addsrc, baddsrc):
                dv = dst[:].rearrange("p (pr m n) -> p pr m n", m=2, n=64)
                av = addsrc[:].rearrange("p (pr m n) -> p pr m n", m=2, n=64)
                for m in range(2):
                    mb = slice(m * 64, (m + 1) * 64)
                    pa2, ba2 = psget()
                    for pr in range(6):
                        OP("pe", "matmul", rbufs + [bHbc], [ba2], pa2[:, pr * 64:(pr + 1) * 64], lhsT=lhs_fn(pr, mb), rhs=Hbc[mb, pr, :], start=True, stop=True)
                    OP("dve", "tensor_tensor", [ba2, baddsrc], [bdst], out=dv[:, :, m, :], in0=pa2[:, 0:384].rearrange("p (a b) -> p a b", b=64),
                       in1=av[:, :, m, :], op=ALU.add)

            OP("dve", "tensor_tensor", [bHc, b_pL[pb]], [b_tmpH], out=tmpH[:], in0=Hc[:], in1=pL[pb][:, 0:6].unsqueeze(2).to_broadcast([128, 6, 64]), op=ALU.mult)
            OP("dve", "tensor_tensor", [b_tmpH, b_G], [b_tmpH], out=tmpH[:], in0=tmpH[:], in1=Gs[:], op=ALU.add)
            ph_rowtiled(lambda pr, mb: KqT[mb, pr, :], [b_KqT], BU, b_BU, UY, b_UY)
            pH, bH = psget()
            for h in range(NH):
                pr, m = h // 2, h % 2
                mb = slice(m * 64, (m + 1) * 64)
                OP("pe", "matmul", [b_NB[pb], b_BU], [bH], pH[mb, pr * 64:(pr + 1) * 64], lhsT=NB[pb][:, h * 64:(h + 1) * 64], rhs=BU[:, h * 64:(h + 1) * 64], start=True, stop=True)
            OP("dve", "tensor_tensor", [bH, b_tmpH], [bHbn], out=Hbn[:], in0=pH[:, 0:384].rearrange("p (a b) -> p a b", b=64), in1=tmpH[:], op=ALU.add)
            OP("dve", "tensor_tensor", [bH, b_tmpH], [bHn], out=Hn[:], in0=pH[:, 0:384].rearrange("p (a b) -> p a b", b=64), in1=tmpH[:], op=ALU.add)
            ph_rowtiled(lambda pr, mb: FMr[pb][mb, pr, :], [b_FMr[pb]], UY, b_UY, Ylocs, b_Yloc)
            per_head_768(lambda h: XC[:, h, 1, :], lambda h: BU[:, h * 64:(h + 1) * 64], lambda h: [b_nC[h // 4], b_BU], UY, b_UY, UY, b_UY)
            ysb, b_y = UY, b_UY
            g = gidx(seq, sc)
            if d == 0:
                P.dma("sp", C.yf_tok[g * SC:(g + 1) * SC, :], ysb[:], r=[b_y], w=[ybuf[g]])
                return
            OP("dve", "tensor_tensor", [b_y, b_yf[pb]], [b_y], out=ysb[:], in0=ysb[:], in1=yft[pb][:], op=ALU.add)
            OP("dve", "tensor_reduce", [b_y], [b_st[1]], out=stt_[:, 1, :], in_=v3(ysb[:]), axis=AX.X, op=ALU.add)
            OP("dve", "tensor_scalar", [b_st[1]], [b_st[1]], out=stt_[:, 1, :], in0=stt_[:, 1, :], scalar1=-1.0 / 64, scalar2=None, op0=ALU.mult)
            OP("dve", "tensor_tensor", [b_y, b_st[1]], [b_y], out=v3(ysb[:]), in0=v3(ysb[:]), in1=bc(1), op=ALU.add)
            OP("dve", "tensor_tensor", [b_y], [b_Yloc], out=Ylocs[:], in0=ysb[:], in1=ysb[:], op=ALU.mult)
            OP("dve", "tensor_reduce", [b_Yloc], [b_st[2]], out=stt_[:, 2, :], in_=v3(Ylocs[:]), axis=AX.X, op=ALU.add)
            OP("act", "activation", [b_st[2], b_k], [b_st[2]], out=stt_[:, 2, :], in_=stt_[:, 2, :], func=AF.Sqrt, bias=gneps[:, 0:1], scale=1.0 / 64)
            OP("dve", "reciprocal", [b_st[2]], [b_st[2]], out=stt_[:, 2, :], in_=stt_[:, 2, :])
            OP("dve", "tensor_tensor", [b_y, b_st[2]], [b_y], out=v3(ysb[:]), in0=v3(ysb[:]), in1=bc(2), op=ALU.mult)
            OP("dve", "tensor_tensor", [b_y, b_k], [b_y], out=ysb[:], in0=ysb[:], in1=tab[:, T_GNW, :], op=ALU.mult)
            OP("dve", "tensor_tensor", [b_y, b_k], [b_y], out=ysb[:], in0=ysb[:], in1=tab[:, T_GNB, :], op=ALU.add)
            OP("dve", "tensor_tensor", [b_y, b_BON[pb]], [b_y], out=ysb[:], in0=ysb[:], in1=BON[pb][:], op=ALU.add)
            OP("dve", "tensor_tensor", [b_y, b_Gt[pb]], [b_y], out=ysb[:], in0=ysb[:], in1=Gt[pb][:], op=ALU.mult)
            runs, _ = row_runs(seq, sc, 0)
            for (p0, p1, spec) in runs:
                P.dma("sp", dram_rows(C.o_tok, spec, 0, DR), ysb[p0:p1, :], r=[b_y], w=[P.buf()])

        for d in range(2):
            OP("dve", "memset", [], [b_H[0]], Hs[0][:], 0.0)
            OP("dve", "memset", [], [b_Hb[0]], Hb[0][:], 0.0)
            order = [(0, 0), (0, 1)] + [(1, s) for s in range(SEQ // SC)]
            if d == 1:
                order = [(0, 1), (0, 0)] + [(1, s) for s in range(SEQ // SC - 1, -1, -1)]
            if RWKV_LIMIT:
                order = [(0, 0), (0, 1), (1, 0)] if d == 0 else [(0, 1), (0, 0), (1, 0)]
            seq0, sc0 = order[0]
            for piece in (P0, P1, P2, P3, P4):
                piece(d, seq0, sc0, 0)
            cur = 0
            for n, (seq, sc) in enumerate(order):
                pb = n % 2
                nxt = order[n + 1] if n + 1 < len(order) else None
                nb = (n + 1) % 2
                E(d, seq, sc, pb)
                if nxt:
                    P0(d, nxt[0], nxt[1], nb)
                I_level(2)
                if nxt:
                    P1(d, nxt[0], nxt[1], nb)
                I_level(4)
                if nxt:
                    P2(d, nxt[0], nxt[1], nb)
                I_level(8)
                I_level(16)
                if nxt:
                    P3(d, nxt[0], nxt[1], nb)
                I_level(32)
                I_level(64)
                Lphase(d, seq, sc, pb, cur)
                if nxt:
                    P4(d, nxt[0], nxt[1], nb)
                cur = 1 - cur
        P.barrier()


def stage_even_C(C, P, L, tiles):
    nc = C.nc
    j = L // 2
    T = TT
    with ExitStack() as st:
        sb = mk_sb(nc, st)
        wout = sb("c_wout", [128, 8, D], BF16)
        b_wout = P.bufs(8, "cwout")
        for kc in range(8):
            P.dma("pool", wout[:, kc, :], C.e_w_out[j, kc * 128:(kc + 1) * 128, :], w=[b_wout[kc]])
        hs = [sb(f"ch{i}", [128, 8, T]) for i in range(2)]
        ot = [sb(f"cot{i}", [128, DR]) for i in range(2)]
        cat = [sb(f"ccat{i}", [128, 8, T], BF16) for i in range(2)]
        b_hs = [P.bufs(8, f"ch{i}_") for i in range(2)]
        b_ot = P.bufs(2, "cot")
        b_cat = [P.bufs(3, f"ccat{i}_") for i in range(2)]
        no = 0
        npb = 0
        for i, ti in enumerate(tiles):
            k = i % 2
            v = 1 if ti == 0 else 0
            h, bh, ct, bct = hs[k], b_hs[k], cat[k], b_cat[k]
            P.dma("sp", h[:], fm(C.hT, ti * T, T), r=[C.hbuf[ti]], w=bh)
            P.dma("pool", ct[:, 6:8, :], C.fT.rearrange("(c p) t -> p c t", p=128)[:, :, ti * T:(ti + 1) * T], w=[bct[2]])
            for tb in range(T // 128):
                o, bo = ot[no % 2], b_ot[no % 2]
                no += 1
                r0 = ti * T + tb * 128
                P.dma("sp", o[:], C.o_tok[r0:r0 + 128, :], w=[bo])
                pab, pbb = 1 + (npb % 3) * 2, 2 + (npb % 3) * 2
                npb += 1
                pa, pb = C.ps[pab], C.ps[pbb]
                for c in range(4):
                    P.op("pe", lambda e, c=c, pa=pa, o=o: e.transpose(pa[:, c * 128:(c + 1) * 128], o[:, c * 128:(c + 1) * 128], C.ident_sb[:]),
                         r=[bo, C.b_const], w=[C.bps[pab]])
                for c in range(4, 6):
                    P.op("pe", lambda e, c=c, pb=pb, o=o: e.transpose(pb[:, (c - 4) * 128:(c - 3) * 128], o[:, c * 128:(c + 1) * 128], C.ident_sb[:]),
                         r=[bo, C.b_const], w=[C.bps[pbb]])
                P.op("act", lambda e, pa=pa, ct=ct, tb=tb: e.activation(
                    out=ct[:, 0:4, tb * 128:(tb + 1) * 128], in_=pa[:, 0:512].rearrange("p (a b) -> p a b", b=128), func=AF.Copy),
                    r=[C.bps[pab]], w=[bct[0]])
                P.op("dve", lambda e, pb=pb, ct=ct, tb=tb: e.tensor_copy(
                    out=ct[:, 4:6, tb * 128:(tb + 1) * 128], in_=pb[:, 0:256].rearrange("p (a b) -> p a b", b=128)),
                    r=[C.bps[pbb]], w=[bct[1]])
            for dc in range(8):
                pob = 7 if dc % 2 == 0 else 0
                po = C.ps[pob]
                for kc in range(8):
                    P.op("pe", lambda e, dc=dc, kc=kc, po=po, ct=ct: e.matmul(
                        po[:, :T], lhsT=wout[:, kc, dc * 128:(dc + 1) * 128], rhs=ct[:, kc, :],
                        start=(kc == 0), stop=(kc == 7)), r=[b_wout[kc]] + bct, w=[C.bps[pob]])
                P.op("dve", lambda e, dc=dc, po=po, h=h, v=v: e.scalar_tensor_tensor(
                    out=h[:, dc, :], in0=po[:, :T], scalar=C.mG[:, L, 1, v, dc:dc + 1], in1=h[:, dc, :],
                    op0=ALU.mult, op1=ALU.add), r=[C.bps[pob], bh[dc], C.b_mod], w=[bh[dc]])
            P.dma("pool", fm(C.hT, ti * T, T), h[:], r=bh, w=[C.hbuf[ti]])
        P.barrier()


def prep_even(inp, shared):
    f = lambda a: np.ascontiguousarray(np.asarray(a, np.float32))
    shared["e_w_in"] = f(inp["e_w_in"])
    shared["e_w_out"] = f(inp["e_w_out"])
    rep = lambda v: np.broadcast_to(np.asarray(v, np.float32)[None, :], (128, np.asarray(v).shape[-1]))
    shared["e_mu"] = np.ascontiguousarray(np.concatenate([rep(inp["e_mu"][j]) for j in range(2)], axis=1))
    tabs = []
    for j in range(2):
        for v in (inp["e_w0"][j, 0], inp["e_w0"][j, 1], inp["e_a0"][j, 0], inp["e_a0"][j, 1], inp["e_k_k"][j], inp["e_k_a"][j],
                  inp["e_r_k"][j], inp["e_gn_w"][j], inp["e_gn_b"][j]):
            tabs.append(rep(v))
    shared["e_tab"] = np.ascontiguousarray(np.concatenate(tabs, axis=1))
    shared["e_w_up"] = f(inp["e_w_up"])
    shared["e_a_up"] = f(inp["e_a_up"])
    shared["e_g_up"] = f(inp["e_g_up"])
    idx = np.arange(SC)
    mk = []
    for d in range(2):
        be = (idx[:, None] <= idx[None, :]) if d == 0 else (idx[:, None] >= idx[None, :])
        strict = be & (idx[:, None] != idx[None, :])
        be = be.astype(np.float32)
        strict = strict.astype(np.float32)
        mk += [-strict, -be, strict, be, -strict.T, be]
    shared["e_masks"] = np.ascontiguousarray(np.concatenate(mk, axis=1))
    def dft(N):
        n = np.arange(N, dtype=np.int64)
        m = (n[:, None] * n[None, :]) % N
        ang = (2.0 * np.pi / N) * m.astype(np.float64)
        return np.cos(ang).astype(np.float32), np.sin(ang).astype(np.float32)
    cN, sN = dft(SEQ)
    shared["cosT"] = cN.astype(ml_dtypes.bfloat16)
    shared["sinT"] = sN.astype(ml_dtypes.bfloat16)
    cc, sc_ = dft(CTX)
    shared["cosc"] = cc.astype(ml_dtypes.bfloat16)
    shared["sinc"] = sc_.astype(ml_dtypes.bfloat16)
    c64, s64 = dft(64)
    cd = np.zeros((128, 4, 128), np.float32)
    for qi, (N, ) in enumerate(((SEQ,), (CTX,))):
        scale = 1.0 / np.sqrt(N * 64.0)
        for g in range(2):
            cd[g * 64:(g + 1) * 64, 2 * qi, g * 64:(g + 1) * 64] = c64 * scale
            cd[g * 64:(g + 1) * 64, 2 * qi + 1, g * 64:(g + 1) * 64] = -s64 * scale
    shared["cdft"] = np.ascontiguousarray(cd.reshape(128, -1))


def full_stages():
    stages = [("ada",)]
    for i in range(DEPTH):
        with_ctx = not (i == DEPTH - 1 and i % 2 == 1)
        stages.append(("ffn", i, 0, with_ctx))
        if i % 2 == 0:
            stages.append(("even", i))
        else:
            stages.append(("odd", i, with_ctx))
        stages.append(("ffn", i, 2, with_ctx))
    stages.append(("final",))
    return stages


def kernel(**inputs):
    ncores = 8
    nc = build_program(full_stages(), debug=False)
    maps = prep_inputs(inputs, ncores=ncores)
    res = run_bass_kernel_spmd(nc, maps, core_ids=list(range(ncores)))
    out = np.stack([np.ascontiguousarray(res.results[b]["outT"].T) for b in range(ncores)], axis=0)
    return out.astype(np.float32)
```

```python
import numpy as np
from contextlib import ExitStack
import concourse.bass as bass
import concourse.mybir as mybir
from concourse.bass_utils import run_bass_kernel_spmd

F32 = mybir.dt.float32
BF16 = mybir.dt.bfloat16
ALU = mybir.AluOpType
AF = mybir.ActivationFunctionType
AX = mybir.AxisListType

D = 1024
SEQ = 4096
CTX = 256
NT = SEQ + CTX
DEPTH = 4
DFF = 2816
NMOD = 9
EPS = 1e-6

EPOCH = 30000
NDMA = 8
DMA_EPOCH = 1800


class Buf:
    __slots__ = ("name", "w", "r")

    def __init__(self, name):
        self.name = name
        self.w = None
        self.r = {}


class Prog:
    ENG = ("pe", "act", "dve", "pool", "sp")

    def __init__(self, nc):
        self.nc = nc
        self.q = {e: [] for e in self.ENG}
        self.cnt = {e: 0 for e in self.ENG}
        self.dcnt = {e: 0 for e in self.ENG}
        self.waited = {e: {} for e in self.ENG}
        self.semkeys = []
        self.semset = set()
        self.sems = {}
        self.nbuf = 0

    def buf(self, name=None):
        self.nbuf += 1
        return Buf(name or f"b{self.nbuf}")

    def bufs(self, n, name="b"):
        return [self.buf(f"{name}{i}") for i in range(n)]

    def _key(self, k):
        if k not in self.semset:
            self.semset.add(k)
            self.semkeys.append(k)
        return k

    def _collect(self, eng, reads, writes, is_dma):
        need = {}

        def add(tok, raw):
            teng, tdma, key, val = tok
            if (not is_dma) and (not tdma) and teng == eng and not raw and eng == "pe":
                return
            if self.waited[eng].get(key, 0) >= val:
                return
            if need.get(key, 0) < val:
                need[key] = val

        for b in reads:
            if b.w is not None:
                add(b.w, True)
        for b in writes:
            if b.w is not None:
                add(b.w, False)
            for t in b.r.values():
                add(t, False)
        return need

    def _register(self, tok, reads, writes):
        teng, tdma, key, val = tok
        rk = key if tdma else teng
        for b in reads:
            b.r[rk] = tok
        for b in writes:
            b.w = tok
            b.r = {}

    def op(self, eng, fn, r=(), w=()):
        need = self._collect(eng, r, w, False)
        for k, v in need.items():
            self.waited[eng][k] = v
        self.cnt[eng] += 1
        k = self.cnt[eng] - 1
        key = self._key((eng, k // EPOCH))
        val = k % EPOCH + 1
        tok = (eng, False, key, val)
        self.waited[eng][key] = max(self.waited[eng].get(key, 0), 0)
        self._register(tok, r, w)
        waits = list(need.items())
        sems = self.sems

        def run(e):
            for wk, wv in waits:
                e.wait_ge(sems[wk], wv)
            fn(e).then_inc(sems[key], 1)

        self.q[eng].append(run)
        return tok

    def dma(self, eng, out, in_, r=(), w=(), **kw):
        need = self._collect(eng, r, w, True)
        i = self.dcnt[eng]
        self.dcnt[eng] += 1
        st = i // (NDMA * DMA_EPOCH)
        j = i % (NDMA * DMA_EPOCH)
        slot = j % NDMA
        rnd = j // NDMA
        key = self._key(("dma", eng, st, slot))
        if rnd > 0 and self.waited[eng].get(key, 0) < 16 * rnd:
            need[key] = max(need.get(key, 0), 16 * rnd)
        for k, v in need.items():
            self.waited[eng][k] = v
        val = 16 * (rnd + 1)
        tok = (eng, True, key, val)
        self._register(tok, r, w)
        waits = list(need.items())
        sems = self.sems

        def run(e):
            for wk, wv in waits:
                e.wait_ge(sems[wk], wv)
            e.dma_start(out=out, in_=in_, **kw).then_inc(sems[key], 16)

        self.q[eng].append(run)
        return tok

    def finish(self, eng, toks):
        waits = []
        for t in toks:
            waits.append((t[2], t[3]))
        sems = self.sems

        def run(e):
            for wk, wv in waits:
                e.wait_ge(sems[wk], wv)

        self.q[eng].append(run)

    def emit(self, stack):
        nc = self.nc
        for i, k in enumerate(self.semkeys):
            self.sems[k] = stack.enter_context(nc.semaphore(f"s{i}"))
        block = stack.enter_context(nc.Block())
        q = self.q

        @block.tensor
        def _(e):
            for f in q["pe"]:
                f(e)

        @block.scalar
        def _(e):
            for f in q["act"]:
                f(e)

        @block.vector
        def _(e):
            for f in q["dve"]:
                f(e)

        @block.gpsimd
        def _(e):
            for f in q["pool"]:
                f(e)

        @block.sync
        def _(e):
            for f in q["sp"]:
                f(e)


def _prog_barrier(self):
    toks = []
    for e in self.ENG:
        if self.cnt[e] > 0:
            k = self.cnt[e] - 1
            toks.append(((e, k // EPOCH), k % EPOCH + 1))
        n = self.dcnt[e]
        if n > 0:
            for back in range(min(n, NDMA)):
                i = n - 1 - back
                st = i // (NDMA * DMA_EPOCH)
                j = i % (NDMA * DMA_EPOCH)
                toks.append((("dma", e, st, j % NDMA), 16 * (j // NDMA + 1)))
    sems = self.sems
    for e in self.ENG:
        waits = []
        for key, val in toks:
            if self.waited[e].get(key, 0) >= val:
                continue
            self.waited[e][key] = val
            waits.append((key, val))

        def run(eng, waits=waits):
            for wk, wv in waits:
                eng.wait_ge(sems[wk], wv)

        self.q[e].append(run)


Prog.barrier = _prog_barrier


TT = 256
NTILES = NT // TT
DIN_E = 2816
DIN_O = 1792


_UNIQ = [0]


def mk_sb(nc, st):
    def sb(name, shape, dt=F32):
        _UNIQ[0] += 1
        return st.enter_context(nc.sbuf_tensor(f"{name}_{_UNIQ[0]}", list(shape), dt))
    return sb


class KC:
    pass


def declare_io(nc, debug):
    C = KC()
    C.nc = nc
    di = lambda name, shape: nc.dram_tensor(name, list(shape), F32, kind="ExternalInput").ap()
    C.xT = di("xT", [D, NT])
    C.cv = di("cv", [128, 16])
    C.ada_w = di("ada_w", [DEPTH, D, NMOD * D])
    C.ada_b = di("ada_b", [128, DEPTH * 72])
    C.norm_g = di("norm_g", [128, DEPTH * 3 * 8])
    C.final_g = di("final_g", [128, 8])
    C.ident = di("ident", [128, 128])
    C.ffn_gu = [di("ffn1_w_gu", [DEPTH, D, 2 * DFF]), di("ffn2_w_gu", [DEPTH, D, 2 * DFF])]
    C.ffn_dn = [di("ffn1_w_down", [DEPTH, DFF, D]), di("ffn2_w_down", [DEPTH, DFF, D])]
    C.outT = nc.dram_tensor("outT", [D, SEQ], F32, kind="ExternalOutput").ap()
    C.hT = nc.dram_tensor("hT", [D, NT], F32).ap()
    if debug:
        C.dbg = nc.dram_tensor("dbg", [D, NT], F32, kind="ExternalOutput").ap()
    return C


def fm(ap2d, t0, n):
    return ap2d.rearrange("(c p) t -> p c t", p=128)[:, :, t0:t0 + n]


def setup_persistent(C, P, st):
    nc = C.nc
    sb = mk_sb(nc, st)
    C.ps = [st.enter_context(nc.psum_tensor(f"ps{i}", [128, 512], F32)) for i in range(8)]
    C.bps = P.bufs(8, "ps")
    C.ident_sb = sb("ident_sb", [128, 128])
    C.ones_bf = sb("ones_bf", [128, 128], BF16)
    C.mod = sb("mod", [128, DEPTH, 72, 2])
    C.mA = sb("mA", [128, DEPTH, 3, 2, 8])
    C.mB = sb("mB", [128, DEPTH, 3, 2, 8])
    C.mG = sb("mG", [128, DEPTH, 3, 2, 8])
    C.g_sb = sb("g_sb", [128, DEPTH, 3, 8])
    C.fg_sb = sb("fg_sb", [128, 8])
    C.eps_sb = sb("eps_sb", [128, 1])
    C.b_const = P.buf("const")
    C.b_mod = P.buf("mod")
    C.hbuf = P.bufs(NTILES, "hT")
    C.xbuf = P.bufs(NTILES, "xT")
    C.obuf = P.bufs(NTILES, "outT")
    P.dma("sp", C.ident_sb[:], C.ident[:, :], w=[C.b_const])
    P.dma("sp", C.g_sb[:].rearrange("p a b c -> p (a b c)"), C.norm_g[:, :], w=[C.b_const])
    P.dma("sp", C.fg_sb[:], C.final_g[:, :], w=[C.b_const])
    P.op("dve", lambda e: e.memset(C.ones_bf[:], 1.0), w=[C.b_const])
    P.op("dve", lambda e: e.memset(C.eps_sb[:], EPS), w=[C.b_const])


def stage_ada(C, P):
    nc = C.nc
    with ExitStack() as st:
        sb = mk_sb(nc, st)
        cvs = sb("cvs", [128, 8, 2])
        cs = sb("cs", [128, 8, 2])
        adab = sb("adab", [128, DEPTH, 72])
        slabs = [sb(f"slab{i}", [128, 8, 1152]) for i in range(2)]
        b_cv, b_cs, b_ab = P.bufs(3, "ada")
        b_slab = P.bufs(2, "slab")
        P.dma("sp", cvs[:].rearrange("p a b -> p (a b)"), C.cv[:, :], w=[b_cv])
        P.dma("sp", adab[:].rearrange("p a b -> p (a b)"), C.ada_b[:, :], w=[b_ab])
        P.op("act", lambda e: e.activation(out=cs[:], in_=cvs[:], func=AF.Silu), r=[b_cv], w=[b_cs])
        ps0 = C.ps[0]
        n = 0
        for L in range(DEPTH):
            wv = C.ada_w[L].rearrange("(kc p) n -> p kc n", p=128)
            for s in range(8):
                slab = slabs[n % 2]
                bs = b_slab[n % 2]
                n += 1
                P.dma("sp", slab[:], wv[:, :, s * 1152:(s + 1) * 1152], w=[bs])
                for qq in range(9):
                    q = s * 9 + qq
                    for kc in range(8):
                        P.op("pe", lambda e, slab=slab, qq=qq, q=q, kc=kc: e.matmul(
                            ps0[:, 2 * q:2 * q + 2], lhsT=slab[:, kc, qq * 128:(qq + 1) * 128], rhs=cs[:, kc, :],
                            start=(kc == 0), stop=(kc == 7)), r=[bs, b_cs], w=[C.bps[0]])
            P.op("dve", lambda e, L=L: e.tensor_tensor(
                out=C.mod[:, L, :, :], in0=ps0[:, 0:144].rearrange("p (q v) -> p q v", v=2),
                in1=adab[:, L, :].unsqueeze(2).to_broadcast([128, 72, 2]), op=ALU.add),
                r=[C.bps[0], b_ab], w=[C.b_mod])
        for L in range(DEPTH):
            for s in range(3):
                for v in range(2):
                    P.op("dve", lambda e, L=L, s=s, v=v: e.scalar_tensor_tensor(
                        out=C.mA[:, L, s, v, :], in0=C.mod[:, L, (3 * s + 1) * 8:(3 * s + 1) * 8 + 8, v], scalar=1.0,
                        in1=C.g_sb[:, L, s, :], op0=ALU.add, op1=ALU.mult), r=[C.b_mod, C.b_const], w=[C.b_mod])
                    P.op("dve", lambda e, L=L, s=s, v=v: e.tensor_copy(
                        out=C.mB[:, L, s, v, :], in_=C.mod[:, L, (3 * s) * 8:(3 * s) * 8 + 8, v]),
                        r=[C.b_mod], w=[C.b_mod])
                    P.op("dve", lambda e, L=L, s=s, v=v: e.tensor_scalar(
                        out=C.mG[:, L, s, v, :], in0=C.mod[:, L, (3 * s + 2) * 8:(3 * s + 2) * 8 + 8, v],
                        scalar1=(1.0 if s == 1 else 0.5), scalar2=None, op0=ALU.mult), r=[C.b_mod], w=[C.b_mod])
        P.barrier()


def emit_norm(C, P, h, b_h, T, sqbuf, b_sq, rstd, b_rstd, psb):
    ps = C.ps[psb]
    P.op("act", lambda e: e.activation(out=sqbuf[:, :, :T], in_=h[:, :, :T], func=AF.Square), r=b_h, w=[b_sq])
    for c in range(8):
        P.op("pe", lambda e, c=c: e.matmul(ps[:, :T], lhsT=C.ones_bf[:], rhs=sqbuf[:, c, :T], start=(c == 0), stop=(c == 7)),
             r=[b_sq, C.b_const], w=[C.bps[psb]])
    P.op("act", lambda e: e.activation(out=rstd[:, :T], in_=ps[:, :T], func=AF.Sqrt, bias=C.eps_sb[:, 0:1], scale=1.0 / D),
         r=[C.bps[psb], C.b_const], w=[b_rstd])
    P.op("dve", lambda e: e.reciprocal(out=rstd[:, :T], in_=rstd[:, :T]), r=[b_rstd], w=[b_rstd])


def emit_modulate(C, P, h, b_h, y, b_y, rstd, b_rstd, tmps, b_tmps, L, s, v, T):
    for c in range(8):
        tmp = tmps[c % 2]
        bt = b_tmps[c % 2]
        P.op("dve", lambda e, c=c, tmp=tmp: e.scalar_tensor_tensor(
            out=tmp[:, :T], in0=h[:, c, :T], scalar=C.mA[:, L, s, v, c:c + 1], in1=rstd[:, :T],
            op0=ALU.mult, op1=ALU.mult), r=[b_h[c], b_rstd, C.b_mod], w=[bt])
        P.op("act", lambda e, c=c, tmp=tmp: e.activation(
            out=y[:, c, :T], in_=tmp[:, :T], func=AF.Identity, bias=C.mB[:, L, s, v, c:c + 1], scale=1.0),
            r=[bt, C.b_mod], w=[b_y])


def stage_ffn(C, P, L, s, src, srcbuf, tiles):
    nc = C.nc
    wgu_d = C.ffn_gu[s // 2][L]
    wdn_d = C.ffn_dn[s // 2][L]
    T = TT
    with ExitStack() as st:
        sb = mk_sb(nc, st)
        wgu = sb("wgu", [128, 8, 2 * DFF], BF16)
        wdn = sb("wdn", [128, 22, D], BF16)
        hs = [sb(f"h{i}", [128, 8, T]) for i in range(2)]
        ys = [sb(f"y{i}", [128, 8, T], BF16) for i in range(2)]
        actb = sb("actb", [128, 22, T], BF16)
        sq = sb("sq", [128, 8, T], BF16)
        rstd = sb("rstd", [128, T])
        tmps = [sb(f"tmp{i}", [128, T]) for i in range(2)]
        sgs = [sb(f"sg{i}", [128, T]) for i in range(2)]
        b_wgu = P.bufs(8, "wgu")
        b_wdn = P.bufs(22, "wdn")
        b_hs = [P.bufs(8, f"h{i}_") for i in range(2)]
        b_ys = P.bufs(2, "y")
        b_act = P.bufs(22, "act")
        b_sq, b_rstd = P.bufs(2, "nrm")
        b_tmps = P.bufs(2, "tmp")
        b_sgs = P.bufs(2, "sg")
        for kc in range(8):
            P.dma("pool", wgu[:, kc, :], wgu_d[kc * 128:(kc + 1) * 128, :], w=[b_wgu[kc]])
        for f in range(22):
            P.dma("pool", wdn[:, f, :], wdn_d[f * 128:(f + 1) * 128, :], w=[b_wdn[f]])

        def prologue(i):
            ti = tiles[i]
            h, bh, y, by = hs[i % 2], b_hs[i % 2], ys[i % 2], b_ys[i % 2]
            v = 1 if ti == 0 else 0
            P.dma("sp", h[:], fm(src, ti * T, T), r=[srcbuf[ti]], w=bh)
            emit_norm(C, P, h, bh, T, sq, b_sq, rstd, b_rstd, 0)
            emit_modulate(C, P, h, bh, y, by, rstd, b_rstd, tmps, b_tmps, L, s, v, T)

        prologue(0)
        for i, ti in enumerate(tiles):
            h, bh, y, by = hs[i % 2], b_hs[i % 2], ys[i % 2], b_ys[i % 2]
            v = 1 if ti == 0 else 0
            for f in range(22):
                pgb = 1 + 2 * (f % 2)
                pub = 2 + 2 * (f % 2)
                pg, pu = C.ps[pgb], C.ps[pub]
                for kc in range(8):
                    P.op("pe", lambda e, f=f, kc=kc, pg=pg, y=y: e.matmul(
                        pg[:, :T], lhsT=wgu[:, kc, f * 128:(f + 1) * 128], rhs=y[:, kc, :],
                        start=(kc == 0), stop=(kc == 7)), r=[b_wgu[kc], by], w=[C.bps[pgb]])
                for kc in range(8):
                    P.op("pe", lambda e, f=f, kc=kc, pu=pu, y=y: e.matmul(
                        pu[:, :T], lhsT=wgu[:, kc, DFF + f * 128:DFF + (f + 1) * 128], rhs=y[:, kc, :],
                        start=(kc == 0), stop=(kc == 7)), r=[b_wgu[kc], by], w=[C.bps[pub]])
                sg, bsg = sgs[f % 2], b_sgs[f % 2]
                P.op("act", lambda e, sg=sg, pg=pg: e.activation(out=sg[:], in_=pg[:, :T], func=AF.Silu),
                     r=[C.bps[pgb]], w=[bsg])
                P.op("dve", lambda e, f=f, sg=sg, pu=pu: e.tensor_tensor(
                    out=actb[:, f, :], in0=sg[:], in1=pu[:, :T], op=ALU.mult), r=[bsg, C.bps[pub]], w=[b_act[f]])
                if f == 10 and i + 1 < len(tiles):
                    prologue(i + 1)
            for dc in range(8):
                pob = 5 + dc % 3
                po = C.ps[pob]
                for f in range(22):
                    P.op("pe", lambda e, f=f, dc=dc, po=po: e.matmul(
                        po[:, :T], lhsT=wdn[:, f, dc * 128:(dc + 1) * 128], rhs=actb[:, f, :],
                        start=(f == 0), stop=(f == 21)), r=[b_wdn[f], b_act[f]], w=[C.bps[pob]])
                P.op("dve", lambda e, dc=dc, po=po, h=h, v=v: e.scalar_tensor_tensor(
                    out=h[:, dc, :], in0=po[:, :T], scalar=C.mG[:, L, s, v, dc:dc + 1], in1=h[:, dc, :],
                    op0=ALU.mult, op1=ALU.add), r=[C.bps[pob], bh[dc], C.b_mod], w=[bh[dc]])
            P.dma("sp", fm(C.hT, ti * T, T), h[:], r=bh, w=[C.hbuf[ti]])
        P.barrier()


def stage_final(C, P, tiles):
    nc = C.nc
    T = TT
    toks = []
    with ExitStack() as st:
        sb = mk_sb(nc, st)
        hs = [sb(f"fh{i}", [128, 8, T]) for i in range(2)]
        os_ = [sb(f"fo{i}", [128, 8, T]) for i in range(2)]
        sq = sb("fsq", [128, 8, T], BF16)
        rstd = sb("frstd", [128, T])
        b_hs = [P.bufs(8, f"fh{i}_") for i in range(2)]
        b_os = P.bufs(2, "fo")
        b_sq, b_rstd = P.bufs(2, "fn")
        for i, ti in enumerate(tiles):
            h, bh, o, bo = hs[i % 2], b_hs[i % 2], os_[i % 2], b_os[i % 2]
            P.dma("sp", h[:], fm(C.hT, ti * T, T), r=[C.hbuf[ti]], w=bh)
            emit_norm(C, P, h, bh, T, sq, b_sq, rstd, b_rstd, 0)
            for c in range(8):
                P.op("dve", lambda e, c=c, h=h, o=o: e.scalar_tensor_tensor(
                    out=o[:, c, :], in0=h[:, c, :], scalar=C.fg_sb[:, c:c + 1], in1=rstd[:],
                    op0=ALU.mult, op1=ALU.mult), r=[bh[c], b_rstd, C.b_const], w=[bo])
            toks.append(P.dma("sp", fm(C.outT, (ti - 1) * T, T), o[:], r=[bo], w=[C.obuf[ti]]))
        P.barrier()
    return toks


def build_program(stages, debug=False):
    nc = bass.Bass("TRN2", target_bir_lowering=False)
    C = declare_io(nc, debug)
    declare_odd(C)
    declare_even(C)
    P = Prog(nc)
    all_tiles = list(range(NTILES))
    lat_tiles = list(range(1, NTILES))
    with ExitStack() as st:
        setup_persistent(C, P, st)
        src, srcbuf = C.xT, C.xbuf
        toks = []
        for sg in stages:
            kind = sg[0]
            if kind == "ada":
                stage_ada(C, P)
            elif kind == "ffn":
                _, L, s, with_ctx = sg
                stage_ffn(C, P, L, s, src, srcbuf, all_tiles if with_ctx else lat_tiles)
                src, srcbuf = C.hT, C.hbuf
            elif kind == "odd":
                _, L, with_ctx = sg
                stage_odd(C, P, L, all_tiles if with_ctx else lat_tiles)
            elif kind == "even":
                _, L = sg
                zbuf = P.bufs(NT // 128, "ztok")
                fbuf = P.bufs(NTILES, "fT")
                stage_even_A(C, P, L, all_tiles, zbuf, fbuf)
                stage_rwkv(C, P, L)
                stage_even_C(C, P, L, all_tiles)
            elif kind == "evenA":
                _, L = sg
                zbuf = P.bufs(NT // 128, "ztok")
                fbuf = P.bufs(NTILES, "fT")
                stage_even_A(C, P, L, all_tiles, zbuf, fbuf)
            elif kind == "final":
                toks += stage_final(C, P, lat_tiles)
            else:
                raise ValueError(kind)
        if debug:
            for ti in range(NTILES):
                toks.append(P.dma("sp", C.dbg[:, ti * TT:(ti + 1) * TT], C.hT[:, ti * TT:(ti + 1) * TT], r=[C.hbuf[ti]], w=[P.buf()]))
        P.finish("sp", toks)
        P.emit(st)
    return nc


def chunked(v, n=128):
    v = np.asarray(v, np.float32)
    lead = v.shape[:-1]
    k = v.shape[-1] // n
    return np.ascontiguousarray(np.moveaxis(v.reshape(lead + (k, n)), -1, 0))


def prep_inputs(inp, ncores=8):
    f = lambda a: np.ascontiguousarray(np.asarray(a, np.float32))
    shared = {
        "ada_w": f(inp["ada_w"]),
        "ada_b": chunked(inp["ada_b"]).reshape(128, -1),
        "norm_g": chunked(inp["norm_g"]).reshape(128, -1),
        "final_g": chunked(inp["final_g"]).reshape(128, -1),
        "ident": np.eye(128, dtype=np.float32),
        "ffn1_w_gu": f(inp["ffn1_w_gu"]), "ffn2_w_gu": f(inp["ffn2_w_gu"]),
        "ffn1_w_down": f(inp["ffn1_w_down"]), "ffn2_w_down": f(inp["ffn2_w_down"]),
    }
    prep_odd(inp, shared)
    prep_even(inp, shared)
    maps = []
    cc = chunked(inp["c_ctx"])
    for b in range(ncores):
        m = dict(shared)
        m["xT"] = np.ascontiguousarray(np.concatenate([np.asarray(inp["ctx"][b]).T, np.asarray(inp["x"][b]).T], axis=1), dtype=np.float32)
        cb = chunked(inp["c"][b])
        m["cv"] = np.ascontiguousarray(np.stack([cb, cc], axis=-1).reshape(128, 16))
        maps.append(m)
    return maps


CONVW = 31
SEQS = {0: (0, CTX), 1: (CTX, NT)}


def seq_of_tile(ti):
    return 0 if ti == 0 else 1


def declare_odd(C):
    nc = C.nc
    di = lambda name, shape: nc.dram_tensor(name, list(shape), F32, kind="ExternalInput").ap()
    C.o_w_in = di("o_w_in", [2, D, DIN_O])
    C.o_w_out = di("o_w_out", [2, D, D])
    C.o_cw = di("o_cw", [128, 2 * 6 * CONVW])
    C.o_small = di("o_small", [128, 2 * 14])
    C.o_pwbd = di("o_pwbd", [128, 2 * 2 * 128])
    C.o_mask = di("o_mask", [128, 2 * 16])
    C.o_invc = di("o_invc", [256, NT])
    C.uT = nc.dram_tensor("uT", [768, NT], F32).ap()
    C.qT = nc.dram_tensor("qT", [256, NT], F32).ap()


def stage_odd(C, P, L, tiles):
    nc = C.nc
    j = L // 2
    T = TT
    ubuf = P.bufs(NTILES, "uT")
    qbuf = P.bufs(NTILES, "qT")
    with ExitStack() as st:
        sb = mk_sb(nc, st)
        win = sb("o_win", [128, 8, DIN_O], BF16)
        wout = sb("o_wout", [128, 8, D], BF16)
        cw = sb("o_cw", [128, 6, CONVW])
        small = sb("o_small", [128, 14])
        pwbd = sb("o_pwbd", [128, 2, 128])
        mask = sb("o_mask", [128, 2, 16])
        b_win = P.bufs(8, "owin")
        b_wout = P.bufs(8, "owout")
        b_oc = P.buf("oconst")
        for kc in range(8):
            P.dma("pool", win[:, kc, :], C.o_w_in[j, kc * 128:(kc + 1) * 128, :], w=[b_win[kc]])
        for kc in range(8):
            P.dma("pool", wout[:, kc, :], C.o_w_out[j, kc * 128:(kc + 1) * 128, :], w=[b_wout[kc]])
        P.dma("sp", cw[:].rearrange("p a b -> p (a b)"), C.o_cw[:, j * 186:(j + 1) * 186], w=[b_oc])
        P.dma("sp", small[:], C.o_small[:, j * 14:(j + 1) * 14], w=[b_oc])
        P.dma("sp", pwbd[:].rearrange("p a b -> p (a b)"), C.o_pwbd[:, j * 256:(j + 1) * 256], w=[b_oc])
        P.dma("sp", mask[:].rearrange("p a b -> p (a b)"), C.o_mask[:, :], w=[b_oc])
        hs = [sb(f"oh{i}", [128, 8, T]) for i in range(2)]
        ys = [sb(f"oy{i}", [128, 8, T], BF16) for i in range(2)]
        us = [sb(f"ou{i}", [128, 6, T]) for i in range(2)]
        qs = [sb(f"oq{i}", [128, 2, T]) for i in range(2)]
        sq = sb("osq", [128, 8, T], BF16)
        rstd = sb("orstd", [128, T])
        tmps = [sb(f"otmp{i}", [128, T]) for i in range(2)]
        sgs = [sb(f"osg{i}", [128, T]) for i in range(2)]
        b_hs = [P.bufs(8, f"oh{i}_") for i in range(2)]
        b_ys = P.bufs(2, "oy")
        b_us = P.bufs(2, "ou")
        b_qs = P.bufs(2, "oq")
        b_sq, b_rstd = P.bufs(2, "onrm")
        b_tmps = P.bufs(2, "otmp")
        b_sgs = P.bufs(2, "osg")

        def prologue(i):
            ti = tiles[i]
            h, bh, y, by = hs[i % 2], b_hs[i % 2], ys[i % 2], b_ys[i % 2]
            v = 1 if ti == 0 else 0
            P.dma("sp", h[:], fm(C.hT, ti * T, T), r=[C.hbuf[ti]], w=bh)
            emit_norm(C, P, h, bh, T, sq, b_sq, rstd, b_rstd, 0)
            emit_modulate(C, P, h, bh, y, by, rstd, b_rstd, tmps, b_tmps, L, 1, v, T)

        prologue(0)
        for i, ti in enumerate(tiles):
            y, by = ys[i % 2], b_ys[i % 2]
            u, bu, q, bq = us[i % 2], b_us[i % 2], qs[i % 2], b_qs[i % 2]
            if i + 1 < len(tiles):
                prologue(i + 1)
            for oc in range(6):
                pab = 1 + 2 * (oc % 2)
                pbb = 2 + 2 * (oc % 2)
                pa, pb = C.ps[pab], C.ps[pbb]
                for kc in range(8):
                    P.op("pe", lambda e, oc=oc, kc=kc, pa=pa, y=y: e.matmul(
                        pa[:, :T], lhsT=win[:, kc, oc * 128:(oc + 1) * 128], rhs=y[:, kc, :],
                        start=(kc == 0), stop=(kc == 7)), r=[b_win[kc], by], w=[C.bps[pab]])
                for kc in range(8):
                    P.op("pe", lambda e, oc=oc, kc=kc, pb=pb, y=y: e.matmul(
                        pb[:, :T], lhsT=win[:, kc, 768 + oc * 128:768 + (oc + 1) * 128], rhs=y[:, kc, :],
                        start=(kc == 0), stop=(kc == 7)), r=[b_win[kc], by], w=[C.bps[pbb]])
                sg, bsg = sgs[oc % 2], b_sgs[oc % 2]
                P.op("act", lambda e, sg=sg, pb=pb: e.activation(out=sg[:], in_=pb[:, :T], func=AF.Sigmoid),
                     r=[C.bps[pbb]], w=[bsg])
                P.op("dve", lambda e, oc=oc, sg=sg, pa=pa, u=u: e.tensor_tensor(
                    out=u[:, oc, :], in0=sg[:], in1=pa[:, :T], op=ALU.mult), r=[bsg, C.bps[pab]], w=[bu])
            for qc in range(2):
                pqb = 5 + qc
                pq = C.ps[pqb]
                for kc in range(8):
                    P.op("pe", lambda e, qc=qc, kc=kc, pq=pq, y=y: e.matmul(
                        pq[:, :T], lhsT=win[:, kc, 1536 + qc * 128:1536 + (qc + 1) * 128], rhs=y[:, kc, :],
                        start=(kc == 0), stop=(kc == 7)), r=[b_win[kc], by], w=[C.bps[pqb]])
                P.op("act", lambda e, qc=qc, pq=pq, q=q: e.activation(out=q[:, qc, :], in_=pq[:, :T], func=AF.Copy),
                     r=[C.bps[pqb]], w=[bq])
            P.dma("pool", C.uT.rearrange("(c p) t -> p c t", p=128)[:, :, ti * T:(ti + 1) * T], u[:], r=[bu], w=[ubuf[ti]])
            P.dma("pool", C.qT.rearrange("(c p) t -> p c t", p=128)[:, :, ti * T:(ti + 1) * T], q[:], r=[bq], w=[qbuf[ti]])
        HU = T + 30
        HQ = T + 16
        uh = [sb(f"ouh{i}", [128, 6, HU]) for i in range(2)]
        qh = [sb(f"oqh{i}", [128, 2, HQ]) for i in range(2)]
        ic = [sb(f"oic{i}", [128, 2, T]) for i in range(2)]
        cv = [sb(f"ocv{i}", [128, 6, T]) for i in range(2)]
        pl = [sb(f"opl{i}", [128, 2, T]) for i in range(2)]
        cat = [sb(f"ocat{i}", [128, 8, T], BF16) for i in range(2)]
        sq6 = sb("osq6", [128, 6, T], BF16)
        b_uh = P.bufs(2, "ouh")
        b_qh = P.bufs(2, "oqh")
        b_ic = P.bufs(2, "oic")
        b_cv = [P.bufs(6, f"ocv{i}_") for i in range(2)]
        b_pl = [P.bufs(2, f"opl{i}_") for i in range(2)]
        b_cat = [P.bufs(8, f"ocat{i}_") for i in range(2)]

        def loadsB(i):
            ti = tiles[i]
            s0, s1 = SEQS[seq_of_tile(ti)]
            t0 = ti * T
            k = i % 2
            for (buf, bb, dram, nch, hl, hr, tb) in ((uh[k], b_uh[k], C.uT, 6, 15, 15, ubuf), (qh[k], b_qh[k], C.qT, 2, 8, 7, qbuf)):
                lo, hi = max(t0 - hl, s0), min(t0 + T + hr, s1)
                if lo > t0 - hl or hi < t0 + T + hr:
                    P.op("pool", lambda e, buf=buf: e.memset(buf[:], 0.0), w=[bb])
                deps = [tb[x] for x in (ti - 1, ti, ti + 1) if 0 <= x < NTILES and seq_of_tile(x) == seq_of_tile(ti)]
                P.dma("sp", buf[:, :, lo - (t0 - hl):hi - (t0 - hl)],
                      dram.rearrange("(c p) t -> p c t", p=128)[:, :, lo:hi], r=deps, w=[bb])
            P.dma("sp", ic[k][:], C.o_invc.rearrange("(c p) t -> p c t", p=128)[:, :, t0:t0 + T], w=[b_ic[k]])
            P.dma("sp", hs[k][:], fm(C.hT, t0, T), r=[C.hbuf[ti]], w=b_hs[k])

        loadsB(0)
        for i, ti in enumerate(tiles):
            k = i % 2
            v = 1 if ti == 0 else 0
            if i + 1 < len(tiles):
                loadsB(i + 1)
            u_h, q_h, icv, conv, pool, ct, h, bh = uh[k], qh[k], ic[k], cv[k], pl[k], cat[k], hs[k], b_hs[k]
            for c in range(6):
                P.op("dve", lambda e, c=c, conv=conv, u_h=u_h: e.tensor_scalar(
                    out=conv[:, c, :], in0=u_h[:, c, 0:T], scalar1=cw[:, c, 0:1], scalar2=small[:, c:c + 1],
                    op0=ALU.mult, op1=ALU.add), r=[b_uh[k], b_oc], w=[b_cv[k][c]])
                for kk in range(1, CONVW):
                    P.op("dve", lambda e, c=c, kk=kk, conv=conv, u_h=u_h: e.scalar_tensor_tensor(
                        out=conv[:, c, :], in0=u_h[:, c, kk:kk + T], scalar=cw[:, c, kk:kk + 1], in1=conv[:, c, :],
                        op0=ALU.mult, op1=ALU.add), r=[b_uh[k], b_oc, b_cv[k][c]], w=[b_cv[k][c]])
            P.op("act", lambda e, conv=conv: e.activation(out=sq6[:], in_=conv[:], func=AF.Square), r=b_cv[k], w=[b_sq])
            for c in range(6):
                P.op("pe", lambda e, c=c: e.matmul(C.ps[0][:, :T], lhsT=C.ones_bf[:], rhs=sq6[:, c, :], start=(c == 0), stop=(c == 5)),
                     r=[b_sq, C.b_const], w=[C.bps[0]])
            P.op("act", lambda e: e.activation(out=rstd[:], in_=C.ps[0][:, :T], func=AF.Sqrt, bias=C.eps_sb[:, 0:1], scale=1.0 / 768),
                 r=[C.bps[0], C.b_const], w=[b_rstd])
            P.op("dve", lambda e: e.reciprocal(out=rstd[:], in_=rstd[:]), r=[b_rstd], w=[b_rstd])
            for c in range(6):
                tmp, bt = tmps[c % 2], b_tmps[c % 2]
                P.op("dve", lambda e, c=c, tmp=tmp, conv=conv: e.scalar_tensor_tensor(
                    out=tmp[:], in0=conv[:, c, :], scalar=small[:, 6 + c:7 + c], in1=rstd[:], op0=ALU.mult, op1=ALU.mult),
                    r=[b_cv[k][c], b_rstd, b_oc], w=[bt])
                P.op("act", lambda e, c=c, tmp=tmp, ct=ct: e.activation(out=ct[:, c, :], in_=tmp[:], func=AF.Silu),
                     r=[bt], w=[b_cat[k][c]])
            for c in range(2):
                koffs = range(6, 10) if c == 0 else range(0, 16)
                first = True
                for kk in koffs:
                    if first:
                        P.op("dve", lambda e, c=c, kk=kk, pool=pool, q_h=q_h: e.tensor_scalar(
                            out=pool[:, c, :], in0=q_h[:, c, kk:kk + T], scalar1=mask[:, c, kk:kk + 1], scalar2=None, op0=ALU.mult),
                            r=[b_qh[k], b_oc], w=[b_pl[k][c]])
                        first = False
                    else:
                        P.op("dve", lambda e, c=c, kk=kk, pool=pool, q_h=q_h: e.scalar_tensor_tensor(
                            out=pool[:, c, :], in0=q_h[:, c, kk:kk + T], scalar=mask[:, c, kk:kk + 1], in1=pool[:, c, :],
                            op0=ALU.mult, op1=ALU.add), r=[b_qh[k], b_oc, b_pl[k][c]], w=[b_pl[k][c]])
                P.op("dve", lambda e, c=c, pool=pool, icv=icv: e.tensor_tensor(
                    out=pool[:, c, :], in0=pool[:, c, :], in1=icv[:, c, :], op=ALU.mult), r=[b_pl[k][c], b_ic[k]], w=[b_pl[k][c]])
                P.op("dve", lambda e, c=c, pool=pool, q_h=q_h: e.tensor_tensor(
                    out=pool[:, c, :], in0=pool[:, c, :], in1=q_h[:, c, 8:8 + T], op=ALU.subtract), r=[b_pl[k][c], b_qh[k]], w=[b_pl[k][c]])
                ppb = 1 + c
                P.op("pe", lambda e, c=c, pool=pool, ppb=ppb: e.matmul(C.ps[ppb][:, :T], lhsT=pwbd[:, c, :], rhs=pool[:, c, :], start=True, stop=True),
                     r=[b_pl[k][c], b_oc], w=[C.bps[ppb]])
                P.op("act", lambda e, c=c, ct=ct, ppb=ppb: e.activation(
                    out=ct[:, 6 + c, :], in_=C.ps[ppb][:, :T], func=AF.Identity, scale=small[:, 12 + c:13 + c]),
                    r=[C.bps[ppb], b_oc], w=[b_cat[k][6 + c]])
            for dc in range(8):
                pob = 3 + dc % 4
                po = C.ps[pob]
                for kc in range(8):
                    P.op("pe", lambda e, dc=dc, kc=kc, po=po, ct=ct: e.matmul(
                        po[:, :T], lhsT=wout[:, kc, dc * 128:(dc + 1) * 128], rhs=ct[:, kc, :],
                        start=(kc == 0), stop=(kc == 7)), r=[b_wout[kc], b_cat[k][kc]], w=[C.bps[pob]])
                P.op("dve", lambda e, dc=dc, po=po, h=h, v=v: e.scalar_tensor_tensor(
                    out=h[:, dc, :], in0=po[:, :T], scalar=C.mG[:, L, 1, v, dc:dc + 1], in1=h[:, dc, :],
                    op0=ALU.mult, op1=ALU.add), r=[C.bps[pob], bh[dc], C.b_mod], w=[bh[dc]])
            P.dma("pool", fm(C.hT, ti * T, T), h[:], r=bh, w=[C.hbuf[ti]])
        P.barrier()


def prep_odd(inp, shared):
    cw = np.asarray(inp["o_conv_w"], np.float32)
    a = cw.transpose(0, 2, 1).reshape(2, 6, 128, CONVW)
    shared["o_cw"] = np.ascontiguousarray(a.transpose(2, 0, 1, 3).reshape(128, -1))
    sm = np.concatenate([chunked(inp["o_conv_b"]), chunked(inp["o_cnorm_g"]), chunked(inp["o_pool_scale"])], axis=-1)
    shared["o_small"] = np.ascontiguousarray(sm.reshape(128, -1))
    pw = np.asarray(inp["o_pool_w"], np.float32)
    bd = np.zeros((128, 2, 2, 128), np.float32)
    for j in range(2):
        for c in range(2):
            for gi in range(2):
                bd[gi * 64:(gi + 1) * 64, j, c, gi * 64:(gi + 1) * 64] = pw[j, 2 * c + gi]
    shared["o_pwbd"] = np.ascontiguousarray(bd.reshape(128, -1))
    widths = (2, 4, 8, 16)
    mask = np.zeros((128, 2, 16), np.float32)
    invc = np.zeros((256, NT), np.float32)
    for g, wd in enumerate(widths):
        c, gi = g // 2, g % 2
        for kk in range(16):
            off = kk - 8
            if -(wd // 2) <= off <= wd - 1 - wd // 2:
                mask[gi * 64:(gi + 1) * 64, c, kk] = 1.0
        for (s0, n) in ((0, CTX), (CTX, SEQ)):
            t = np.arange(n)
            lo = np.maximum(t - wd // 2, 0)
            hi = np.minimum(t + (wd - 1 - wd // 2), n - 1)
            invc[g * 64:(g + 1) * 64, s0:s0 + n] = (1.0 / (hi - lo + 1).astype(np.float32))[None, :]
    shared["o_mask"] = np.ascontiguousarray(mask.reshape(128, -1))
    shared["o_invc"] = invc
    shared["o_w_in"] = np.ascontiguousarray(np.asarray(inp["o_w_in"], np.float32))
    shared["o_w_out"] = np.ascontiguousarray(np.asarray(inp["o_w_out"], np.float32))


import ml_dtypes

DA = 2560
DR = 768
NH = 12
SC = 128
DECAY = 0.606531
GN_EPS = 64e-5
TABW = 9 * DR
RWKV_LIMIT = 0


def declare_even(C):
    nc = C.nc
    di = lambda name, shape, dt=F32: nc.dram_tensor(name, list(shape), dt, kind="ExternalInput").ap()
    C.e_w_in = di("e_w_in", [2, D, DIN_E])
    C.e_w_out = di("e_w_out", [2, D, D])
    C.e_mu = di("e_mu", [128, 2 * DA])
    C.e_tab = di("e_tab", [128, 2 * TABW])
    C.e_w_up = di("e_w_up", [2, 2, 64, DR])
    C.e_a_up = di("e_a_up", [2, 2, 64, DR])
    C.e_g_up = di("e_g_up", [2, 128, DR])
    C.e_masks = di("e_masks", [128, 2 * 768])
    C.cosT = di("cosT", [SEQ, SEQ], BF16)
    C.sinT = di("sinT", [SEQ, SEQ], BF16)
    C.cosc = di("cosc", [CTX, CTX], BF16)
    C.sinc = di("sinc", [CTX, CTX], BF16)
    C.cdft = di("cdft", [128, 4 * 128])
    C.z_tok = nc.dram_tensor("z_tok", [NT, DIN_E], F32).ap()
    C.fT = nc.dram_tensor("fT", [256, NT], F32).ap()
    C.yf_tok = nc.dram_tensor("yf_tok", [NT, DR], F32).ap()
    C.o_tok = nc.dram_tensor("o_tok", [NT, DR], F32).ap()


def stage_even_A(C, P, L, tiles, zbuf, fbuf):
    nc = C.nc
    j = L // 2
    T = TT
    with ExitStack() as st:
        sb = mk_sb(nc, st)
        win = sb("e_win", [128, 8, DIN_E], BF16)
        b_win = P.bufs(8, "ewin")
        for kc in range(8):
            P.dma("pool", win[:, kc, :], C.e_w_in[j, kc * 128:(kc + 1) * 128, :], w=[b_win[kc]])
        hs = [sb(f"eh{i}", [128, 8, T]) for i in range(2)]
        ys = [sb(f"ey{i}", [128, 8, T], BF16) for i in range(2)]
        zt = [sb(f"ezt{i}", [128, DIN_E]) for i in range(2)]
        sq = sb("esq", [128, 8, T], BF16)
        rstd = sb("erstd", [128, T])
        tmps = [sb(f"etmp{i}", [128, T]) for i in range(2)]
        b_hs = [P.bufs(8, f"eh{i}_") for i in range(2)]
        b_ys = P.bufs(2, "ey")
        b_zt = P.bufs(2, "ezt")
        b_sq, b_rstd = P.bufs(2, "enrm")
        b_tmps = P.bufs(2, "etmp")

        def prologue(i):
            ti = tiles[i]
            h, bh, y, by = hs[i % 2], b_hs[i % 2], ys[i % 2], b_ys[i % 2]
            v = 1 if ti == 0 else 0
            P.dma("sp", h[:], fm(C.hT, ti * T, T), r=[C.hbuf[ti]], w=bh)
            emit_norm(C, P, h, bh, T, sq, b_sq, rstd, b_rstd, 0)
            emit_modulate(C, P, h, bh, y, by, rstd, b_rstd, tmps, b_tmps, L, 1, v, T)

        colblocks = [(c0, min(512, DIN_E - c0)) for c0 in range(0, DIN_E, 512)]
        prologue(0)
        nz = 0
        nps = 0
        for i, ti in enumerate(tiles):
            y, by = ys[i % 2], b_ys[i % 2]
            if i + 1 < len(tiles):
                prologue(i + 1)
            for tb in range(T // 128):
                z, bz = zt[nz % 2], b_zt[nz % 2]
                nz += 1
                for (c0, wd) in colblocks:
                    pb = 1 + nps % 7
                    nps += 1
                    ps = C.ps[pb]
                    for kc in range(8):
                        P.op("pe", lambda e, kc=kc, ps=ps, y=y, tb=tb, c0=c0, wd=wd: e.matmul(
                            ps[:, :wd], lhsT=y[:, kc, tb * 128:(tb + 1) * 128], rhs=win[:, kc, c0:c0 + wd],
                            start=(kc == 0), stop=(kc == 7)), r=[b_win[kc], by], w=[C.bps[pb]])
                    if nps % 2 == 0:
                        P.op("act", lambda e, ps=ps, z=z, c0=c0, wd=wd: e.activation(out=z[:, c0:c0 + wd], in_=ps[:, :wd], func=AF.Copy),
                             r=[C.bps[pb]], w=[bz])
                    else:
                        P.op("dve", lambda e, ps=ps, z=z, c0=c0, wd=wd: e.tensor_copy(out=z[:, c0:c0 + wd], in_=ps[:, :wd]),
                             r=[C.bps[pb]], w=[bz])
                r0 = ti * T + tb * 128
                P.dma("pool", C.z_tok[r0:r0 + 128, :], z[:], r=[bz], w=[zbuf[r0 // 128]])
        P.barrier()
    with ExitStack() as st:
        sb = mk_sb(nc, st)
        cd = sb("cdft", [128, 4, 128], BF16)
        b_cd = P.buf("cdft")
        P.dma("pool", cd[:].rearrange("p a b -> p (a b)"), C.cdft[:, :], w=[b_cd])
        seqs = []
        if 0 in tiles:
            seqs.append((0, CTX, C.cosc, C.sinc, 2, 3, CTX))
        seqs.append((CTX, SEQ, C.cosT, C.sinT, 0, 1, 512))
        zf = sb("zf", [128, SEQ // 128, 256], BF16)
        cosb = [sb(f"cosb{i}", [128, SEQ // 128, 512], BF16) for i in range(2)]
        sinb = [sb(f"sinb{i}", [128, SEQ // 128, 512], BF16) for i in range(2)]
        pq = [sb(f"pq{i}", [128, 2, 512], BF16) for i in range(2)]
        fo = [sb(f"fo{i}", [128, 512]) for i in range(2)]
        b_zf = P.buf("zf")
        b_cos = P.bufs(2, "cosb")
        b_sin = P.bufs(2, "sinb")
        b_pq = P.bufs(2, "pq")
        b_fo = P.bufs(2, "fo")
        nk = 0
        nf = 0
        for (s0, N, ctab, stab, ic, isn, KW) in seqs:
            nch = N // 128
            P.dma("pool", zf[:, :nch, :], C.z_tok[s0:s0 + N, DA:DA + 256].rearrange("(nc p) c -> p nc c", p=128),
                  r=[zbuf[(s0 // 128) + x] for x in range(nch)], w=[b_zf])
            for kb in range(N // KW):
                cb, sbb = cosb[nk % 2], sinb[nk % 2]
                bc, bs = b_cos[nk % 2], b_sin[nk % 2]
                nk += 1
                P.dma("sp", cb[:, :nch, :KW], ctab[:, kb * KW:(kb + 1) * KW].rearrange("(nc p) k -> p nc k", p=128), w=[bc])
                P.dma("sp", sbb[:, :nch, :KW], stab[:, kb * KW:(kb + 1) * KW].rearrange("(nc p) k -> p nc k", p=128), w=[bs])
                for hh in range(2):
                    pp, bpq = pq[nf % 2], b_pq[nf % 2]
                    f, bf = fo[nf % 2], b_fo[nf % 2]
                    nf += 1
                    for n_ in range(nch):
                        P.op("pe", lambda e, n_=n_, cb=cb, hh=hh, KW=KW, nch=nch: e.matmul(
                            C.ps[1][:, :KW], lhsT=zf[:, n_, hh * 128:(hh + 1) * 128], rhs=cb[:, n_, :KW],
                            start=(n_ == 0), stop=(n_ == nch - 1)), r=[b_zf, bc], w=[C.bps[1]])
                    for n_ in range(nch):
                        P.op("pe", lambda e, n_=n_, sbb=sbb, hh=hh, KW=KW, nch=nch: e.matmul(
                            C.ps[2][:, :KW], lhsT=zf[:, n_, hh * 128:(hh + 1) * 128], rhs=sbb[:, n_, :KW],
                            start=(n_ == 0), stop=(n_ == nch - 1)), r=[b_zf, bs], w=[C.bps[2]])
                    P.op("act", lambda e, pp=pp, KW=KW: e.activation(out=pp[:, 0, :KW], in_=C.ps[1][:, :KW], func=AF.Copy),
                         r=[C.bps[1]], w=[bpq])
                    P.op("dve", lambda e, pp=pp, KW=KW: e.tensor_copy(out=pp[:, 1, :KW], in_=C.ps[2][:, :KW]),
                         r=[C.bps[2]], w=[bpq])
                    pfb = 3 + nf % 2
                    P.op("pe", lambda e, pp=pp, KW=KW, pfb=pfb, ic=ic: e.matmul(
                        C.ps[pfb][:, :KW], lhsT=cd[:, ic, :], rhs=pp[:, 0, :KW], start=True, stop=False),
                        r=[b_cd, bpq], w=[C.bps[pfb]])
                    P.op("pe", lambda e, pp=pp, KW=KW, pfb=pfb, isn=isn: e.matmul(
                        C.ps[pfb][:, :KW], lhsT=cd[:, isn, :], rhs=pp[:, 1, :KW], start=False, stop=True),
                        r=[b_cd, bpq], w=[C.bps[pfb]])
                    P.op("act", lambda e, f=f, KW=KW, pfb=pfb: e.activation(out=f[:, :KW], in_=C.ps[pfb][:, :KW], func=AF.Copy),
                         r=[C.bps[pfb]], w=[bf])
                    c0 = s0 + kb * KW
                    P.dma("pool", C.fT[hh * 128:(hh + 1) * 128, c0:c0 + KW], f[:, :KW], r=[bf],
                          w=[fbuf[(c0 // TT) + x] for x in range(max(1, KW // TT))])
        P.barrier()


def stage_rwkv(C, P, L):
    nc = C.nc
    j = L // 2
    colmajor = (j % 2 == 1)
    NSC = NT // SC
    ybuf = P.bufs(NSC, "yf")

    def OP(eng, method, r, w, *args, **kw):
        return P.op(eng, lambda e: getattr(e, method)(*args, **kw), r=r, w=w)

    def row_runs(seq, sc, delta):
        N = CTX if seq == 0 else SEQ
        lo = sc * SC + delta
        hi = lo + SC
        a, b = max(lo, 0), min(hi, N)
        runs = []
        if seq == 0 or not colmajor:
            base = 0 if seq == 0 else CTX
            runs.append((a - lo, b - lo, ("lin", base + a, base + b)))
        else:
            pos = a
            while pos < b:
                w_ = pos // 64
                e_ = min(b, (w_ + 1) * 64)
                runs.append((pos - lo, e_ - lo, ("cm", w_, pos % 64, pos % 64 + (e_ - pos))))
                pos = e_
        return runs, (a > lo or b < hi)

    def dram_rows(tens, spec, c0, c1):
        if spec[0] == "lin":
            return tens[spec[1]:spec[2], c0:c1]
        _, w_, r0, r1 = spec
        return tens[CTX:NT, c0:c1].rearrange("(r w) c -> w r c", w=64)[w_, r0:r1, :]

    with ExitStack() as st:
        sb = mk_sb(nc, st)
        mu = sb("r_mu", [128, DA])
        tab = sb("r_tab", [128, 9, DR])
        wa_up = sb("r_waup", [128, 2, DR])
        gup = sb("r_gup", [128, DR])
        masks = sb("r_masks", [128, 2, 768])
        ones = sb("r_ones", [128, 128])
        gneps = sb("r_gneps", [128, 1])
        zcs = [sb(f"r_zc{i}", [128, DA]) for i in range(2)]
        zp = sb("r_zp", [128, DA])
        zn = sb("r_zn", [128, DA])
        lrT = sb("r_lrT", [128, 128])
        sgT = sb("r_sgT", [128, 128])
        S = {i: sb(f"r_S{i}", [128, DR]) for i in (1, 2, 3, 4, 6, 7)}
        S8 = [sb(f"r_S8_{i}", [128, DR]) for i in range(2)]
        S8b = [sb(f"r_S8b_{i}", [128, DR], BF16) for i in range(2)]
        S5 = [sb("r_S5", [128, DR])] * 2
        NB = [sb(f"r_NB_{i}", [128, DR], BF16) for i in range(2)]
        KH = sb("r_KH", [128, DR], BF16)
        vb = sb("r_vb", [128, DR], BF16)
        FMb = sb("r_FMb", [128, 6, 128], BF16)
        FMk = sb("r_FMk", [128, 6, 128], BF16)
        FMkq = sb("r_FMkq", [128, 6, 128], BF16)
        FMr = [sb(f"r_FMr{i}", [128, 6, 128], BF16) for i in range(2)]
        XC = sb("r_XC", [128, NH, 2, 128], BF16)
        BD = sb("r_BD", [128, NH, 2, 128], BF16)
        Yt = sb("r_Yt", [128, NH, 128], BF16)
        Zt = sb("r_Zt", [128, NH, 128], BF16)
        Zf = sb("r_Zf", [128, NH, 128])
        KqT = sb("r_KqT", [128, 6, 128], BF16)
        BU = sb("r_BU", [128, DR], BF16)
        Hb = [sb(f"r_Hb{i}", [128, 6, 64], BF16) for i in range(2)]
        UY = sb("r_UY", [128, DR])
        Gs = sb("r_G", [128, 6, 64])
        Ylocs = sb("r_Yloc", [128, DR])
        Hs = [sb(f"r_H{i}", [128, 6, 64]) for i in range(2)]
        tmpH = sb("r_tmpH", [128, 6, 64])
        pL = [sb(f"r_pL{i}", [128, 8]) for i in range(2)]
        stt_ = sb("r_st", [128, 4, NH])
        yft = [sb(f"r_yf{i}", [128, DR]) for i in range(2)]
        BON = [sb(f"r_bon{i}", [128, DR]) for i in range(2)]
        Gt = [sb(f"r_g{i}", [128, DR]) for i in range(2)]

        b_k = P.buf("r_const")
        b_zp, b_zn, b_lr, b_sg = P.bufs(4, "r_z")
        b_zcs = P.bufs(2, "r_zc")
        b_S = {i: P.buf(f"r_S{i}") for i in (1, 2, 3, 4, 6, 7)}
        b_S8 = P.bufs(2, "r_S8")
        b_S8b = P.bufs(2, "r_S8b")
        b_S5 = [P.buf("r_S5")] * 2
        b_NB = P.bufs(2, "r_NB")
        b_KH, b_vb = P.bufs(2, "r_khvb")
        b_Hb = P.bufs(2, "r_Hb")
        b_FMb, b_FMk, b_FMkq = P.bufs(3, "r_FM")
        b_FMr = P.bufs(2, "r_FMr")
        b_X = P.bufs(3, "r_X")
        b_nC = P.bufs(3, "r_nC")
        b_Bm = P.bufs(3, "r_Bm")
        b_Y = P.bufs(3, "r_Y")
        b_Z = P.bufs(3, "r_Z")
        b_Zf = P.bufs(3, "r_Zf")
        b_KqT, b_BU, b_UY, b_G, b_Yloc, b_tmpH = P.bufs(6, "r_m")
        b_pL = P.bufs(2, "r_pL")
        b_st = P.bufs(4, "r_st")
        b_yf = P.bufs(2, "r_yf")
        b_BON = P.bufs(2, "r_bon")
        b_Gt = P.bufs(2, "r_gt")
        b_H = P.bufs(2, "r_H")

        P.dma("sp", mu[:], C.e_mu[:, j * DA:(j + 1) * DA], w=[b_k])
        P.dma("sp", tab[:].rearrange("p a b -> p (a b)"), C.e_tab[:, j * TABW:(j + 1) * TABW], w=[b_k])
        for d in range(2):
            P.dma("sp", wa_up[0:64, d, :], C.e_w_up[j, d, :, :], w=[b_k])
            P.dma("sp", wa_up[64:128, d, :], C.e_a_up[j, d, :, :], w=[b_k])
        P.dma("sp", gup[:], C.e_g_up[j, :, :], w=[b_k])
        P.dma("sp", masks[:].rearrange("p a b -> p (a b)"), C.e_masks[:, :], w=[b_k])
        OP("dve", "memset", [], [b_k], ones[:], 1.0)
        OP("dve", "memset", [], [b_k], gneps[:], GN_EPS)

        psn = [0]

        def psget():
            b = psn[0] % 8
            psn[0] += 1
            return C.ps[b], C.bps[b]

        T_W0, T_A0, T_KK, T_KA, T_RK, T_GNW, T_GNB = 0, 2, 4, 5, 6, 7, 8
        def zviews(pb):
            zc = zcs[pb]
            return zc, b_zcs[pb], zc[:, 0:DR], zc[:, DR:2 * DR], zc[:, 2 * DR:3 * DR]
        v3 = lambda ap: ap.rearrange("p (h n) -> p h n", n=64)
        bc = lambda kk_: stt_[:, kk_, :].unsqueeze(2).to_broadcast([128, NH, 64])
        vh = lambda h: vb[:, h * 64:(h + 1) * 64]

        def gidx(seq, sc):
            return sc if seq == 0 else 2 + sc

        def P0c(d, seq, sc, pb):
            zc, b_zc, r_, k_, v_ = zviews(pb)
            for (tile_, bt, delta) in ((zc, b_zc, 0),):
                runs, clipped = row_runs(seq, sc, delta)
                if clipped:
                    OP("pool", "memset", [], [bt], tile_[:], 0.0)
                for (p0, p1, spec) in runs:
                    P.dma("sp", tile_[p0:p1, :], dram_rows(C.z_tok, spec, 0, DA), w=[bt])

        def P0y(d, seq, sc, pb):
            if d == 1:
                g = gidx(seq, sc)
                P.dma("sp", yft[pb][:], C.yf_tok[g * SC:(g + 1) * SC, :], r=[ybuf[g]], w=[b_yf[pb]])

        def P0pn(d, seq, sc, pb):
            for (tile_, bt, delta) in ((zp, b_zp, -1), (zn, b_zn, 1)):
                runs, clipped = row_runs(seq, sc, delta)
                if clipped:
                    OP("pool", "memset", [], [bt], tile_[:], 0.0)
                for (p0, p1, spec) in runs:
                    P.dma("sp", tile_[p0:p1, :], dram_rows(C.z_tok, spec, 0, DA), w=[bt])

        def P1(d, seq, sc, pb):
            zc, b_zc, r_, k_, v_ = zviews(pb)
            OP("dve", "tensor_tensor", [b_zp, b_zn], [b_zp], out=zp[:], in0=zp[:], in1=zn[:], op=ALU.add)
            OP("dve", "scalar_tensor_tensor", [b_zp, b_zc], [b_zp], out=zp[:], in0=zp[:], scalar=0.5, in1=zc[:], op0=ALU.mult, op1=ALU.subtract)
            OP("dve", "tensor_tensor", [b_zp, b_k], [b_zp], out=zp[:], in0=zp[:], in1=mu[:], op=ALU.mult)
            OP("dve", "tensor_tensor", [b_zc, b_zp], [b_zc], out=zc[:], in0=zc[:], in1=zp[:], op=ALU.add)
            pT, bT = psget()
            OP("pe", "transpose", [b_zc, C.b_const], [bT], pT[:, 0:128], zc[:, 2304:2432], C.ident_sb[:])
            OP("pe", "transpose", [b_zc, C.b_const], [bT], pT[:, 128:256], zc[:, 2432:2560], C.ident_sb[:])
            OP("act", "activation", [bT], [b_lr], out=lrT[0:64, :], in_=pT[0:64, 0:128], func=AF.Tanh)
            OP("act", "activation", [bT], [b_lr], out=lrT[64:128, :], in_=pT[64:128, 0:128], func=AF.Copy)
            if d == 1:
                OP("act", "activation", [bT], [b_sg], out=sgT[:], in_=pT[:, 128:256], func=AF.Sigmoid)

        def lowrank(dst, bdst, lhsT, rhs, rbufs, addtab):
            pa, ba = psget()
            pb_, bb = psget()
            OP("pe", "matmul", rbufs, [ba], pa[:, 0:512], lhsT=lhsT, rhs=rhs[:, 0:512], start=True, stop=True)
            OP("pe", "matmul", rbufs, [bb], pb_[:, 0:256], lhsT=lhsT, rhs=rhs[:, 512:768], start=True, stop=True)
            if addtab is None:
                OP("act", "activation", [ba], [bdst], out=dst[:, 0:512], in_=pa[:, 0:512], func=AF.Copy)
                OP("act", "activation", [bb], [bdst], out=dst[:, 512:768], in_=pb_[:, 0:256], func=AF.Copy)
            else:
                OP("dve", "tensor_tensor", [ba, b_k], [bdst], out=dst[:, 0:512], in0=pa[:, 0:512], in1=addtab[:, 0:512], op=ALU.add)
                OP("dve", "tensor_tensor", [bb, b_k], [bdst], out=dst[:, 512:768], in0=pb_[:, 0:256], in1=addtab[:, 512:768], op=ALU.add)

        def P2(d, seq, sc, pb):
            zc, b_zc, r_, k_, v_ = zviews(pb)
            lowrank(S[1], b_S[1], lrT[0:64, :], wa_up[0:64, d, :], [b_lr, b_k], tab[:, T_W0 + d, :])
            OP("act", "activation", [b_S[1]], [b_S[1]], out=S[1][:], in_=S[1][:], func=AF.Sigmoid)
            OP("dve", "tensor_scalar", [b_S[1]], [b_S[1]], out=S[1][:], in0=S[1][:], scalar1=-DECAY, scalar2=None, op0=ALU.mult)
            lowrank(S[2], b_S[2], lrT[64:128, :], wa_up[64:128, d, :], [b_lr, b_k], tab[:, T_A0 + d, :])
            OP("act", "activation", [b_S[2]], [b_S[2]], out=S[2][:], in_=S[2][:], func=AF.Sigmoid)
            if d == 1:
                lowrank(Gt[pb], b_Gt[pb], sgT[:], gup[:], [b_sg, b_k], None)
            OP("dve", "tensor_tensor", [b_zc, b_k], [b_S[3]], out=S[3][:], in0=k_, in1=tab[:, T_KK, :], op=ALU.mult)
            OP("dve", "tensor_tensor", [b_S[3]], [b_S[7]], out=S[7][:], in0=S[3][:], in1=S[3][:], op=ALU.mult)
            OP("dve", "tensor_reduce", [b_S[7]], [b_st[0]], out=stt_[:, 0, :], in_=v3(S[7][:]), axis=AX.X, op=ALU.add)
            OP("act", "activation", [b_st[0]], [b_st[0]], out=stt_[:, 0, :], in_=stt_[:, 0, :], func=AF.Sqrt)
            OP("dve", "tensor_scalar", [b_st[0]], [b_st[0]], out=stt_[:, 0, :], in0=stt_[:, 0, :], scalar1=1e-12, scalar2=None, op0=ALU.max)
            OP("dve", "reciprocal", [b_st[0]], [b_st[0]], out=stt_[:, 0, :], in_=stt_[:, 0, :])
            OP("dve", "tensor_tensor", [b_S[3], b_st[0]], [b_S[3]], out=v3(S[3][:]), in0=v3(S[3][:]), in1=bc(0), op=ALU.mult)
            OP("dve", "scalar_tensor_tensor", [b_S[2], b_k], [b_S[4]], out=S[4][:], in0=S[2][:], scalar=-1.0, in1=tab[:, T_KA, :], op0=ALU.add, op1=ALU.mult)
            OP("dve", "tensor_tensor", [b_S[4], b_zc], [b_S[4]], out=S[4][:], in0=S[4][:], in1=k_, op=ALU.mult)
            OP("dve", "tensor_tensor", [b_S[4], b_zc], [b_S[4]], out=S[4][:], in0=S[4][:], in1=k_, op=ALU.add)
            OP("dve", "tensor_tensor", [b_S[3], b_S[2]], [b_S5[pb]], out=S5[pb][:], in0=S[3][:], in1=S[2][:], op=ALU.mult)

        def P3(d, seq, sc, pb):
            zc, b_zc, r_, k_, v_ = zviews(pb)
            tri = masks[:, d, 640:768]
            pCa, bCa = psget()
            pCb, bCb = psget()
            OP("pe", "matmul", [b_k, b_S[1]], [bCa], pCa[:, 0:512], lhsT=tri, rhs=S[1][:, 0:512], start=True, stop=True)
            OP("pe", "matmul", [b_k, b_S[1]], [bCb], pCb[:, 0:256], lhsT=tri, rhs=S[1][:, 512:768], start=True, stop=True)
            pTa, bTa = psget()
            pTb, bTb = psget()
            OP("pe", "matmul", [b_k, b_S[1]], [bTa], pTa[:, 0:512], lhsT=ones[:], rhs=S[1][:, 0:512], start=True, stop=True)
            OP("pe", "matmul", [b_k, b_S[1]], [bTb], pTb[:, 0:256], lhsT=ones[:], rhs=S[1][:, 512:768], start=True, stop=True)
            pP, bP = psget()
            for pr in range(6):
                OP("pe", "matmul", [b_k, b_S[1]], [bP], pP[:, 2 * pr:2 * pr + 2], lhsT=S[1][:, pr * 128:(pr + 1) * 128], rhs=ones[:, 0:2], start=True, stop=True)
            OP("act", "activation", [bP], [b_pL[pb]], out=pL[pb][:, 0:6], in_=pP[:, 0:12].rearrange("p (a b) -> p a b", b=2)[:, :, 0], func=AF.Exp)
            OP("act", "activation", [bCa], [b_S[6]], out=S[6][:, 0:512], in_=pCa[:, 0:512], func=AF.Copy)
            OP("act", "activation", [bCb], [b_S[6]], out=S[6][:, 512:768], in_=pCb[:, 0:256], func=AF.Copy)
            OP("dve", "tensor_tensor", [b_S[6], b_S[1]], [b_S[7]], out=S[7][:], in0=S[6][:], in1=S[1][:], op=ALU.subtract)
            OP("act", "activation", [b_S[7]], [b_S[7]], out=S[7][:], in_=S[7][:], func=AF.Exp)
            OP("dve", "tensor_tensor", [b_S[3], b_S[7]], [b_S8[pb]], out=S8[pb][:], in0=S[3][:], in1=S[7][:], op=ALU.mult)
            OP("act", "activation", [b_S8[pb]], [b_S8b[pb]], out=S8b[pb][:], in_=S8[pb][:], func=AF.Copy)
            OP("act", "activation", [b_S[6]], [b_S[7]], out=S[7][:], in_=S[6][:], func=AF.Exp, scale=-1.0)
            OP("dve", "tensor_tensor", [b_S5[pb], b_S[7]], [b_S[2]], out=S[2][:], in0=S5[pb][:], in1=S[7][:], op=ALU.mult)
            OP("dve", "tensor_tensor", [b_S[4], b_S[7]], [b_S[3]], out=S[3][:], in0=S[4][:], in1=S[7][:], op=ALU.mult)
            OP("act", "activation", [b_S[6]], [b_S[7]], out=S[7][:], in_=S[6][:], func=AF.Exp)
            OP("dve", "tensor_tensor", [b_zc, b_S[7]], [b_S[1]], out=S[1][:], in0=r_, in1=S[7][:], op=ALU.mult)
            OP("dve", "tensor_tensor", [bTa, b_S[6]], [b_S[7]], out=S[7][:, 0:512], in0=pTa[:, 0:512], in1=S[6][:, 0:512], op=ALU.subtract)
            OP("dve", "tensor_tensor", [bTb, b_S[6]], [b_S[7]], out=S[7][:, 512:768], in0=pTb[:, 0:256], in1=S[6][:, 512:768], op=ALU.subtract)
            OP("act", "activation", [b_S[7]], [b_S[7]], out=S[7][:], in_=S[7][:], func=AF.Exp)
            OP("dve", "scalar_tensor_tensor", [b_S5[pb], b_S[7]], [b_NB[pb]], out=NB[pb][:], in0=S5[pb][:], scalar=-1.0, in1=S[7][:], op0=ALU.mult, op1=ALU.mult)
            OP("dve", "tensor_tensor", [b_S[4], b_S[7]], [b_KH], out=KH[:], in0=S[4][:], in1=S[7][:], op=ALU.mult)
            OP("act", "activation", [b_zc], [b_vb], out=vb[:], in_=v_, func=AF.Copy)

        def P4(d, seq, sc, pb):
            zc, b_zc, r_, k_, v_ = zviews(pb)
            for (src, bsrc, dst, bdst) in ((S[2], b_S[2], FMb, b_FMb), (S[3], b_S[3], FMk, b_FMk),
                                           (S8[pb], b_S8[pb], FMkq, b_FMkq), (S[1], b_S[1], FMr[pb], b_FMr[pb])):
                pa, ba = psget()
                pb_, bb = psget()
                for pr in range(4):
                    OP("pe", "transpose", [bsrc, C.b_const], [ba], pa[:, pr * 128:(pr + 1) * 128], src[:, pr * 128:(pr + 1) * 128], C.ident_sb[:])
                for pr in range(4, 6):
                    OP("pe", "transpose", [bsrc, C.b_const], [bb], pb_[:, (pr - 4) * 128:(pr - 3) * 128], src[:, pr * 128:(pr + 1) * 128], C.ident_sb[:])
                OP("act", "activation", [ba], [bdst], out=dst[:, 0:4, :], in_=pa[:, 0:512].rearrange("p (a b) -> p a b", b=128), func=AF.Copy)
                OP("act", "activation", [bb], [bdst], out=dst[:, 4:6, :], in_=pb_[:, 0:256].rearrange("p (a b) -> p a b", b=128), func=AF.Copy)
            if d == 1:
                lowrank(S[1], b_S[1], lrT[64:128, :], wa_up[64:128, 0, :], [b_lr, b_k], tab[:, T_A0 + 0, :])
                OP("act", "activation", [b_S[1]], [b_S[1]], out=S[1][:], in_=S[1][:], func=AF.Sigmoid)
                OP("dve", "scalar_tensor_tensor", [b_S[1], b_k], [b_S[1]], out=S[1][:], in0=S[1][:], scalar=-1.0, in1=tab[:, T_KA, :], op0=ALU.add, op1=ALU.mult)
                OP("dve", "tensor_tensor", [b_S[1], b_zc], [b_S[1]], out=S[1][:], in0=S[1][:], in1=k_, op=ALU.mult)
                OP("dve", "tensor_tensor", [b_S[1], b_zc], [b_S[1]], out=S[1][:], in0=S[1][:], in1=k_, op=ALU.add)
                OP("dve", "tensor_tensor", [b_S[1], b_S[4]], [b_S[1]], out=S[1][:], in0=S[1][:], in1=S[4][:], op=ALU.add)
                OP("dve", "tensor_tensor", [b_zc, b_k], [b_S[2]], out=S[2][:], in0=r_, in1=tab[:, T_RK, :], op=ALU.mult)
                OP("dve", "tensor_tensor", [b_S[2], b_S[1]], [b_S[2]], out=S[2][:], in0=S[2][:], in1=S[1][:], op=ALU.mult)
                OP("dve", "tensor_reduce", [b_S[2]], [b_st[3]], out=stt_[:, 3, :], in_=v3(S[2][:]), axis=AX.X, op=ALU.add)
                OP("dve", "tensor_tensor", [b_zc, b_st[3]], [b_BON[pb]], out=v3(BON[pb][:]), in0=v3(v_), in1=bc(3), op=ALU.mult)

        XCv = XC[:].rearrange("p (pr m) a b -> p pr m (a b)", m=2)
        BDv = BD[:].rearrange("p (pr m) a b -> p pr m (a b)", m=2)
        Ytv = Yt[:].rearrange("p (pr m) b -> p pr m b", m=2)
        X = XC[:, :, 0, :]

        def per_head_768(lhs_fn, rhs_fn, rbufs_fn, dst, bdst, addsrc=None, baddsrc=None):
            pa, ba = psget()
            pb_, bb = psget()
            for h in range(NH):
                (pp, bp, c0) = (pa, ba, h * 64) if h < 8 else (pb_, bb, (h - 8) * 64)
                OP("pe", "matmul", rbufs_fn(h), [bp], pp[:, c0:c0 + 64], lhsT=lhs_fn(h), rhs=rhs_fn(h), start=True, stop=True)
            if addsrc is None:
                OP("act", "activation", [ba], [bdst], out=dst[:, 0:512], in_=pa[:, 0:512], func=AF.Copy)
                OP("act", "activation", [bb], [bdst], out=dst[:, 512:768], in_=pb_[:, 0:256], func=AF.Copy)
            else:
                OP("dve", "tensor_tensor", [ba, baddsrc], [bdst], out=dst[:, 0:512], in0=pa[:, 0:512], in1=addsrc[:, 0:512], op=ALU.add)
                OP("dve", "tensor_tensor", [bb, baddsrc], [bdst], out=dst[:, 512:768], in0=pb_[:, 0:256], in1=addsrc[:, 512:768], op=ALU.add)

        def E(d, seq, sc, pb):
            mXC = masks[:, d, 0:256]
            mBD = masks[:, d, 256:512]
            mM2 = masks[:, d, 512:640]
            FMkr_b = [b_FMkq, b_FMr[pb]]
            for hq in range(3):
                for (FMx, bFM, msk, dstv, wb) in ((FMb, b_FMb, mXC, XCv, [b_X[hq], b_nC[hq]]), (FMk, b_FMk, mBD, BDv, [b_Bm[hq]])):
                    for m in range(2):
                        mb = slice(m * 64, (m + 1) * 64)
                        pa, ba = psget()
                        for q in range(2):
                            hp = 2 * hq + q
                            OP("pe", "matmul", [bFM] + FMkr_b, [ba], pa[:, q * 256:q * 256 + 128], lhsT=FMx[mb, hp, :], rhs=FMkq[mb, hp, :], start=True, stop=True)
                            OP("pe", "matmul", [bFM] + FMkr_b, [ba], pa[:, q * 256 + 128:(q + 1) * 256], lhsT=FMx[mb, hp, :], rhs=FMr[pb][mb, hp, :], start=True, stop=True)
                        OP("dve", "tensor_tensor", [ba, b_k], wb, out=dstv[:, 2 * hq:2 * hq + 2, m, :],
                           in0=pa[:, 0:512].rearrange("p (h x) -> p h x", x=256), in1=msk.unsqueeze(1).to_broadcast([128, 2, 256]), op=ALU.mult)
                for m in range(2):
                    mb = slice(m * 64, (m + 1) * 64)
                    pa, ba = psget()
                    for q in range(2):
                        pr = 2 * hq + q
                        OP("pe", "matmul", [b_FMkq, b_FMb], [ba], pa[:, q * 128:(q + 1) * 128], lhsT=FMkq[mb, pr, :], rhs=FMb[mb, pr, :], start=True, stop=True)
                    OP("dve", "tensor_tensor", [ba, b_k], [b_Y[hq]], out=Ytv[:, 2 * hq:2 * hq + 2, m, :], in0=pa[:, 0:256].rearrange("p (h x) -> p h x", x=128),
                       in1=mM2.unsqueeze(1).to_broadcast([128, 2, 128]), op=ALU.mult)
            for g in range(3):
                g4 = slice(4 * g, 4 * g + 4)
                OP("act", "activation", [b_X[g]], [b_Zf[g]], out=Zf[:, g4, :], in_=X[:, g4, :], func=AF.Copy)
                OP("dve", "tensor_tensor", [b_Zf[g], C.b_const], [b_Zf[g]], out=Zf[:, g4, :], in0=Zf[:, g4, :],
                   in1=C.ident_sb[:].unsqueeze(1).to_broadcast([128, 4, 128]), op=ALU.add)
                OP("act", "activation", [b_Zf[g]], [b_Z[g]], out=Zt[:, g4, :], in_=Zf[:, g4, :], func=AF.Copy)
            pG, bG = psget()
            for h in range(NH):
                pr, m = h // 2, h % 2
                mb = slice(m * 64, (m + 1) * 64)
                OP("pe", "matmul", [b_KH, b_vb], [bG], pG[mb, pr * 64:(pr + 1) * 64], lhsT=KH[:, h * 64:(h + 1) * 64], rhs=vh(h), start=True, stop=True)
            OP("act", "activation", [bG], [b_G], out=Gs[:], in_=pG[:, 0:384].rearrange("p (a b) -> p a b", b=64), func=AF.Copy)
            per_head_768(lambda h: BD[:, h, 0, :], vh, lambda h: [b_Bm[h // 4], b_vb], BU, b_BU)
            per_head_768(lambda h: BD[:, h, 1, :], vh, lambda h: [b_Bm[h // 4], b_vb], Ylocs, b_Yloc)

        def I_level(lv):
            last = (lv == SC // 2)
            for g in range(3):
                g4 = slice(4 * g, 4 * g + 4)
                if not last:
                    pX, bX = psget()
                    for hh in range(4):
                        h = 4 * g + hh
                        OP("pe", "matmul", [b_Y[g], b_X[g]], [bX], pX[:, hh * 128:(hh + 1) * 128], lhsT=Yt[:, h, :], rhs=X[:, h, :], start=True, stop=True)
                pY, bY = psget()
                for hh in range(4):
                    h = 4 * g + hh
                    OP("pe", "matmul", [b_Y[g], b_X[g]], [bY], pY[:, hh * 128:(hh + 1) * 128], lhsT=X[:, h, :], rhs=Yt[:, h, :], start=True, stop=True)
                if not last:
                    OP("act", "activation", [bX], [b_X[g]], out=X[:, g4, :], in_=pX[:, 0:512].rearrange("p (h x) -> p h x", x=128), func=AF.Copy)
                OP("act", "activation", [bY], [b_Y[g]], out=Yt[:, g4, :], in_=pY[:, 0:512].rearrange("p (h x) -> p h x", x=128), func=AF.Copy)
                pZ, bZ = psget()
                for hh in range(4):
                    h = 4 * g + hh
                    OP("pe", "matmul", [b_Y[g], b_Z[g]], [bZ], pZ[:, hh * 128:(hh + 1) * 128], lhsT=Yt[:, h, :], rhs=Zt[:, h, :], start=True, stop=True)
                OP("dve", "tensor_tensor", [bZ, b_Zf[g]], [b_Zf[g]], out=Zf[:, g4, :], in0=pZ[:, 0:512].rearrange("p (h x) -> p h x", x=128),
                   in1=Zf[:, g4, :], op=ALU.add)
                OP("act", "activation", [b_Zf[g]], [b_Z[g]], out=Zt[:, g4, :], in_=Zf[:, g4, :], func=AF.Copy)

        def Lphase(d, seq, sc, pb, cur):
            pa, ba = psget()
            pb_, bb = psget()
            for h in range(NH):
                pr, m = h // 2, h % 2
                mb = slice(m * 64, (m + 1) * 64)
                (pp, bp, c0) = (pa, ba, pr * 128) if pr < 4 else (pb_, bb, (pr - 4) * 128)
                OP("pe", "matmul", [b_S8b[pb], b_Z[h // 4]], [bp], pp[mb, c0:c0 + 128], lhsT=S8b[pb][:, h * 64:(h + 1) * 64], rhs=Zt[:, h, :], start=True, stop=True)
            OP("act", "activation", [ba], [b_KqT], out=KqT[:, 0:4, :], in_=pa[:, 0:512].rearrange("p (a b) -> p a b", b=128), func=AF.Copy)
            OP("act", "activation", [bb], [b_KqT], out=KqT[:, 4:6, :], in_=pb_[:, 0:256].rearrange("p (a b) -> p a b", b=128), func=AF.Copy)
            per_head_768(lambda h: Zt[:, h, :], lambda h: BU[:, h * 64:(h + 1) * 64], lambda h: [b_Z[h // 4], b_BU], UY, b_UY)
            Hc, bHc, Hn, bHn = Hs[cur], b_H[cur], Hs[1 - cur], b_H[1 - cur]
            Hbc, bHbc, Hbn, bHbn = Hb[cur], b_Hb[cur], Hb[1 - cur], b_Hb[1 - cur]

            def ph_rowtiled(lhs_fn, rbufs, dst, bdst, addsrc, baddsrc):
                dv = dst[:].rearrange("p (pr m n) -> p pr m n", m=2, n=64)
                av = addsrc[:].rearrange("p (pr m n) -> p pr m n", m=2, n=64)
                for m in range(2):
                    mb = slice(m * 64, (m + 1) * 64)
                    pa2, ba2 = psget()
                    for pr in range(6):
                        OP("pe", "matmul", rbufs + [bHbc], [ba2], pa2[:, pr * 64:(pr + 1) * 64], lhsT=lhs_fn(pr, mb), rhs=Hbc[mb, pr, :], start=True, stop=True)
                    OP("dve", "tensor_tensor", [ba2, baddsrc], [bdst], out=dv[:, :, m, :], in0=pa2[:, 0:384].rearrange("p (a b) -> p a b", b=64),
                       in1=av[:, :, m, :], op=ALU.add)

            OP("dve", "tensor_tensor", [bHc, b_pL[pb]], [b_tmpH], out=tmpH[:], in0=Hc[:], in1=pL[pb][:, 0:6].unsqueeze(2).to_broadcast([128, 6, 64]), op=ALU.mult)
            OP("dve", "tensor_tensor", [b_tmpH, b_G], [b_tmpH], out=tmpH[:], in0=tmpH[:], in1=Gs[:], op=ALU.add)
            ph_rowtiled(lambda pr, mb: KqT[mb, pr, :], [b_KqT], BU, b_BU, UY, b_UY)
            pH, bH = psget()
            for h in range(NH):
                pr, m = h // 2, h % 2
                mb = slice(m * 64, (m + 1) * 64)
                OP("pe", "matmul", [b_NB[pb], b_BU], [bH], pH[mb, pr * 64:(pr + 1) * 64], lhsT=NB[pb][:, h * 64:(h + 1) * 64], rhs=BU[:, h * 64:(h + 1) * 64], start=True, stop=True)
            OP("dve", "tensor_tensor", [bH, b_tmpH], [bHbn], out=Hbn[:], in0=pH[:, 0:384].rearrange("p (a b) -> p a b", b=64), in1=tmpH[:], op=ALU.add)
            OP("dve", "tensor_tensor", [bH, b_tmpH], [bHn], out=Hn[:], in0=pH[:, 0:384].rearrange("p (a b) -> p a b", b=64), in1=tmpH[:], op=ALU.add)
            ph_rowtiled(lambda pr, mb: FMr[pb][mb, pr, :], [b_FMr[pb]], UY, b_UY, Ylocs, b_Yloc)
            per_head_768(lambda h: XC[:, h, 1, :], lambda h: BU[:, h * 64:(h + 1) * 64], lambda h: [b_nC[h // 4], b_BU], UY, b_UY, UY, b_UY)
            ysb, b_y = UY, b_UY
            g = gidx(seq, sc)
            if d == 0:
                P.dma("sp", C.yf_tok[g * SC:(g + 1) * SC, :], ysb[:], r=[b_y], w=[ybuf[g]])
                return
            OP("dve", "tensor_tensor", [b_y, b_yf[pb]], [b_y], out=ysb[:], in0=ysb[:], in1=yft[pb][:], op=ALU.add)
            OP("dve", "tensor_reduce", [b_y], [b_st[1]], out=stt_[:, 1, :], in_=v3(ysb[:]), axis=AX.X, op=ALU.add)
            OP("dve", "tensor_scalar", [b_st[1]], [b_st[1]], out=stt_[:, 1, :], in0=stt_[:, 1, :], scalar1=-1.0 / 64, scalar2=None, op0=ALU.mult)
            OP("dve", "tensor_tensor", [b_y, b_st[1]], [b_y], out=v3(ysb[:]), in0=v3(ysb[:]), in1=bc(1), op=ALU.add)
            OP("dve", "tensor_tensor", [b_y], [b_Yloc], out=Ylocs[:], in0=ysb[:], in1=ysb[:], op=ALU.mult)
            OP("dve", "tensor_reduce", [b_Yloc], [b_st[2]], out=stt_[:, 2, :], in_=v3(Ylocs[:]), axis=AX.X, op=ALU.add)
            OP("act", "activation", [b_st[2], b_k], [b_st[2]], out=stt_[:, 2, :], in_=stt_[:, 2, :], func=AF.Sqrt, bias=gneps[:, 0:1], scale=1.0 / 64)
            OP("dve", "reciprocal", [b_st[2]], [b_st[2]], out=stt_[:, 2, :], in_=stt_[:, 2, :])
            OP("dve", "tensor_tensor", [b_y, b_st[2]], [b_y], out=v3(ysb[:]), in0=v3(ysb[:]), in1=bc(2), op=ALU.mult)
            OP("dve", "tensor_tensor", [b_y, b_k], [b_y], out=ysb[:], in0=ysb[:], in1=tab[:, T_GNW, :], op=ALU.mult)
            OP("dve", "tensor_tensor", [b_y, b_k], [b_y], out=ysb[:], in0=ysb[:], in1=tab[:, T_GNB, :], op=ALU.add)
            OP("dve", "tensor_tensor", [b_y, b_BON[pb]], [b_y], out=ysb[:], in0=ysb[:], in1=BON[pb][:], op=ALU.add)
            OP("dve", "tensor_tensor", [b_y, b_Gt[pb]], [b_y], out=ysb[:], in0=ysb[:], in1=Gt[pb][:], op=ALU.mult)
            runs, _ = row_runs(seq, sc, 0)
            for (p0, p1, spec) in runs:
                P.dma("sp", dram_rows(C.o_tok, spec, 0, DR), ysb[p0:p1, :], r=[b_y], w=[P.buf()])

        for d in range(2):
            OP("dve", "memset", [], [b_H[0]], Hs[0][:], 0.0)
            OP("dve", "memset", [], [b_Hb[0]], Hb[0][:], 0.0)
            order = [(0, 0), (0, 1)] + [(1, s) for s in range(SEQ // SC)]
            if d == 1:
                order = [(0, 1), (0, 0)] + [(1, s) for s in range(SEQ // SC - 1, -1, -1)]
            if RWKV_LIMIT:
                order = [(0, 0), (0, 1), (1, 0)] if d == 0 else [(0, 1), (0, 0), (1, 0)]
            seq0, sc0 = order[0]
            P0c(d, seq0, sc0, 0)
            P0y(d, seq0, sc0, 0)
            P0pn(d, seq0, sc0, 0)
            if len(order) > 1:
                P0c(d, order[1][0], order[1][1], 1)
                P0y(d, order[1][0], order[1][1], 1)
            P1(d, seq0, sc0, 0)
            if len(order) > 1:
                P0pn(d, order[1][0], order[1][1], 1)
            for piece in (P2, P3, P4):
                piece(d, seq0, sc0, 0)
            cur = 0
            for n, (seq, sc) in enumerate(order):
                pb = n % 2
                nxt = order[n + 1] if n + 1 < len(order) else None
                nx2 = order[n + 2] if n + 2 < len(order) else None
                nb = (n + 1) % 2
                if nx2:
                    P0c(d, nx2[0], nx2[1], pb)
                E(d, seq, sc, pb)
                I_level(2)
                if nxt:
                    P1(d, nxt[0], nxt[1], nb)
                if nx2:
                    P0pn(d, nx2[0], nx2[1], pb)
                I_level(4)
                if nxt:
                    P2(d, nxt[0], nxt[1], nb)
                I_level(8)
                I_level(16)
                if nxt:
                    P3(d, nxt[0], nxt[1], nb)
                I_level(32)
                I_level(64)
                Lphase(d, seq, sc, pb, cur)
                if nx2:
                    P0y(d, nx2[0], nx2[1], pb)
                if nxt:
                    P4(d, nxt[0], nxt[1], nb)
                cur = 1 - cur
        P.barrier()


def stage_even_C(C, P, L, tiles):
    nc = C.nc
    j = L // 2
    T = TT
    with ExitStack() as st:
        sb = mk_sb(nc, st)
        wout = sb("c_wout", [128, 8, D], BF16)
        b_wout = P.bufs(8, "cwout")
        for kc in range(8):
            P.dma("pool", wout[:, kc, :], C.e_w_out[j, kc * 128:(kc + 1) * 128, :], w=[b_wout[kc]])
        hs = [sb(f"ch{i}", [128, 8, T]) for i in range(2)]
        ot = [sb(f"cot{i}", [128, DR]) for i in range(2)]
        cat = [sb(f"ccat{i}", [128, 8, T], BF16) for i in range(2)]
        b_hs = [P.bufs(8, f"ch{i}_") for i in range(2)]
        b_ot = P.bufs(2, "cot")
        b_cat = [P.bufs(3, f"ccat{i}_") for i in range(2)]
        no = 0
        npb = 0
        for i, ti in enumerate(tiles):
            k = i % 2
            v = 1 if ti == 0 else 0
            h, bh, ct, bct = hs[k], b_hs[k], cat[k], b_cat[k]
            P.dma("sp", h[:], fm(C.hT, ti * T, T), r=[C.hbuf[ti]], w=bh)
            P.dma("pool", ct[:, 6:8, :], C.fT.rearrange("(c p) t -> p c t", p=128)[:, :, ti * T:(ti + 1) * T], w=[bct[2]])
            for tb in range(T // 128):
                o, bo = ot[no % 2], b_ot[no % 2]
                no += 1
                r0 = ti * T + tb * 128
                P.dma("sp", o[:], C.o_tok[r0:r0 + 128, :], w=[bo])
                pab, pbb = 1 + (npb % 3) * 2, 2 + (npb % 3) * 2
                npb += 1
                pa, pb = C.ps[pab], C.ps[pbb]
                for c in range(4):
                    P.op("pe", lambda e, c=c, pa=pa, o=o: e.transpose(pa[:, c * 128:(c + 1) * 128], o[:, c * 128:(c + 1) * 128], C.ident_sb[:]),
                         r=[bo, C.b_const], w=[C.bps[pab]])
                for c in range(4, 6):
                    P.op("pe", lambda e, c=c, pb=pb, o=o: e.transpose(pb[:, (c - 4) * 128:(c - 3) * 128], o[:, c * 128:(c + 1) * 128], C.ident_sb[:]),
                         r=[bo, C.b_const], w=[C.bps[pbb]])
                P.op("act", lambda e, pa=pa, ct=ct, tb=tb: e.activation(
                    out=ct[:, 0:4, tb * 128:(tb + 1) * 128], in_=pa[:, 0:512].rearrange("p (a b) -> p a b", b=128), func=AF.Copy),
                    r=[C.bps[pab]], w=[bct[0]])
                P.op("dve", lambda e, pb=pb, ct=ct, tb=tb: e.tensor_copy(
                    out=ct[:, 4:6, tb * 128:(tb + 1) * 128], in_=pb[:, 0:256].rearrange("p (a b) -> p a b", b=128)),
                    r=[C.bps[pbb]], w=[bct[1]])
            for dc in range(8):
                pob = 7 if dc % 2 == 0 else 0
                po = C.ps[pob]
                for kc in range(8):
                    P.op("pe", lambda e, dc=dc, kc=kc, po=po, ct=ct: e.matmul(
                        po[:, :T], lhsT=wout[:, kc, dc * 128:(dc + 1) * 128], rhs=ct[:, kc, :],
                        start=(kc == 0), stop=(kc == 7)), r=[b_wout[kc]] + bct, w=[C.bps[pob]])
                P.op("dve", lambda e, dc=dc, po=po, h=h, v=v: e.scalar_tensor_tensor(
                    out=h[:, dc, :], in0=po[:, :T], scalar=C.mG[:, L, 1, v, dc:dc + 1], in1=h[:, dc, :],
                    op0=ALU.mult, op1=ALU.add), r=[C.bps[pob], bh[dc], C.b_mod], w=[bh[dc]])
            P.dma("pool", fm(C.hT, ti * T, T), h[:], r=bh, w=[C.hbuf[ti]])
        P.barrier()


def prep_even(inp, shared):
    f = lambda a: np.ascontiguousarray(np.asarray(a, np.float32))
    shared["e_w_in"] = f(inp["e_w_in"])
    shared["e_w_out"] = f(inp["e_w_out"])
    rep = lambda v: np.broadcast_to(np.asarray(v, np.float32)[None, :], (128, np.asarray(v).shape[-1]))
    shared["e_mu"] = np.ascontiguousarray(np.concatenate([rep(inp["e_mu"][j]) for j in range(2)], axis=1))
    tabs = []
    for j in range(2):
        for v in (inp["e_w0"][j, 0], inp["e_w0"][j, 1], inp["e_a0"][j, 0], inp["e_a0"][j, 1], inp["e_k_k"][j], inp["e_k_a"][j],
                  inp["e_r_k"][j], inp["e_gn_w"][j], inp["e_gn_b"][j]):
            tabs.append(rep(v))
    shared["e_tab"] = np.ascontiguousarray(np.concatenate(tabs, axis=1))
    shared["e_w_up"] = f(inp["e_w_up"])
    shared["e_a_up"] = f(inp["e_a_up"])
    shared["e_g_up"] = f(inp["e_g_up"])
    idx = np.arange(SC)
    mk = []
    for d in range(2):
        be = (idx[:, None] <= idx[None, :]) if d == 0 else (idx[:, None] >= idx[None, :])
        strict = be & (idx[:, None] != idx[None, :])
        be = be.astype(np.float32)
        strict = strict.astype(np.float32)
        mk += [-strict, -be, strict, be, -strict.T, be]
    shared["e_masks"] = np.ascontiguousarray(np.concatenate(mk, axis=1))
    def dft(N):
        n = np.arange(N, dtype=np.int64)
        m = (n[:, None] * n[None, :]) % N
        ang = (2.0 * np.pi / N) * m.astype(np.float64)
        return np.cos(ang).astype(np.float32), np.sin(ang).astype(np.float32)
    cN, sN = dft(SEQ)
    shared["cosT"] = cN.astype(ml_dtypes.bfloat16)
    shared["sinT"] = sN.astype(ml_dtypes.bfloat16)
    cc, sc_ = dft(CTX)
    shared["cosc"] = cc.astype(ml_dtypes.bfloat16)
    shared["sinc"] = sc_.astype(ml_dtypes.bfloat16)
    c64, s64 = dft(64)
    cd = np.zeros((128, 4, 128), np.float32)
    for qi, (N, ) in enumerate(((SEQ,), (CTX,))):
        scale = 1.0 / np.sqrt(N * 64.0)
        for g in range(2):
            cd[g * 64:(g + 1) * 64, 2 * qi, g * 64:(g + 1) * 64] = c64 * scale
            cd[g * 64:(g + 1) * 64, 2 * qi + 1, g * 64:(g + 1) * 64] = -s64 * scale
    shared["cdft"] = np.ascontiguousarray(cd.reshape(128, -1))


def full_stages():
    stages = [("ada",)]
    for i in range(DEPTH):
        with_ctx = not (i == DEPTH - 1 and i % 2 == 1)
        stages.append(("ffn", i, 0, with_ctx))
        if i % 2 == 0:
            stages.append(("even", i))
        else:
            stages.append(("odd", i, with_ctx))
        stages.append(("ffn", i, 2, with_ctx))
    stages.append(("final",))
    return stages


def kernel(**inputs):
    ncores = 8
    nc = build_program(full_stages(), debug=False)
    maps = prep_inputs(inputs, ncores=ncores)
    res = run_bass_kernel_spmd(nc, maps, core_ids=list(range(ncores)))
    out = np.stack([np.ascontiguousarray(res.results[b]["outT"].T) for b in range(ncores)], axis=0)
    return out.astype(np.float32)
```

```python
import numpy as np
from contextlib import ExitStack
import concourse.bass as bass
import concourse.mybir as mybir
from concourse.bass_utils import run_bass_kernel_spmd

F32 = mybir.dt.float32
BF16 = mybir.dt.bfloat16
ALU = mybir.AluOpType
AF = mybir.ActivationFunctionType
AX = mybir.AxisListType

D = 1024
SEQ = 4096
CTX = 256
NT = SEQ + CTX
DEPTH = 4
DFF = 2816
NMOD = 9
EPS = 1e-6

EPOCH = 30000
NDMA = 8
DMA_EPOCH = 1800


class Buf:
    __slots__ = ("name", "w", "r")

    def __init__(self, name):
        self.name = name
        self.w = None
        self.r = {}


class Prog:
    ENG = ("pe", "act", "dve", "pool", "sp")

    def __init__(self, nc):
        self.nc = nc
        self.q = {e: [] for e in self.ENG}
        self.cnt = {e: 0 for e in self.ENG}
        self.dcnt = {e: 0 for e in self.ENG}
        self.waited = {e: {} for e in self.ENG}
        self.semkeys = []
        self.semset = set()
        self.sems = {}
        self.nbuf = 0

    def buf(self, name=None):
        self.nbuf += 1
        return Buf(name or f"b{self.nbuf}")

    def bufs(self, n, name="b"):
        return [self.buf(f"{name}{i}") for i in range(n)]

    def _key(self, k):
        if k not in self.semset:
            self.semset.add(k)
            self.semkeys.append(k)
        return k

    def _collect(self, eng, reads, writes, is_dma):
        need = {}

        def add(tok, raw):
            teng, tdma, key, val = tok
            if (not is_dma) and (not tdma) and teng == eng and not raw and eng == "pe":
                return
            if self.waited[eng].get(key, 0) >= val:
                return
            if need.get(key, 0) < val:
                need[key] = val

        for b in reads:
            if b.w is not None:
                add(b.w, True)
        for b in writes:
            if b.w is not None:
                add(b.w, False)
            for t in b.r.values():
                add(t, False)
        return need

    def _register(self, tok, reads, writes):
        teng, tdma, key, val = tok
        rk = key if tdma else teng
        for b in reads:
            b.r[rk] = tok
        for b in writes:
            b.w = tok
            b.r = {}

    def op(self, eng, fn, r=(), w=()):
        need = self._collect(eng, r, w, False)
        for k, v in need.items():
            self.waited[eng][k] = v
        self.cnt[eng] += 1
        k = self.cnt[eng] - 1
        key = self._key((eng, k // EPOCH))
        val = k % EPOCH + 1
        tok = (eng, False, key, val)
        self.waited[eng][key] = max(self.waited[eng].get(key, 0), 0)
        self._register(tok, r, w)
        waits = list(need.items())
        sems = self.sems

        def run(e):
            for wk, wv in waits:
                e.wait_ge(sems[wk], wv)
            fn(e).then_inc(sems[key], 1)

        self.q[eng].append(run)
        return tok

    def dma(self, eng, out, in_, r=(), w=(), **kw):
        need = self._collect(eng, r, w, True)
        i = self.dcnt[eng]
        self.dcnt[eng] += 1
        st = i // (NDMA * DMA_EPOCH)
        j = i % (NDMA * DMA_EPOCH)
        slot = j % NDMA
        rnd = j // NDMA
        key = self._key(("dma", eng, st, slot))
        if rnd > 0 and self.waited[eng].get(key, 0) < 16 * rnd:
            need[key] = max(need.get(key, 0), 16 * rnd)
        for k, v in need.items():
            self.waited[eng][k] = v
        val = 16 * (rnd + 1)
        tok = (eng, True, key, val)
        self._register(tok, r, w)
        waits = list(need.items())
        sems = self.sems

        def run(e):
            for wk, wv in waits:
                e.wait_ge(sems[wk], wv)
            e.dma_start(out=out, in_=in_, **kw).then_inc(sems[key], 16)

        self.q[eng].append(run)
        return tok

    def finish(self, eng, toks):
        waits = []
        for t in toks:
            waits.append((t[2], t[3]))
        sems = self.sems

        def run(e):
            for wk, wv in waits:
                e.wait_ge(sems[wk], wv)

        self.q[eng].append(run)

    def emit(self, stack):
        nc = self.nc
        for i, k in enumerate(self.semkeys):
            self.sems[k] = stack.enter_context(nc.semaphore(f"s{i}"))
        block = stack.enter_context(nc.Block())
        q = self.q

        @block.tensor
        def _(e):
            for f in q["pe"]:
                f(e)

        @block.scalar
        def _(e):
            for f in q["act"]:
                f(e)

        @block.vector
        def _(e):
            for f in q["dve"]:
                f(e)

        @block.gpsimd
        def _(e):
            for f in q["pool"]:
                f(e)

        @block.sync
        def _(e):
            for f in q["sp"]:
                f(e)


def _prog_barrier(self):
    toks = []
    for e in self.ENG:
        if self.cnt[e] > 0:
            k = self.cnt[e] - 1
            toks.append(((e, k // EPOCH), k % EPOCH + 1))
        n = self.dcnt[e]
        if n > 0:
            for back in range(min(n, NDMA)):
                i = n - 1 - back
                st = i // (NDMA * DMA_EPOCH)
                j = i % (NDMA * DMA_EPOCH)
                toks.append((("dma", e, st, j % NDMA), 16 * (j // NDMA + 1)))
    sems = self.sems
    for e in self.ENG:
        waits = []
        for key, val in toks:
            if self.waited[e].get(key, 0) >= val:
                continue
            self.waited[e][key] = val
            waits.append((key, val))

        def run(eng, waits=waits):
            for wk, wv in waits:
                eng.wait_ge(sems[wk], wv)

        self.q[e].append(run)


Prog.barrier = _prog_barrier


TT = 256
NTILES = NT // TT
DIN_E = 2816
DIN_O = 1792


_UNIQ = [0]


def mk_sb(nc, st):
    def sb(name, shape, dt=F32):
        _UNIQ[0] += 1
        return st.enter_context(nc.sbuf_tensor(f"{name}_{_UNIQ[0]}", list(shape), dt))
    return sb


class KC:
    pass


def declare_io(nc, debug):
    C = KC()
    C.nc = nc
    di = lambda name, shape: nc.dram_tensor(name, list(shape), F32, kind="ExternalInput").ap()
    C.xT = di("xT", [D, NT])
    C.cv = di("cv", [128, 16])
    C.ada_w = di("ada_w", [DEPTH, D, NMOD * D])
    C.ada_b = di("ada_b", [128, DEPTH * 72])
    C.norm_g = di("norm_g", [128, DEPTH * 3 * 8])
    C.final_g = di("final_g", [128, 8])
    C.ident = di("ident", [128, 128])
    C.ffn_gu = [di("ffn1_w_gu", [DEPTH, D, 2 * DFF]), di("ffn2_w_gu", [DEPTH, D, 2 * DFF])]
    C.ffn_dn = [di("ffn1_w_down", [DEPTH, DFF, D]), di("ffn2_w_down", [DEPTH, DFF, D])]
    C.outT = nc.dram_tensor("outT", [D, SEQ], F32, kind="ExternalOutput").ap()
    C.hT = nc.dram_tensor("hT", [D, NT], F32).ap()
    if debug:
        C.dbg = nc.dram_tensor("dbg", [D, NT], F32, kind="ExternalOutput").ap()
    return C


def fm(ap2d, t0, n):
    return ap2d.rearrange("(c p) t -> p c t", p=128)[:, :, t0:t0 + n]


def setup_persistent(C, P, st):
    nc = C.nc
    sb = mk_sb(nc, st)
    C.ps = [st.enter_context(nc.psum_tensor(f"ps{i}", [128, 512], F32)) for i in range(8)]
    C.bps = P.bufs(8, "ps")
    C.ident_sb = sb("ident_sb", [128, 128])
    C.ones_bf = sb("ones_bf", [128, 128], BF16)
    C.mod = sb("mod", [128, DEPTH, 72, 2])
    C.mA = sb("mA", [128, DEPTH, 3, 2, 8])
    C.mB = sb("mB", [128, DEPTH, 3, 2, 8])
    C.mG = sb("mG", [128, DEPTH, 3, 2, 8])
    C.g_sb = sb("g_sb", [128, DEPTH, 3, 8])
    C.fg_sb = sb("fg_sb", [128, 8])
    C.eps_sb = sb("eps_sb", [128, 1])
    C.b_const = P.buf("const")
    C.b_mod = P.buf("mod")
    C.hbuf = P.bufs(NTILES, "hT")
    C.xbuf = P.bufs(NTILES, "xT")
    C.obuf = P.bufs(NTILES, "outT")
    P.dma("sp", C.ident_sb[:], C.ident[:, :], w=[C.b_const])
    P.dma("sp", C.g_sb[:].rearrange("p a b c -> p (a b c)"), C.norm_g[:, :], w=[C.b_const])
    P.dma("sp", C.fg_sb[:], C.final_g[:, :], w=[C.b_const])
    P.op("dve", lambda e: e.memset(C.ones_bf[:], 1.0), w=[C.b_const])
    P.op("dve", lambda e: e.memset(C.eps_sb[:], EPS), w=[C.b_const])


def stage_ada(C, P):
    nc = C.nc
    with ExitStack() as st:
        sb = mk_sb(nc, st)
        cvs = sb("cvs", [128, 8, 2])
        cs = sb("cs", [128, 8, 2])
        adab = sb("adab", [128, DEPTH, 72])
        slabs = [sb(f"slab{i}", [128, 8, 1152]) for i in range(2)]
        b_cv, b_cs, b_ab = P.bufs(3, "ada")
        b_slab = P.bufs(2, "slab")
        P.dma("sp", cvs[:].rearrange("p a b -> p (a b)"), C.cv[:, :], w=[b_cv])
        P.dma("sp", adab[:].rearrange("p a b -> p (a b)"), C.ada_b[:, :], w=[b_ab])
        P.op("act", lambda e: e.activation(out=cs[:], in_=cvs[:], func=AF.Silu), r=[b_cv], w=[b_cs])
        ps0 = C.ps[0]
        n = 0
        for L in range(DEPTH):
            wv = C.ada_w[L].rearrange("(kc p) n -> p kc n", p=128)
            for s in range(8):
                slab = slabs[n % 2]
                bs = b_slab[n % 2]
                n += 1
                P.dma("sp", slab[:], wv[:, :, s * 1152:(s + 1) * 1152], w=[bs])
                for qq in range(9):
                    q = s * 9 + qq
                    for kc in range(8):
                        P.op("pe", lambda e, slab=slab, qq=qq, q=q, kc=kc: e.matmul(
                            ps0[:, 2 * q:2 * q + 2], lhsT=slab[:, kc, qq * 128:(qq + 1) * 128], rhs=cs[:, kc, :],
                            start=(kc == 0), stop=(kc == 7)), r=[bs, b_cs], w=[C.bps[0]])
            P.op("dve", lambda e, L=L: e.tensor_tensor(
                out=C.mod[:, L, :, :], in0=ps0[:, 0:144].rearrange("p (q v) -> p q v", v=2),
                in1=adab[:, L, :].unsqueeze(2).to_broadcast([128, 72, 2]), op=ALU.add),
                r=[C.bps[0], b_ab], w=[C.b_mod])
        for L in range(DEPTH):
            for s in range(3):
                for v in range(2):
                    P.op("dve", lambda e, L=L, s=s, v=v: e.scalar_tensor_tensor(
                        out=C.mA[:, L, s, v, :], in0=C.mod[:, L, (3 * s + 1) * 8:(3 * s + 1) * 8 + 8, v], scalar=1.0,
                        in1=C.g_sb[:, L, s, :], op0=ALU.add, op1=ALU.mult), r=[C.b_mod, C.b_const], w=[C.b_mod])
                    P.op("dve", lambda e, L=L, s=s, v=v: e.tensor_copy(
                        out=C.mB[:, L, s, v, :], in_=C.mod[:, L, (3 * s) * 8:(3 * s) * 8 + 8, v]),
                        r=[C.b_mod], w=[C.b_mod])
                    P.op("dve", lambda e, L=L, s=s, v=v: e.tensor_scalar(
                        out=C.mG[:, L, s, v, :], in0=C.mod[:, L, (3 * s + 2) * 8:(3 * s + 2) * 8 + 8, v],
                        scalar1=(1.0 if s == 1 else 0.5), scalar2=None, op0=ALU.mult), r=[C.b_mod], w=[C.b_mod])
        P.barrier()


def emit_norm(C, P, h, b_h, T, sqbuf, b_sq, rstd, b_rstd, psb):
    ps = C.ps[psb]
    P.op("act", lambda e: e.activation(out=sqbuf[:, :, :T], in_=h[:, :, :T], func=AF.Square), r=b_h, w=[b_sq])
    for c in range(8):
        P.op("pe", lambda e, c=c: e.matmul(ps[:, :T], lhsT=C.ones_bf[:], rhs=sqbuf[:, c, :T], start=(c == 0), stop=(c == 7)),
             r=[b_sq, C.b_const], w=[C.bps[psb]])
    P.op("act", lambda e: e.activation(out=rstd[:, :T], in_=ps[:, :T], func=AF.Sqrt, bias=C.eps_sb[:, 0:1], scale=1.0 / D),
         r=[C.bps[psb], C.b_const], w=[b_rstd])
    P.op("dve", lambda e: e.reciprocal(out=rstd[:, :T], in_=rstd[:, :T]), r=[b_rstd], w=[b_rstd])


def emit_modulate(C, P, h, b_h, y, b_y, rstd, b_rstd, tmps, b_tmps, L, s, v, T):
    for c in range(8):
        tmp = tmps[c % 2]
        bt = b_tmps[c % 2]
        P.op("dve", lambda e, c=c, tmp=tmp: e.scalar_tensor_tensor(
            out=tmp[:, :T], in0=h[:, c, :T], scalar=C.mA[:, L, s, v, c:c + 1], in1=rstd[:, :T],
            op0=ALU.mult, op1=ALU.mult), r=[b_h[c], b_rstd, C.b_mod], w=[bt])
        P.op("act", lambda e, c=c, tmp=tmp: e.activation(
            out=y[:, c, :T], in_=tmp[:, :T], func=AF.Identity, bias=C.mB[:, L, s, v, c:c + 1], scale=1.0),
            r=[bt, C.b_mod], w=[b_y])


def stage_ffn(C, P, L, s, src, srcbuf, tiles):
    nc = C.nc
    wgu_d = C.ffn_gu[s // 2][L]
    wdn_d = C.ffn_dn[s // 2][L]
    T = TT
    with ExitStack() as st:
        sb = mk_sb(nc, st)
        wgu = sb("wgu", [128, 8, 2 * DFF], BF16)
        wdn = sb("wdn", [128, 22, D], BF16)
        hs = [sb(f"h{i}", [128, 8, T]) for i in range(2)]
        ys = [sb(f"y{i}", [128, 8, T], BF16) for i in range(2)]
        actb = sb("actb", [128, 22, T], BF16)
        sq = sb("sq", [128, 8, T], BF16)
        rstd = sb("rstd", [128, T])
        tmps = [sb(f"tmp{i}", [128, T]) for i in range(2)]
        sgs = [sb(f"sg{i}", [128, T]) for i in range(2)]
        b_wgu = P.bufs(8, "wgu")
        b_wdn = P.bufs(22, "wdn")
        b_hs = [P.bufs(8, f"h{i}_") for i in range(2)]
        b_ys = P.bufs(2, "y")
        b_act = P.bufs(22, "act")
        b_sq, b_rstd = P.bufs(2, "nrm")
        b_tmps = P.bufs(2, "tmp")
        b_sgs = P.bufs(2, "sg")
        for kc in range(8):
            P.dma("pool", wgu[:, kc, :], wgu_d[kc * 128:(kc + 1) * 128, :], w=[b_wgu[kc]])
        for f in range(22):
            P.dma("pool", wdn[:, f, :], wdn_d[f * 128:(f + 1) * 128, :], w=[b_wdn[f]])

        def prologue(i):
            ti = tiles[i]
            h, bh, y, by = hs[i % 2], b_hs[i % 2], ys[i % 2], b_ys[i % 2]
            v = 1 if ti == 0 else 0
            P.dma("sp", h[:], fm(src, ti * T, T), r=[srcbuf[ti]], w=bh)
            emit_norm(C, P, h, bh, T, sq, b_sq, rstd, b_rstd, 0)
            emit_modulate(C, P, h, bh, y, by, rstd, b_rstd, tmps, b_tmps, L, s, v, T)

        prologue(0)
        for i, ti in enumerate(tiles):
            h, bh, y, by = hs[i % 2], b_hs[i % 2], ys[i % 2], b_ys[i % 2]
            v = 1 if ti == 0 else 0
            for f in range(22):
                pgb = 1 + 2 * (f % 2)
                pub = 2 + 2 * (f % 2)
                pg, pu = C.ps[pgb], C.ps[pub]
                for kc in range(8):
                    P.op("pe", lambda e, f=f, kc=kc, pg=pg, y=y: e.matmul(
                        pg[:, :T], lhsT=wgu[:, kc, f * 128:(f + 1) * 128], rhs=y[:, kc, :],
                        start=(kc == 0), stop=(kc == 7)), r=[b_wgu[kc], by], w=[C.bps[pgb]])
                for kc in range(8):
                    P.op("pe", lambda e, f=f, kc=kc, pu=pu, y=y: e.matmul(
                        pu[:, :T], lhsT=wgu[:, kc, DFF + f * 128:DFF + (f + 1) * 128], rhs=y[:, kc, :],
                        start=(kc == 0), stop=(kc == 7)), r=[b_wgu[kc], by], w=[C.bps[pub]])
                sg, bsg = sgs[f % 2], b_sgs[f % 2]
                P.op("act", lambda e, sg=sg, pg=pg: e.activation(out=sg[:], in_=pg[:, :T], func=AF.Silu),
                     r=[C.bps[pgb]], w=[bsg])
                P.op("dve", lambda e, f=f, sg=sg, pu=pu: e.tensor_tensor(
                    out=actb[:, f, :], in0=sg[:], in1=pu[:, :T], op=ALU.mult), r=[bsg, C.bps[pub]], w=[b_act[f]])
                if f == 10 and i + 1 < len(tiles):
                    prologue(i + 1)
            for dc in range(8):
                pob = 5 + dc % 3
                po = C.ps[pob]
                for f in range(22):
                    P.op("pe", lambda e, f=f, dc=dc, po=po: e.matmul(
                        po[:, :T], lhsT=wdn[:, f, dc * 128:(dc + 1) * 128], rhs=actb[:, f, :],
                        start=(f == 0), stop=(f == 21)), r=[b_wdn[f], b_act[f]], w=[C.bps[pob]])
                P.op("dve", lambda e, dc=dc, po=po, h=h, v=v: e.scalar_tensor_tensor(
                    out=h[:, dc, :], in0=po[:, :T], scalar=C.mG[:, L, s, v, dc:dc + 1], in1=h[:, dc, :],
                    op0=ALU.mult, op1=ALU.add), r=[C.bps[pob], bh[dc], C.b_mod], w=[bh[dc]])
            P.dma("sp", fm(C.hT, ti * T, T), h[:], r=bh, w=[C.hbuf[ti]])
        P.barrier()


def stage_final(C, P, tiles):
    nc = C.nc
    T = TT
    toks = []
    with ExitStack() as st:
        sb = mk_sb(nc, st)
        hs = [sb(f"fh{i}", [128, 8, T]) for i in range(2)]
        os_ = [sb(f"fo{i}", [128, 8, T]) for i in range(2)]
        sq = sb("fsq", [128, 8, T], BF16)
        rstd = sb("frstd", [128, T])
        b_hs = [P.bufs(8, f"fh{i}_") for i in range(2)]
        b_os = P.bufs(2, "fo")
        b_sq, b_rstd = P.bufs(2, "fn")
        for i, ti in enumerate(tiles):
            h, bh, o, bo = hs[i % 2], b_hs[i % 2], os_[i % 2], b_os[i % 2]
            P.dma("sp", h[:], fm(C.hT, ti * T, T), r=[C.hbuf[ti]], w=bh)
            emit_norm(C, P, h, bh, T, sq, b_sq, rstd, b_rstd, 0)
            for c in range(8):
                P.op("dve", lambda e, c=c, h=h, o=o: e.scalar_tensor_tensor(
                    out=o[:, c, :], in0=h[:, c, :], scalar=C.fg_sb[:, c:c + 1], in1=rstd[:],
                    op0=ALU.mult, op1=ALU.mult), r=[bh[c], b_rstd, C.b_const], w=[bo])
            toks.append(P.dma("sp", fm(C.outT, (ti - 1) * T, T), o[:], r=[bo], w=[C.obuf[ti]]))
        P.barrier()
    return toks


def build_program(stages, debug=False):
    nc = bass.Bass("TRN2", target_bir_lowering=False)
    C = declare_io(nc, debug)
    declare_odd(C)
    declare_even(C)
    P = Prog(nc)
    all_tiles = list(range(NTILES))
    lat_tiles = list(range(1, NTILES))
    with ExitStack() as st:
        setup_persistent(C, P, st)
        src, srcbuf = C.xT, C.xbuf
        toks = []
        for sg in stages:
            kind = sg[0]
            if kind == "ada":
                stage_ada(C, P)
            elif kind == "ffn":
                _, L, s, with_ctx = sg
                stage_ffn(C, P, L, s, src, srcbuf, all_tiles if with_ctx else lat_tiles)
                src, srcbuf = C.hT, C.hbuf
            elif kind == "odd":
                _, L, with_ctx = sg
                stage_odd(C, P, L, all_tiles if with_ctx else lat_tiles)
            elif kind == "even":
                _, L = sg
                zbuf = P.bufs(NT // 128, "ztok")
                fbuf = P.bufs(NTILES, "fT")
                stage_even_A(C, P, L, all_tiles, zbuf, fbuf)
                stage_rwkv(C, P, L)
                stage_even_C(C, P, L, all_tiles)
            elif kind == "evenA":
                _, L = sg
                zbuf = P.bufs(NT // 128, "ztok")
                fbuf = P.bufs(NTILES, "fT")
                stage_even_A(C, P, L, all_tiles, zbuf, fbuf)
            elif kind == "final":
                toks += stage_final(C, P, lat_tiles)
            else:
                raise ValueError(kind)
        if debug:
            for ti in range(NTILES):
                toks.append(P.dma("sp", C.dbg[:, ti * TT:(ti + 1) * TT], C.hT[:, ti * TT:(ti + 1) * TT], r=[C.hbuf[ti]], w=[P.buf()]))
        P.finish("sp", toks)
        P.emit(st)
    return nc


def chunked(v, n=128):
    v = np.asarray(v, np.float32)
    lead = v.shape[:-1]
    k = v.shape[-1] // n
    return np.ascontiguousarray(np.moveaxis(v.reshape(lead + (k, n)), -1, 0))


def prep_inputs(inp, ncores=8):
    f = lambda a: np.ascontiguousarray(np.asarray(a, np.float32))
    shared = {
        "ada_w": f(inp["ada_w"]),
        "ada_b": chunked(inp["ada_b"]).reshape(128, -1),
        "norm_g": chunked(inp["norm_g"]).reshape(128, -1),
        "final_g": chunked(inp["final_g"]).reshape(128, -1),
        "ident": np.eye(128, dtype=np.float32),
        "ffn1_w_gu": f(inp["ffn1_w_gu"]), "ffn2_w_gu": f(inp["ffn2_w_gu"]),
        "ffn1_w_down": f(inp["ffn1_w_down"]), "ffn2_w_down": f(inp["ffn2_w_down"]),
    }
    prep_odd(inp, shared)
    prep_even(inp, shared)
    maps = []
    cc = chunked(inp["c_ctx"])
    for b in range(ncores):
        m = dict(shared)
        m["xT"] = np.ascontiguousarray(np.concatenate([np.asarray(inp["ctx"][b]).T, np.asarray(inp["x"][b]).T], axis=1), dtype=np.float32)
        cb = chunked(inp["c"][b])
        m["cv"] = np.ascontiguousarray(np.stack([cb, cc], axis=-1).reshape(128, 16))
        maps.append(m)
    return maps


CONVW = 31
SEQS = {0: (0, CTX), 1: (CTX, NT)}


def seq_of_tile(ti):
    return 0 if ti == 0 else 1


def declare_odd(C):
    nc = C.nc
    di = lambda name, shape: nc.dram_tensor(name, list(shape), F32, kind="ExternalInput").ap()
    C.o_w_in = di("o_w_in", [2, D, DIN_O])
    C.o_w_out = di("o_w_out", [2, D, D])
    C.o_cw = di("o_cw", [128, 2 * 6 * CONVW])
    C.o_small = di("o_small", [128, 2 * 14])
    C.o_pwbd = di("o_pwbd", [128, 2 * 2 * 128])
    C.o_mask = di("o_mask", [128, 2 * 16])
    C.o_invc = di("o_invc", [256, NT])
    C.uT = nc.dram_tensor("uT", [768, NT], F32).ap()
    C.qT = nc.dram_tensor("qT", [256, NT], F32).ap()


def stage_odd(C, P, L, tiles):
    nc = C.nc
    j = L // 2
    T = TT
    ubuf = P.bufs(NTILES, "uT")
    qbuf = P.bufs(NTILES, "qT")
    with ExitStack() as st:
        sb = mk_sb(nc, st)
        win = sb("o_win", [128, 8, DIN_O], BF16)
        wout = sb("o_wout", [128, 8, D], BF16)
        cw = sb("o_cw", [128, 6, CONVW])
        small = sb("o_small", [128, 14])
        pwbd = sb("o_pwbd", [128, 2, 128])
        mask = sb("o_mask", [128, 2, 16])
        b_win = P.bufs(8, "owin")
        b_wout = P.bufs(8, "owout")
        b_oc = P.buf("oconst")
        for kc in range(8):
            P.dma("pool", win[:, kc, :], C.o_w_in[j, kc * 128:(kc + 1) * 128, :], w=[b_win[kc]])
        for kc in range(8):
            P.dma("pool", wout[:, kc, :], C.o_w_out[j, kc * 128:(kc + 1) * 128, :], w=[b_wout[kc]])
        P.dma("sp", cw[:].rearrange("p a b -> p (a b)"), C.o_cw[:, j * 186:(j + 1) * 186], w=[b_oc])
        P.dma("sp", small[:], C.o_small[:, j * 14:(j + 1) * 14], w=[b_oc])
        P.dma("sp", pwbd[:].rearrange("p a b -> p (a b)"), C.o_pwbd[:, j * 256:(j + 1) * 256], w=[b_oc])
        P.dma("sp", mask[:].rearrange("p a b -> p (a b)"), C.o_mask[:, :], w=[b_oc])
        hs = [sb(f"oh{i}", [128, 8, T]) for i in range(2)]
        ys = [sb(f"oy{i}", [128, 8, T], BF16) for i in range(2)]
        us = [sb(f"ou{i}", [128, 6, T]) for i in range(2)]
        qs = [sb(f"oq{i}", [128, 2, T]) for i in range(2)]
        sq = sb("osq", [128, 8, T], BF16)
        rstd = sb("orstd", [128, T])
        tmps = [sb(f"otmp{i}", [128, T]) for i in range(2)]
        sgs = [sb(f"osg{i}", [128, T]) for i in range(2)]
        b_hs = [P.bufs(8, f"oh{i}_") for i in range(2)]
        b_ys = P.bufs(2, "oy")
        b_us = P.bufs(2, "ou")
        b_qs = P.bufs(2, "oq")
        b_sq, b_rstd = P.bufs(2, "onrm")
        b_tmps = P.bufs(2, "otmp")
        b_sgs = P.bufs(2, "osg")

        def prologue(i):
            ti = tiles[i]
            h, bh, y, by = hs[i % 2], b_hs[i % 2], ys[i % 2], b_ys[i % 2]
            v = 1 if ti == 0 else 0
            P.dma("sp", h[:], fm(C.hT, ti * T, T), r=[C.hbuf[ti]], w=bh)
            emit_norm(C, P, h, bh, T, sq, b_sq, rstd, b_rstd, 0)
            emit_modulate(C, P, h, bh, y, by, rstd, b_rstd, tmps, b_tmps, L, 1, v, T)

        prologue(0)
        for i, ti in enumerate(tiles):
            y, by = ys[i % 2], b_ys[i % 2]
            u, bu, q, bq = us[i % 2], b_us[i % 2], qs[i % 2], b_qs[i % 2]
            if i + 1 < len(tiles):
                prologue(i + 1)
            for oc in range(6):
                pab = 1 + 2 * (oc % 2)
                pbb = 2 + 2 * (oc % 2)
                pa, pb = C.ps[pab], C.ps[pbb]
                for kc in range(8):
                    P.op("pe", lambda e, oc=oc, kc=kc, pa=pa, y=y: e.matmul(
                        pa[:, :T], lhsT=win[:, kc, oc * 128:(oc + 1) * 128], rhs=y[:, kc, :],
                        start=(kc == 0), stop=(kc == 7)), r=[b_win[kc], by], w=[C.bps[pab]])
                for kc in range(8):
                    P.op("pe", lambda e, oc=oc, kc=kc, pb=pb, y=y: e.matmul(
                        pb[:, :T], lhsT=win[:, kc, 768 + oc * 128:768 + (oc + 1) * 128], rhs=y[:, kc, :],
                        start=(kc == 0), stop=(kc == 7)), r=[b_win[kc], by], w=[C.bps[pbb]])
                sg, bsg = sgs[oc % 2], b_sgs[oc % 2]
                P.op("act", lambda e, sg=sg, pb=pb: e.activation(out=sg[:], in_=pb[:, :T], func=AF.Sigmoid),
                     r=[C.bps[pbb]], w=[bsg])
                P.op("dve", lambda e, oc=oc, sg=sg, pa=pa, u=u: e.tensor_tensor(
                    out=u[:, oc, :], in0=sg[:], in1=pa[:, :T], op=ALU.mult), r=[bsg, C.bps[pab]], w=[bu])
            for qc in range(2):
                pqb = 5 + qc
                pq = C.ps[pqb]
                for kc in range(8):
                    P.op("pe", lambda e, qc=qc, kc=kc, pq=pq, y=y: e.matmul(
                        pq[:, :T], lhsT=win[:, kc, 1536 + qc * 128:1536 + (qc + 1) * 128], rhs=y[:, kc, :],
                        start=(kc == 0), stop=(kc == 7)), r=[b_win[kc], by], w=[C.bps[pqb]])
                P.op("act", lambda e, qc=qc, pq=pq, q=q: e.activation(out=q[:, qc, :], in_=pq[:, :T], func=AF.Copy),
                     r=[C.bps[pqb]], w=[bq])
            P.dma("pool", C.uT.rearrange("(c p) t -> p c t", p=128)[:, :, ti * T:(ti + 1) * T], u[:], r=[bu], w=[ubuf[ti]])
            P.dma("pool", C.qT.rearrange("(c p) t -> p c t", p=128)[:, :, ti * T:(ti + 1) * T], q[:], r=[bq], w=[qbuf[ti]])
        HU = T + 30
        HQ = T + 16
        uh = [sb(f"ouh{i}", [128, 6, HU], BF16) for i in range(2)]
        dg = sb("odg", [128, 6, CONVW, 128], BF16)
        b_dg = P.buf("odg")
        for c in range(6):
            for kk in range(CONVW):
                P.op("dve", lambda e, c=c, kk=kk: e.tensor_scalar(out=dg[:, c, kk, :], in0=C.ident_sb[:], scalar1=cw[:, c, kk:kk + 1], scalar2=None, op0=ALU.mult),
                     r=[b_oc, C.b_const], w=[b_dg])
        qh = [sb(f"oqh{i}", [128, 2, HQ]) for i in range(2)]
        ic = [sb(f"oic{i}", [128, 2, T]) for i in range(2)]
        cv = [sb(f"ocv{i}", [128, 6, T]) for i in range(2)]
        pl = [sb(f"opl{i}", [128, 2, T]) for i in range(2)]
        cat = [sb(f"ocat{i}", [128, 8, T], BF16) for i in range(2)]
        sq6 = sb("osq6", [128, 6, T], BF16)
        b_uh = P.bufs(2, "ouh")
        b_qh = P.bufs(2, "oqh")
        b_ic = P.bufs(2, "oic")
        b_cv = [P.bufs(6, f"ocv{i}_") for i in range(2)]
        b_pl = [P.bufs(2, f"opl{i}_") for i in range(2)]
        b_cat = [P.bufs(8, f"ocat{i}_") for i in range(2)]

        def loadsB(i):
            ti = tiles[i]
            s0, s1 = SEQS[seq_of_tile(ti)]
            t0 = ti * T
            k = i % 2
            for (buf, bb, dram, nch, hl, hr, tb) in ((uh[k], b_uh[k], C.uT, 6, 15, 15, ubuf), (qh[k], b_qh[k], C.qT, 2, 8, 7, qbuf)):
                lo, hi = max(t0 - hl, s0), min(t0 + T + hr, s1)
                if lo > t0 - hl or hi < t0 + T + hr:
                    P.op("pool", lambda e, buf=buf: e.memset(buf[:], 0.0), w=[bb])
                deps = [tb[x] for x in (ti - 1, ti, ti + 1) if 0 <= x < NTILES and seq_of_tile(x) == seq_of_tile(ti)]
                P.dma("pool" if nch == 6 else "sp", buf[:, :, lo - (t0 - hl):hi - (t0 - hl)],
                      dram.rearrange("(c p) t -> p c t", p=128)[:, :, lo:hi], r=deps, w=[bb])
            P.dma("sp", ic[k][:], C.o_invc.rearrange("(c p) t -> p c t", p=128)[:, :, t0:t0 + T], w=[b_ic[k]])
            P.dma("sp", hs[k][:], fm(C.hT, t0, T), r=[C.hbuf[ti]], w=b_hs[k])

        loadsB(0)
        for i, ti in enumerate(tiles):
            k = i % 2
            v = 1 if ti == 0 else 0
            if i + 1 < len(tiles):
                loadsB(i + 1)
            u_h, q_h, icv, conv, pool, ct, h, bh = uh[k], qh[k], ic[k], cv[k], pl[k], cat[k], hs[k], b_hs[k]
            for c in range(6):
                pcb = 6 + c % 2
                for kk in range(CONVW):
                    P.op("pe", lambda e, c=c, kk=kk, u_h=u_h, pcb=pcb: e.matmul(
                        C.ps[pcb][:, :T], lhsT=dg[:, c, kk, :], rhs=u_h[:, c, kk:kk + T], start=(kk == 0), stop=(kk == CONVW - 1)),
                        r=[b_dg, b_uh[k]], w=[C.bps[pcb]])
                P.op("act", lambda e, c=c, conv=conv, pcb=pcb: e.activation(
                    out=conv[:, c, :], in_=C.ps[pcb][:, :T], func=AF.Identity, bias=small[:, c:c + 1], scale=1.0),
                    r=[C.bps[pcb], b_oc], w=[b_cv[k][c]])
            P.op("act", lambda e, conv=conv: e.activation(out=sq6[:], in_=conv[:], func=AF.Square), r=b_cv[k], w=[b_sq])
            for c in range(6):
                P.op("pe", lambda e, c=c: e.matmul(C.ps[0][:, :T], lhsT=C.ones_bf[:], rhs=sq6[:, c, :], start=(c == 0), stop=(c == 5)),
                     r=[b_sq, C.b_const], w=[C.bps[0]])
            P.op("act", lambda e: e.activation(out=rstd[:], in_=C.ps[0][:, :T], func=AF.Sqrt, bias=C.eps_sb[:, 0:1], scale=1.0 / 768),
                 r=[C.bps[0], C.b_const], w=[b_rstd])
            P.op("dve", lambda e: e.reciprocal(out=rstd[:], in_=rstd[:]), r=[b_rstd], w=[b_rstd])
            for c in range(6):
                tmp, bt = tmps[c % 2], b_tmps[c % 2]
                P.op("dve", lambda e, c=c, tmp=tmp, conv=conv: e.scalar_tensor_tensor(
                    out=tmp[:], in0=conv[:, c, :], scalar=small[:, 6 + c:7 + c], in1=rstd[:], op0=ALU.mult, op1=ALU.mult),
                    r=[b_cv[k][c], b_rstd, b_oc], w=[bt])
                P.op("act", lambda e, c=c, tmp=tmp, ct=ct: e.activation(out=ct[:, c, :], in_=tmp[:], func=AF.Silu),
                     r=[bt], w=[b_cat[k][c]])
            for c in range(2):
                koffs = range(6, 10) if c == 0 else range(0, 16)
                first = True
                for kk in koffs:
                    if first:
                        P.op("dve", lambda e, c=c, kk=kk, pool=pool, q_h=q_h: e.tensor_scalar(
                            out=pool[:, c, :], in0=q_h[:, c, kk:kk + T], scalar1=mask[:, c, kk:kk + 1], scalar2=None, op0=ALU.mult),
                            r=[b_qh[k], b_oc], w=[b_pl[k][c]])
                        first = False
                    else:
                        P.op("dve", lambda e, c=c, kk=kk, pool=pool, q_h=q_h: e.scalar_tensor_tensor(
                            out=pool[:, c, :], in0=q_h[:, c, kk:kk + T], scalar=mask[:, c, kk:kk + 1], in1=pool[:, c, :],
                            op0=ALU.mult, op1=ALU.add), r=[b_qh[k], b_oc, b_pl[k][c]], w=[b_pl[k][c]])
                P.op("dve", lambda e, c=c, pool=pool, icv=icv: e.tensor_tensor(
                    out=pool[:, c, :], in0=pool[:, c, :], in1=icv[:, c, :], op=ALU.mult), r=[b_pl[k][c], b_ic[k]], w=[b_pl[k][c]])
                P.op("dve", lambda e, c=c, pool=pool, q_h=q_h: e.tensor_tensor(
                    out=pool[:, c, :], in0=pool[:, c, :], in1=q_h[:, c, 8:8 + T], op=ALU.subtract), r=[b_pl[k][c], b_qh[k]], w=[b_pl[k][c]])
                ppb = 1 + c
                P.op("pe", lambda e, c=c, pool=pool, ppb=ppb: e.matmul(C.ps[ppb][:, :T], lhsT=pwbd[:, c, :], rhs=pool[:, c, :], start=True, stop=True),
                     r=[b_pl[k][c], b_oc], w=[C.bps[ppb]])
                P.op("act", lambda e, c=c, ct=ct, ppb=ppb: e.activation(
                    out=ct[:, 6 + c, :], in_=C.ps[ppb][:, :T], func=AF.Identity, scale=small[:, 12 + c:13 + c]),
                    r=[C.bps[ppb], b_oc], w=[b_cat[k][6 + c]])
            for dc in range(8):
                pob = 3 + dc % 3
                po = C.ps[pob]
                for kc in range(8):
                    P.op("pe", lambda e, dc=dc, kc=kc, po=po, ct=ct: e.matmul(
                        po[:, :T], lhsT=wout[:, kc, dc * 128:(dc + 1) * 128], rhs=ct[:, kc, :],
                        start=(kc == 0), stop=(kc == 7)), r=[b_wout[kc], b_cat[k][kc]], w=[C.bps[pob]])
                P.op("dve", lambda e, dc=dc, po=po, h=h, v=v: e.scalar_tensor_tensor(
                    out=h[:, dc, :], in0=po[:, :T], scalar=C.mG[:, L, 1, v, dc:dc + 1], in1=h[:, dc, :],
                    op0=ALU.mult, op1=ALU.add), r=[C.bps[pob], bh[dc], C.b_mod], w=[bh[dc]])
            P.dma("pool", fm(C.hT, ti * T, T), h[:], r=bh, w=[C.hbuf[ti]])
        P.barrier()


def prep_odd(inp, shared):
    cw = np.asarray(inp["o_conv_w"], np.float32)
    a = cw.transpose(0, 2, 1).reshape(2, 6, 128, CONVW)
    shared["o_cw"] = np.ascontiguousarray(a.transpose(2, 0, 1, 3).reshape(128, -1))
    sm = np.concatenate([chunked(inp["o_conv_b"]), chunked(inp["o_cnorm_g"]), chunked(inp["o_pool_scale"])], axis=-1)
    shared["o_small"] = np.ascontiguousarray(sm.reshape(128, -1))
    pw = np.asarray(inp["o_pool_w"], np.float32)
    bd = np.zeros((128, 2, 2, 128), np.float32)
    for j in range(2):
        for c in range(2):
            for gi in range(2):
                bd[gi * 64:(gi + 1) * 64, j, c, gi * 64:(gi + 1) * 64] = pw[j, 2 * c + gi]
    shared["o_pwbd"] = np.ascontiguousarray(bd.reshape(128, -1))
    widths = (2, 4, 8, 16)
    mask = np.zeros((128, 2, 16), np.float32)
    invc = np.zeros((256, NT), np.float32)
    for g, wd in enumerate(widths):
        c, gi = g // 2, g % 2
        for kk in range(16):
            off = kk - 8
            if -(wd // 2) <= off <= wd - 1 - wd // 2:
                mask[gi * 64:(gi + 1) * 64, c, kk] = 1.0
        for (s0, n) in ((0, CTX), (CTX, SEQ)):
            t = np.arange(n)
            lo = np.maximum(t - wd // 2, 0)
            hi = np.minimum(t + (wd - 1 - wd // 2), n - 1)
            invc[g * 64:(g + 1) * 64, s0:s0 + n] = (1.0 / (hi - lo + 1).astype(np.float32))[None, :]
    shared["o_mask"] = np.ascontiguousarray(mask.reshape(128, -1))
    shared["o_invc"] = invc
    shared["o_w_in"] = np.ascontiguousarray(np.asarray(inp["o_w_in"], np.float32))
    shared["o_w_out"] = np.ascontiguousarray(np.asarray(inp["o_w_out"], np.float32))


import ml_dtypes

DA = 2560
DR = 768
NH = 12
SC = 128
DECAY = 0.606531
GN_EPS = 64e-5
TABW = 9 * DR
RWKV_LIMIT = 0


def declare_even(C):
    nc = C.nc
    di = lambda name, shape, dt=F32: nc.dram_tensor(name, list(shape), dt, kind="ExternalInput").ap()
    C.e_w_in = di("e_w_in", [2, D, DIN_E])
    C.e_w_out = di("e_w_out", [2, D, D])
    C.e_mu = di("e_mu", [128, 2 * DA])
    C.e_tab = di("e_tab", [128, 2 * TABW])
    C.e_w_up = di("e_w_up", [2, 2, 64, DR])
    C.e_a_up = di("e_a_up", [2, 2, 64, DR])
    C.e_g_up = di("e_g_up", [2, 128, DR])
    C.e_masks = di("e_masks", [128, 2 * 768])
    C.cosT = di("cosT", [SEQ, SEQ], BF16)
    C.sinT = di("sinT", [SEQ, SEQ], BF16)
    C.cosc = di("cosc", [CTX, CTX], BF16)
    C.sinc = di("sinc", [CTX, CTX], BF16)
    C.cdft = di("cdft", [128, 4 * 128])
    C.z_tok = nc.dram_tensor("z_tok", [NT, DIN_E], F32).ap()
    C.fT = nc.dram_tensor("fT", [256, NT], F32).ap()
    C.yf_tok = nc.dram_tensor("yf_tok", [NT, DR], F32).ap()
    C.o_tok = nc.dram_tensor("o_tok", [NT, DR], F32).ap()


def stage_even_A(C, P, L, tiles, zbuf, fbuf):
    nc = C.nc
    j = L // 2
    T = TT
    with ExitStack() as st:
        sb = mk_sb(nc, st)
        win = sb("e_win", [128, 8, DIN_E], BF16)
        b_win = P.bufs(8, "ewin")
        for kc in range(8):
            P.dma("pool", win[:, kc, :], C.e_w_in[j, kc * 128:(kc + 1) * 128, :], w=[b_win[kc]])
        hs = [sb(f"eh{i}", [128, 8, T]) for i in range(2)]
        ys = [sb(f"ey{i}", [128, 8, T], BF16) for i in range(2)]
        zt = [sb(f"ezt{i}", [128, DIN_E]) for i in range(2)]
        sq = sb("esq", [128, 8, T], BF16)
        rstd = sb("erstd", [128, T])
        tmps = [sb(f"etmp{i}", [128, T]) for i in range(2)]
        b_hs = [P.bufs(8, f"eh{i}_") for i in range(2)]
        b_ys = P.bufs(2, "ey")
        b_zt = P.bufs(2, "ezt")
        b_sq, b_rstd = P.bufs(2, "enrm")
        b_tmps = P.bufs(2, "etmp")

        def prologue(i):
            ti = tiles[i]
            h, bh, y, by = hs[i % 2], b_hs[i % 2], ys[i % 2], b_ys[i % 2]
            v = 1 if ti == 0 else 0
            P.dma("sp", h[:], fm(C.hT, ti * T, T), r=[C.hbuf[ti]], w=bh)
            emit_norm(C, P, h, bh, T, sq, b_sq, rstd, b_rstd, 0)
            emit_modulate(C, P, h, bh, y, by, rstd, b_rstd, tmps, b_tmps, L, 1, v, T)

        colblocks = [(c0, min(512, DIN_E - c0)) for c0 in range(0, DIN_E, 512)]
        prologue(0)
        nz = 0
        nps = 0
        for i, ti in enumerate(tiles):
            y, by = ys[i % 2], b_ys[i % 2]
            if i + 1 < len(tiles):
                prologue(i + 1)
            for tb in range(T // 128):
                z, bz = zt[nz % 2], b_zt[nz % 2]
                nz += 1
                for (c0, wd) in colblocks:
                    pb = 1 + nps % 7
                    nps += 1
                    ps = C.ps[pb]
                    for kc in range(8):
                        P.op("pe", lambda e, kc=kc, ps=ps, y=y, tb=tb, c0=c0, wd=wd: e.matmul(
                            ps[:, :wd], lhsT=y[:, kc, tb * 128:(tb + 1) * 128], rhs=win[:, kc, c0:c0 + wd],
                            start=(kc == 0), stop=(kc == 7)), r=[b_win[kc], by], w=[C.bps[pb]])
                    if nps % 2 == 0:
                        P.op("act", lambda e, ps=ps, z=z, c0=c0, wd=wd: e.activation(out=z[:, c0:c0 + wd], in_=ps[:, :wd], func=AF.Copy),
                             r=[C.bps[pb]], w=[bz])
                    else:
                        P.op("dve", lambda e, ps=ps, z=z, c0=c0, wd=wd: e.tensor_copy(out=z[:, c0:c0 + wd], in_=ps[:, :wd]),
                             r=[C.bps[pb]], w=[bz])
                r0 = ti * T + tb * 128
                P.dma("pool", C.z_tok[r0:r0 + 128, :], z[:], r=[bz], w=[zbuf[r0 // 128]])
        P.barrier()
    with ExitStack() as st:
        sb = mk_sb(nc, st)
        cd = sb("cdft", [128, 4, 128], BF16)
        b_cd = P.buf("cdft")
        P.dma("pool", cd[:].rearrange("p a b -> p (a b)"), C.cdft[:, :], w=[b_cd])
        seqs = []
        if 0 in tiles:
            seqs.append((0, CTX, C.cosc, C.sinc, 2, 3, CTX))
        seqs.append((CTX, SEQ, C.cosT, C.sinT, 0, 1, 512))
        zf = sb("zf", [128, SEQ // 128, 256], BF16)
        cosb = [sb(f"cosb{i}", [128, SEQ // 128, 512], BF16) for i in range(2)]
        sinb = [sb(f"sinb{i}", [128, SEQ // 128, 512], BF16) for i in range(2)]
        pq = [sb(f"pq{i}", [128, 2, 512], BF16) for i in range(2)]
        fo = [sb(f"fo{i}", [128, 512]) for i in range(2)]
        b_zf = P.buf("zf")
        b_cos = P.bufs(2, "cosb")
        b_sin = P.bufs(2, "sinb")
        b_pq = P.bufs(2, "pq")
        b_fo = P.bufs(2, "fo")
        nk = 0
        nf = 0
        for (s0, N, ctab, stab, ic, isn, KW) in seqs:
            nch = N // 128
            P.dma("pool", zf[:, :nch, :], C.z_tok[s0:s0 + N, DA:DA + 256].rearrange("(nc p) c -> p nc c", p=128),
                  r=[zbuf[(s0 // 128) + x] for x in range(nch)], w=[b_zf])
            for kb in range(N // KW):
                cb, sbb = cosb[nk % 2], sinb[nk % 2]
                bc, bs = b_cos[nk % 2], b_sin[nk % 2]
                nk += 1
                P.dma("sp", cb[:, :nch, :KW], ctab[:, kb * KW:(kb + 1) * KW].rearrange("(nc p) k -> p nc k", p=128), w=[bc])
                P.dma("sp", sbb[:, :nch, :KW], stab[:, kb * KW:(kb + 1) * KW].rearrange("(nc p) k -> p nc k", p=128), w=[bs])
                for hh in range(2):
                    pp, bpq = pq[nf % 2], b_pq[nf % 2]
                    f, bf = fo[nf % 2], b_fo[nf % 2]
                    nf += 1
                    for n_ in range(nch):
                        P.op("pe", lambda e, n_=n_, cb=cb, hh=hh, KW=KW, nch=nch: e.matmul(
                            C.ps[1][:, :KW], lhsT=zf[:, n_, hh * 128:(hh + 1) * 128], rhs=cb[:, n_, :KW],
                            start=(n_ == 0), stop=(n_ == nch - 1)), r=[b_zf, bc], w=[C.bps[1]])
                    for n_ in range(nch):
                        P.op("pe", lambda e, n_=n_, sbb=sbb, hh=hh, KW=KW, nch=nch: e.matmul(
                            C.ps[2][:, :KW], lhsT=zf[:, n_, hh * 128:(hh + 1) * 128], rhs=sbb[:, n_, :KW],
                            start=(n_ == 0), stop=(n_ == nch - 1)), r=[b_zf, bs], w=[C.bps[2]])
                    P.op("act", lambda e, pp=pp, KW=KW: e.activation(out=pp[:, 0, :KW], in_=C.ps[1][:, :KW], func=AF.Copy),
                         r=[C.bps[1]], w=[bpq])
                    P.op("dve", lambda e, pp=pp, KW=KW: e.tensor_copy(out=pp[:, 1, :KW], in_=C.ps[2][:, :KW]),
                         r=[C.bps[2]], w=[bpq])
                    pfb = 3 + nf % 2
                    P.op("pe", lambda e, pp=pp, KW=KW, pfb=pfb, ic=ic: e.matmul(
                        C.ps[pfb][:, :KW], lhsT=cd[:, ic, :], rhs=pp[:, 0, :KW], start=True, stop=False),
                        r=[b_cd, bpq], w=[C.bps[pfb]])
                    P.op("pe", lambda e, pp=pp, KW=KW, pfb=pfb, isn=isn: e.matmul(
                        C.ps[pfb][:, :KW], lhsT=cd[:, isn, :], rhs=pp[:, 1, :KW], start=False, stop=True),
                        r=[b_cd, bpq], w=[C.bps[pfb]])
                    P.op("act", lambda e, f=f, KW=KW, pfb=pfb: e.activation(out=f[:, :KW], in_=C.ps[pfb][:, :KW], func=AF.Copy),
                         r=[C.bps[pfb]], w=[bf])
                    c0 = s0 + kb * KW
                    P.dma("pool", C.fT[hh * 128:(hh + 1) * 128, c0:c0 + KW], f[:, :KW], r=[bf],
                          w=[fbuf[(c0 // TT) + x] for x in range(max(1, KW // TT))])
        P.barrier()


def stage_rwkv(C, P, L):
    nc = C.nc
    j = L // 2
    colmajor = (j % 2 == 1)
    NSC = NT // SC
    ybuf = P.bufs(NSC, "yf")

    def OP(eng, method, r, w, *args, **kw):
        return P.op(eng, lambda e: getattr(e, method)(*args, **kw), r=r, w=w)

    def row_runs(seq, sc, delta):
        N = CTX if seq == 0 else SEQ
        lo = sc * SC + delta
        hi = lo + SC
        a, b = max(lo, 0), min(hi, N)
        runs = []
        if seq == 0 or not colmajor:
            base = 0 if seq == 0 else CTX
            runs.append((a - lo, b - lo, ("lin", base + a, base + b)))
        else:
            pos = a
            while pos < b:
                w_ = pos // 64
                e_ = min(b, (w_ + 1) * 64)
                runs.append((pos - lo, e_ - lo, ("cm", w_, pos % 64, pos % 64 + (e_ - pos))))
                pos = e_
        return runs, (a > lo or b < hi)

    def dram_rows(tens, spec, c0, c1):
        if spec[0] == "lin":
            return tens[spec[1]:spec[2], c0:c1]
        _, w_, r0, r1 = spec
        return tens[CTX:NT, c0:c1].rearrange("(r w) c -> w r c", w=64)[w_, r0:r1, :]

    with ExitStack() as st:
        sb = mk_sb(nc, st)
        mu = sb("r_mu", [128, DA])
        tab = sb("r_tab", [128, 9, DR])
        wa_up = sb("r_waup", [128, 2, DR])
        gup = sb("r_gup", [128, DR])
        masks = sb("r_masks", [128, 2, 768])
        ones = sb("r_ones", [128, 128])
        gneps = sb("r_gneps", [128, 1])
        zcs = [sb(f"r_zc{i}", [128, DA]) for i in range(2)]
        zp = sb("r_zp", [128, DA])
        zn = sb("r_zn", [128, DA])
        lrT = sb("r_lrT", [128, 128])
        sgT = sb("r_sgT", [128, 128])
        S = {i: sb(f"r_S{i}", [128, DR]) for i in (1, 2, 3, 4, 6, 7)}
        S8 = [sb(f"r_S8_{i}", [128, DR]) for i in range(2)]
        S8b = [sb(f"r_S8b_{i}", [128, DR], BF16) for i in range(2)]
        S5 = [sb("r_S5", [128, DR])] * 2
        NB = [sb(f"r_NB_{i}", [128, DR], BF16) for i in range(2)]
        KH = sb("r_KH", [128, DR], BF16)
        vb = sb("r_vb", [128, DR], BF16)
        FMb = sb("r_FMb", [128, 6, 128], BF16)
        FMk = sb("r_FMk", [128, 6, 128], BF16)
        FMkq = sb("r_FMkq", [128, 6, 128], BF16)
        FMr = [sb(f"r_FMr{i}", [128, 6, 128], BF16) for i in range(2)]
        XC = sb("r_XC", [128, NH, 2, 128], BF16)
        BD = sb("r_BD", [128, NH, 2, 128], BF16)
        Yt = sb("r_Yt", [128, NH, 128], BF16)
        Zt = sb("r_Zt", [128, NH, 128], BF16)
        Zf = sb("r_Zf", [128, NH, 128])
        KqT = sb("r_KqT", [128, 6, 128], BF16)
        BU = sb("r_BU", [128, DR], BF16)
        Hb = [sb(f"r_Hb{i}", [128, 6, 64], BF16) for i in range(2)]
        UY = sb("r_UY", [128, DR])
        Gs = sb("r_G", [128, 6, 64])
        Ylocs = sb("r_Yloc", [128, DR])
        Hs = [sb(f"r_H{i}", [128, 6, 64]) for i in range(2)]
        tmpH = sb("r_tmpH", [128, 6, 64])
        pL = [sb(f"r_pL{i}", [128, 8]) for i in range(2)]
        stt_ = sb("r_st", [128, 4, NH])
        yft = [sb(f"r_yf{i}", [128, DR]) for i in range(2)]
        BON = [sb(f"r_bon{i}", [128, DR]) for i in range(2)]
        Gt = [sb(f"r_g{i}", [128, DR]) for i in range(2)]

        b_k = P.buf("r_const")
        b_zp, b_zn, b_lr, b_sg = P.bufs(4, "r_z")
        b_zcs = P.bufs(2, "r_zc")
        b_S = {i: P.buf(f"r_S{i}") for i in (1, 2, 3, 4, 6, 7)}
        b_S8 = P.bufs(2, "r_S8")
        b_S8b = P.bufs(2, "r_S8b")
        b_S5 = [P.buf("r_S5")] * 2
        b_NB = P.bufs(2, "r_NB")
        b_KH, b_vb = P.bufs(2, "r_khvb")
        b_Hb = P.bufs(2, "r_Hb")
        b_FMb, b_FMk, b_FMkq = P.bufs(3, "r_FM")
        b_FMr = P.bufs(2, "r_FMr")
        b_X = P.bufs(3, "r_X")
        b_nC = P.bufs(3, "r_nC")
        b_Bm = P.bufs(3, "r_Bm")
        b_Y = P.bufs(3, "r_Y")
        b_Z = P.bufs(3, "r_Z")
        b_Zf = P.bufs(3, "r_Zf")
        b_KqT, b_BU, b_UY, b_G, b_Yloc, b_tmpH = P.bufs(6, "r_m")
        b_pL = P.bufs(2, "r_pL")
        b_st = P.bufs(4, "r_st")
        b_yf = P.bufs(2, "r_yf")
        b_BON = P.bufs(2, "r_bon")
        b_Gt = P.bufs(2, "r_gt")
        b_H = P.bufs(2, "r_H")

        P.dma("sp", mu[:], C.e_mu[:, j * DA:(j + 1) * DA], w=[b_k])
        P.dma("sp", tab[:].rearrange("p a b -> p (a b)"), C.e_tab[:, j * TABW:(j + 1) * TABW], w=[b_k])
        for d in range(2):
            P.dma("sp", wa_up[0:64, d, :], C.e_w_up[j, d, :, :], w=[b_k])
            P.dma("sp", wa_up[64:128, d, :], C.e_a_up[j, d, :, :], w=[b_k])
        P.dma("sp", gup[:], C.e_g_up[j, :, :], w=[b_k])
        P.dma("sp", masks[:].rearrange("p a b -> p (a b)"), C.e_masks[:, :], w=[b_k])
        OP("dve", "memset", [], [b_k], ones[:], 1.0)
        OP("dve", "memset", [], [b_k], gneps[:], GN_EPS)

        psn = [0]

        def psget():
            b = psn[0] % 8
            psn[0] += 1
            return C.ps[b], C.bps[b]

        T_W0, T_A0, T_KK, T_KA, T_RK, T_GNW, T_GNB = 0, 2, 4, 5, 6, 7, 8
        def zviews(pb):
            zc = zcs[pb]
            return zc, b_zcs[pb], zc[:, 0:DR], zc[:, DR:2 * DR], zc[:, 2 * DR:3 * DR]
        v3 = lambda ap: ap.rearrange("p (h n) -> p h n", n=64)
        bc = lambda kk_: stt_[:, kk_, :].unsqueeze(2).to_broadcast([128, NH, 64])
        vh = lambda h: vb[:, h * 64:(h + 1) * 64]

        def gidx(seq, sc):
            return sc if seq == 0 else 2 + sc

        def P0c(d, seq, sc, pb):
            zc, b_zc, r_, k_, v_ = zviews(pb)
            for (tile_, bt, delta) in ((zc, b_zc, 0),):
                runs, clipped = row_runs(seq, sc, delta)
                if clipped:
                    OP("pool", "memset", [], [bt], tile_[:], 0.0)
                for (p0, p1, spec) in runs:
                    P.dma("sp", tile_[p0:p1, :], dram_rows(C.z_tok, spec, 0, DA), w=[bt])

        def P0y(d, seq, sc, pb):
            if d == 1:
                g = gidx(seq, sc)
                P.dma("sp", yft[pb][:], C.yf_tok[g * SC:(g + 1) * SC, :], r=[ybuf[g]], w=[b_yf[pb]])

        def P0pn(d, seq, sc, pb):
            for (tile_, bt, delta) in ((zp, b_zp, -1), (zn, b_zn, 1)):
                runs, clipped = row_runs(seq, sc, delta)
                if clipped:
                    OP("pool", "memset", [], [bt], tile_[:], 0.0)
                for (p0, p1, spec) in runs:
                    P.dma("sp", tile_[p0:p1, :], dram_rows(C.z_tok, spec, 0, DA), w=[bt])

        def P1(d, seq, sc, pb):
            zc, b_zc, r_, k_, v_ = zviews(pb)
            OP("dve", "tensor_tensor", [b_zp, b_zn], [b_zp], out=zp[:], in0=zp[:], in1=zn[:], op=ALU.add)
            OP("dve", "scalar_tensor_tensor", [b_zp, b_zc], [b_zp], out=zp[:], in0=zp[:], scalar=0.5, in1=zc[:], op0=ALU.mult, op1=ALU.subtract)
            OP("dve", "tensor_tensor", [b_zp, b_k], [b_zp], out=zp[:], in0=zp[:], in1=mu[:], op=ALU.mult)
            OP("dve", "tensor_tensor", [b_zc, b_zp], [b_zc], out=zc[:], in0=zc[:], in1=zp[:], op=ALU.add)
            pT, bT = psget()
            OP("pe", "transpose", [b_zc, C.b_const], [bT], pT[:, 0:128], zc[:, 2304:2432], C.ident_sb[:])
            OP("pe", "transpose", [b_zc, C.b_const], [bT], pT[:, 128:256], zc[:, 2432:2560], C.ident_sb[:])
            OP("act", "activation", [bT], [b_lr], out=lrT[0:64, :], in_=pT[0:64, 0:128], func=AF.Tanh)
            OP("act", "activation", [bT], [b_lr], out=lrT[64:128, :], in_=pT[64:128, 0:128], func=AF.Copy)
            if d == 1:
                OP("act", "activation", [bT], [b_sg], out=sgT[:], in_=pT[:, 128:256], func=AF.Sigmoid)

        def lowrank(dst, bdst, lhsT, rhs, rbufs, addtab):
            pa, ba = psget()
            pb_, bb = psget()
            OP("pe", "matmul", rbufs, [ba], pa[:, 0:512], lhsT=lhsT, rhs=rhs[:, 0:512], start=True, stop=True)
            OP("pe", "matmul", rbufs, [bb], pb_[:, 0:256], lhsT=lhsT, rhs=rhs[:, 512:768], start=True, stop=True)
            if addtab is None:
                OP("act", "activation", [ba], [bdst], out=dst[:, 0:512], in_=pa[:, 0:512], func=AF.Copy)
                OP("act", "activation", [bb], [bdst], out=dst[:, 512:768], in_=pb_[:, 0:256], func=AF.Copy)
            else:
                OP("dve", "tensor_tensor", [ba, b_k], [bdst], out=dst[:, 0:512], in0=pa[:, 0:512], in1=addtab[:, 0:512], op=ALU.add)
                OP("dve", "tensor_tensor", [bb, b_k], [bdst], out=dst[:, 512:768], in0=pb_[:, 0:256], in1=addtab[:, 512:768], op=ALU.add)

        def P2(d, seq, sc, pb):
            zc, b_zc, r_, k_, v_ = zviews(pb)
            lowrank(S[1], b_S[1], lrT[0:64, :], wa_up[0:64, d, :], [b_lr, b_k], tab[:, T_W0 + d, :])
            OP("act", "activation", [b_S[1]], [b_S[1]], out=S[1][:], in_=S[1][:], func=AF.Sigmoid)
            OP("dve", "tensor_scalar", [b_S[1]], [b_S[1]], out=S[1][:], in0=S[1][:], scalar1=-DECAY, scalar2=None, op0=ALU.mult)
            lowrank(S[2], b_S[2], lrT[64:128, :], wa_up[64:128, d, :], [b_lr, b_k], tab[:, T_A0 + d, :])
            OP("act", "activation", [b_S[2]], [b_S[2]], out=S[2][:], in_=S[2][:], func=AF.Sigmoid)
            if d == 1:
                lowrank(Gt[pb], b_Gt[pb], sgT[:], gup[:], [b_sg, b_k], None)
            OP("dve", "tensor_tensor", [b_zc, b_k], [b_S[3]], out=S[3][:], in0=k_, in1=tab[:, T_KK, :], op=ALU.mult)
            OP("dve", "tensor_tensor", [b_S[3]], [b_S[7]], out=S[7][:], in0=S[3][:], in1=S[3][:], op=ALU.mult)
            OP("dve", "tensor_reduce", [b_S[7]], [b_st[0]], out=stt_[:, 0, :], in_=v3(S[7][:]), axis=AX.X, op=ALU.add)
            OP("act", "activation", [b_st[0]], [b_st[0]], out=stt_[:, 0, :], in_=stt_[:, 0, :], func=AF.Sqrt)
            OP("dve", "tensor_scalar", [b_st[0]], [b_st[0]], out=stt_[:, 0, :], in0=stt_[:, 0, :], scalar1=1e-12, scalar2=None, op0=ALU.max)
            OP("dve", "reciprocal", [b_st[0]], [b_st[0]], out=stt_[:, 0, :], in_=stt_[:, 0, :])
            OP("dve", "tensor_tensor", [b_S[3], b_st[0]], [b_S[3]], out=v3(S[3][:]), in0=v3(S[3][:]), in1=bc(0), op=ALU.mult)
            OP("dve", "scalar_tensor_tensor", [b_S[2], b_k], [b_S[4]], out=S[4][:], in0=S[2][:], scalar=-1.0, in1=tab[:, T_KA, :], op0=ALU.add, op1=ALU.mult)
            OP("dve", "tensor_tensor", [b_S[4], b_zc], [b_S[4]], out=S[4][:], in0=S[4][:], in1=k_, op=ALU.mult)
            OP("dve", "tensor_tensor", [b_S[4], b_zc], [b_S[4]], out=S[4][:], in0=S[4][:], in1=k_, op=ALU.add)
            OP("dve", "tensor_tensor", [b_S[3], b_S[2]], [b_S5[pb]], out=S5[pb][:], in0=S[3][:], in1=S[2][:], op=ALU.mult)

        def P3(d, seq, sc, pb):
            zc, b_zc, r_, k_, v_ = zviews(pb)
            tri = masks[:, d, 640:768]
            pCa, bCa = psget()
            pCb, bCb = psget()
            OP("pe", "matmul", [b_k, b_S[1]], [bCa], pCa[:, 0:512], lhsT=tri, rhs=S[1][:, 0:512], start=True, stop=True)
            OP("pe", "matmul", [b_k, b_S[1]], [bCb], pCb[:, 0:256], lhsT=tri, rhs=S[1][:, 512:768], start=True, stop=True)
            pTa, bTa = psget()
            pTb, bTb = psget()
            OP("pe", "matmul", [b_k, b_S[1]], [bTa], pTa[:, 0:512], lhsT=ones[:], rhs=S[1][:, 0:512], start=True, stop=True)
            OP("pe", "matmul", [b_k, b_S[1]], [bTb], pTb[:, 0:256], lhsT=ones[:], rhs=S[1][:, 512:768], start=True, stop=True)
            pP, bP = psget()
            for pr in range(6):
                OP("pe", "matmul", [b_k, b_S[1]], [bP], pP[:, 2 * pr:2 * pr + 2], lhsT=S[1][:, pr * 128:(pr + 1) * 128], rhs=ones[:, 0:2], start=True, stop=True)
            OP("act", "activation", [bP], [b_pL[pb]], out=pL[pb][:, 0:6], in_=pP[:, 0:12].rearrange("p (a b) -> p a b", b=2)[:, :, 0], func=AF.Exp)
            OP("act", "activation", [bCa], [b_S[6]], out=S[6][:, 0:512], in_=pCa[:, 0:512], func=AF.Copy)
            OP("act", "activation", [bCb], [b_S[6]], out=S[6][:, 512:768], in_=pCb[:, 0:256], func=AF.Copy)
            OP("dve", "tensor_tensor", [b_S[6], b_S[1]], [b_S[7]], out=S[7][:], in0=S[6][:], in1=S[1][:], op=ALU.subtract)
            OP("act", "activation", [b_S[7]], [b_S[7]], out=S[7][:], in_=S[7][:], func=AF.Exp)
            OP("dve", "tensor_tensor", [b_S[3], b_S[7]], [b_S8[pb]], out=S8[pb][:], in0=S[3][:], in1=S[7][:], op=ALU.mult)
            OP("act", "activation", [b_S8[pb]], [b_S8b[pb]], out=S8b[pb][:], in_=S8[pb][:], func=AF.Copy)
            OP("act", "activation", [b_S[6]], [b_S[7]], out=S[7][:], in_=S[6][:], func=AF.Exp, scale=-1.0)
            OP("dve", "tensor_tensor", [b_S5[pb], b_S[7]], [b_S[2]], out=S[2][:], in0=S5[pb][:], in1=S[7][:], op=ALU.mult)
            OP("dve", "tensor_tensor", [b_S[4], b_S[7]], [b_S[3]], out=S[3][:], in0=S[4][:], in1=S[7][:], op=ALU.mult)
            OP("act", "activation", [b_S[6]], [b_S[7]], out=S[7][:], in_=S[6][:], func=AF.Exp)
            OP("dve", "tensor_tensor", [b_zc, b_S[7]], [b_S[1]], out=S[1][:], in0=r_, in1=S[7][:], op=ALU.mult)
            OP("dve", "tensor_tensor", [bTa, b_S[6]], [b_S[7]], out=S[7][:, 0:512], in0=pTa[:, 0:512], in1=S[6][:, 0:512], op=ALU.subtract)
            OP("dve", "tensor_tensor", [bTb, b_S[6]], [b_S[7]], out=S[7][:, 512:768], in0=pTb[:, 0:256], in1=S[6][:, 512:768], op=ALU.subtract)
            OP("act", "activation", [b_S[7]], [b_S[7]], out=S[7][:], in_=S[7][:], func=AF.Exp)
            OP("dve", "scalar_tensor_tensor", [b_S5[pb], b_S[7]], [b_NB[pb]], out=NB[pb][:], in0=S5[pb][:], scalar=-1.0, in1=S[7][:], op0=ALU.mult, op1=ALU.mult)
            OP("dve", "tensor_tensor", [b_S[4], b_S[7]], [b_KH], out=KH[:], in0=S[4][:], in1=S[7][:], op=ALU.mult)
            OP("act", "activation", [b_zc], [b_vb], out=vb[:], in_=v_, func=AF.Copy)

        def P4(d, seq, sc, pb):
            zc, b_zc, r_, k_, v_ = zviews(pb)
            for (src, bsrc, dst, bdst) in ((S[2], b_S[2], FMb, b_FMb), (S[3], b_S[3], FMk, b_FMk),
                                           (S8[pb], b_S8[pb], FMkq, b_FMkq), (S[1], b_S[1], FMr[pb], b_FMr[pb])):
                pa, ba = psget()
                pb_, bb = psget()
                for pr in range(4):
                    OP("pe", "transpose", [bsrc, C.b_const], [ba], pa[:, pr * 128:(pr + 1) * 128], src[:, pr * 128:(pr + 1) * 128], C.ident_sb[:])
                for pr in range(4, 6):
                    OP("pe", "transpose", [bsrc, C.b_const], [bb], pb_[:, (pr - 4) * 128:(pr - 3) * 128], src[:, pr * 128:(pr + 1) * 128], C.ident_sb[:])
                OP("act", "activation", [ba], [bdst], out=dst[:, 0:4, :], in_=pa[:, 0:512].rearrange("p (a b) -> p a b", b=128), func=AF.Copy)
                OP("act", "activation", [bb], [bdst], out=dst[:, 4:6, :], in_=pb_[:, 0:256].rearrange("p (a b) -> p a b", b=128), func=AF.Copy)
            if d == 1:
                lowrank(S[1], b_S[1], lrT[64:128, :], wa_up[64:128, 0, :], [b_lr, b_k], tab[:, T_A0 + 0, :])
                OP("act", "activation", [b_S[1]], [b_S[1]], out=S[1][:], in_=S[1][:], func=AF.Sigmoid)
                OP("dve", "scalar_tensor_tensor", [b_S[1], b_k], [b_S[1]], out=S[1][:], in0=S[1][:], scalar=-1.0, in1=tab[:, T_KA, :], op0=ALU.add, op1=ALU.mult)
                OP("dve", "tensor_tensor", [b_S[1], b_zc], [b_S[1]], out=S[1][:], in0=S[1][:], in1=k_, op=ALU.mult)
                OP("dve", "tensor_tensor", [b_S[1], b_zc], [b_S[1]], out=S[1][:], in0=S[1][:], in1=k_, op=ALU.add)
                OP("dve", "tensor_tensor", [b_S[1], b_S[4]], [b_S[1]], out=S[1][:], in0=S[1][:], in1=S[4][:], op=ALU.add)
                OP("dve", "tensor_tensor", [b_zc, b_k], [b_S[2]], out=S[2][:], in0=r_, in1=tab[:, T_RK, :], op=ALU.mult)
                OP("dve", "tensor_tensor", [b_S[2], b_S[1]], [b_S[2]], out=S[2][:], in0=S[2][:], in1=S[1][:], op=ALU.mult)
                OP("dve", "tensor_reduce", [b_S[2]], [b_st[3]], out=stt_[:, 3, :], in_=v3(S[2][:]), axis=AX.X, op=ALU.add)
                OP("dve", "tensor_tensor", [b_zc, b_st[3]], [b_BON[pb]], out=v3(BON[pb][:]), in0=v3(v_), in1=bc(3), op=ALU.mult)

        XCv = XC[:].rearrange("p (pr m) a b -> p pr m (a b)", m=2)
        BDv = BD[:].rearrange("p (pr m) a b -> p pr m (a b)", m=2)
        Ytv = Yt[:].rearrange("p (pr m) b -> p pr m b", m=2)
        X = XC[:, :, 0, :]

        def per_head_768(lhs_fn, rhs_fn, rbufs_fn, dst, bdst, addsrc=None, baddsrc=None):
            pa, ba = psget()
            pb_, bb = psget()
            for h in range(NH):
                (pp, bp, c0) = (pa, ba, h * 64) if h < 8 else (pb_, bb, (h - 8) * 64)
                OP("pe", "matmul", rbufs_fn(h), [bp], pp[:, c0:c0 + 64], lhsT=lhs_fn(h), rhs=rhs_fn(h), start=True, stop=True)
            if addsrc is None:
                OP("act", "activation", [ba], [bdst], out=dst[:, 0:512], in_=pa[:, 0:512], func=AF.Copy)
                OP("act", "activation", [bb], [bdst], out=dst[:, 512:768], in_=pb_[:, 0:256], func=AF.Copy)
            else:
                OP("dve", "tensor_tensor", [ba, baddsrc], [bdst], out=dst[:, 0:512], in0=pa[:, 0:512], in1=addsrc[:, 0:512], op=ALU.add)
                OP("dve", "tensor_tensor", [bb, baddsrc], [bdst], out=dst[:, 512:768], in0=pb_[:, 0:256], in1=addsrc[:, 512:768], op=ALU.add)

        def E(d, seq, sc, pb):
            mXC = masks[:, d, 0:256]
            mBD = masks[:, d, 256:512]
            mM2 = masks[:, d, 512:640]
            FMkr_b = [b_FMkq, b_FMr[pb]]
            for hq in range(3):
                for (FMx, bFM, msk, dstv, wb) in ((FMb, b_FMb, mXC, XCv, [b_X[hq], b_nC[hq]]), (FMk, b_FMk, mBD, BDv, [b_Bm[hq]])):
                    for m in range(2):
                        mb = slice(m * 64, (m + 1) * 64)
                        pa, ba = psget()
                        for q in range(2):
                            hp = 2 * hq + q
                            OP("pe", "matmul", [bFM] + FMkr_b, [ba], pa[:, q * 256:q * 256 + 128], lhsT=FMx[mb, hp, :], rhs=FMkq[mb, hp, :], start=True, stop=True)
                            OP("pe", "matmul", [bFM] + FMkr_b, [ba], pa[:, q * 256 + 128:(q + 1) * 256], lhsT=FMx[mb, hp, :], rhs=FMr[pb][mb, hp, :], start=True, stop=True)
                        OP("dve", "tensor_tensor", [ba, b_k], wb, out=dstv[:, 2 * hq:2 * hq + 2, m, :],
                           in0=pa[:, 0:512].rearrange("p (h x) -> p h x", x=256), in1=msk.unsqueeze(1).to_broadcast([128, 2, 256]), op=ALU.mult)
                for m in range(2):
                    mb = slice(m * 64, (m + 1) * 64)
                    pa, ba = psget()
                    for q in range(2):
                        pr = 2 * hq + q
                        OP("pe", "matmul", [b_FMkq, b_FMb], [ba], pa[:, q * 128:(q + 1) * 128], lhsT=FMkq[mb, pr, :], rhs=FMb[mb, pr, :], start=True, stop=True)
                    OP("dve", "tensor_tensor", [ba, b_k], [b_Y[hq]], out=Ytv[:, 2 * hq:2 * hq + 2, m, :], in0=pa[:, 0:256].rearrange("p (h x) -> p h x", x=128),
                       in1=mM2.unsqueeze(1).to_broadcast([128, 2, 128]), op=ALU.mult)
            for g in range(3):
                g4 = slice(4 * g, 4 * g + 4)
                OP("act", "activation", [b_X[g]], [b_Zf[g]], out=Zf[:, g4, :], in_=X[:, g4, :], func=AF.Copy)
                OP("dve", "tensor_tensor", [b_Zf[g], C.b_const], [b_Zf[g]], out=Zf[:, g4, :], in0=Zf[:, g4, :],
                   in1=C.ident_sb[:].unsqueeze(1).to_broadcast([128, 4, 128]), op=ALU.add)
                OP("act", "activation", [b_Zf[g]], [b_Z[g]], out=Zt[:, g4, :], in_=Zf[:, g4, :], func=AF.Copy)
            pG, bG = psget()
            for h in range(NH):
                pr, m = h // 2, h % 2
                mb = slice(m * 64, (m + 1) * 64)
                OP("pe", "matmul", [b_KH, b_vb], [bG], pG[mb, pr * 64:(pr + 1) * 64], lhsT=KH[:, h * 64:(h + 1) * 64], rhs=vh(h), start=True, stop=True)
            OP("act", "activation", [bG], [b_G], out=Gs[:], in_=pG[:, 0:384].rearrange("p (a b) -> p a b", b=64), func=AF.Copy)
            per_head_768(lambda h: BD[:, h, 0, :], vh, lambda h: [b_Bm[h // 4], b_vb], BU, b_BU)
            per_head_768(lambda h: BD[:, h, 1, :], vh, lambda h: [b_Bm[h // 4], b_vb], Ylocs, b_Yloc)

        def I_level(lv):
            last = (lv == SC // 2)
            for g in range(3):
                g4 = slice(4 * g, 4 * g + 4)
                if not last:
                    pX, bX = psget()
                    for hh in range(4):
                        h = 4 * g + hh
                        OP("pe", "matmul", [b_Y[g], b_X[g]], [bX], pX[:, hh * 128:(hh + 1) * 128], lhsT=Yt[:, h, :], rhs=X[:, h, :], start=True, stop=True)
                pY, bY = psget()
                for hh in range(4):
                    h = 4 * g + hh
                    OP("pe", "matmul", [b_Y[g], b_X[g]], [bY], pY[:, hh * 128:(hh + 1) * 128], lhsT=X[:, h, :], rhs=Yt[:, h, :], start=True, stop=True)
                if not last:
                    OP("act", "activation", [bX], [b_X[g]], out=X[:, g4, :], in_=pX[:, 0:512].rearrange("p (h x) -> p h x", x=128), func=AF.Copy)
                OP("dve", "tensor_copy", [bY], [b_Y[g]], out=Yt[:, g4, :], in_=pY[:, 0:512].rearrange("p (h x) -> p h x", x=128))
            for g in range(3):
                g4 = slice(4 * g, 4 * g + 4)
                pZ, bZ = psget()
                for hh in range(4):
                    h = 4 * g + hh
                    OP("pe", "matmul", [b_Y[g], b_Z[g]], [bZ], pZ[:, hh * 128:(hh + 1) * 128], lhsT=Yt[:, h, :], rhs=Zt[:, h, :], start=True, stop=True)
                OP("dve", "tensor_tensor", [bZ, b_Zf[g]], [b_Zf[g]], out=Zf[:, g4, :], in0=pZ[:, 0:512].rearrange("p (h x) -> p h x", x=128),
                   in1=Zf[:, g4, :], op=ALU.add)
                OP("act", "activation", [b_Zf[g]], [b_Z[g]], out=Zt[:, g4, :], in_=Zf[:, g4, :], func=AF.Copy)

        def Lphase(d, seq, sc, pb, cur):
            pa, ba = psget()
            pb_, bb = psget()
            for h in range(NH):
                pr, m = h // 2, h % 2
                mb = slice(m * 64, (m + 1) * 64)
                (pp, bp, c0) = (pa, ba, pr * 128) if pr < 4 else (pb_, bb, (pr - 4) * 128)
                OP("pe", "matmul", [b_S8b[pb], b_Z[h // 4]], [bp], pp[mb, c0:c0 + 128], lhsT=S8b[pb][:, h * 64:(h + 1) * 64], rhs=Zt[:, h, :], start=True, stop=True)
            OP("act", "activation", [ba], [b_KqT], out=KqT[:, 0:4, :], in_=pa[:, 0:512].rearrange("p (a b) -> p a b", b=128), func=AF.Copy)
            OP("act", "activation", [bb], [b_KqT], out=KqT[:, 4:6, :], in_=pb_[:, 0:256].rearrange("p (a b) -> p a b", b=128), func=AF.Copy)
            per_head_768(lambda h: Zt[:, h, :], lambda h: BU[:, h * 64:(h + 1) * 64], lambda h: [b_Z[h // 4], b_BU], UY, b_UY)
            Hc, bHc, Hn, bHn = Hs[cur], b_H[cur], Hs[1 - cur], b_H[1 - cur]
            Hbc, bHbc, Hbn, bHbn = Hb[cur], b_Hb[cur], Hb[1 - cur], b_Hb[1 - cur]

            def ph_rowtiled(lhs_fn, rbufs, dst, bdst, addsrc, baddsrc):
                dv = dst[:].rearrange("p (pr m n) -> p pr m n", m=2, n=64)
                av = addsrc[:].rearrange("p (pr m n) -> p pr m n", m=2, n=64)
                for m in range(2):
                    mb = slice(m * 64, (m + 1) * 64)
                    pa2, ba2 = psget()
                    for pr in range(6):
                        OP("pe", "matmul", rbufs + [bHbc], [ba2], pa2[:, pr * 64:(pr + 1) * 64], lhsT=lhs_fn(pr, mb), rhs=Hbc[mb, pr, :], start=True, stop=True)
                    OP("dve", "tensor_tensor", [ba2, baddsrc], [bdst], out=dv[:, :, m, :], in0=pa2[:, 0:384].rearrange("p (a b) -> p a b", b=64),
                       in1=av[:, :, m, :], op=ALU.add)

            OP("dve", "tensor_tensor", [bHc, b_pL[pb]], [b_tmpH], out=tmpH[:], in0=Hc[:], in1=pL[pb][:, 0:6].unsqueeze(2).to_broadcast([128, 6, 64]), op=ALU.mult)
            OP("dve", "tensor_tensor", [b_tmpH, b_G], [b_tmpH], out=tmpH[:], in0=tmpH[:], in1=Gs[:], op=ALU.add)
            ph_rowtiled(lambda pr, mb: KqT[mb, pr, :], [b_KqT], BU, b_BU, UY, b_UY)
            pH, bH = psget()
            for h in range(NH):
                pr, m = h // 2, h % 2
                mb = slice(m * 64, (m + 1) * 64)
                OP("pe", "matmul", [b_NB[pb], b_BU], [bH], pH[mb, pr * 64:(pr + 1) * 64], lhsT=NB[pb][:, h * 64:(h + 1) * 64], rhs=BU[:, h * 64:(h + 1) * 64], start=True, stop=True)
            OP("dve", "tensor_tensor", [bH, b_tmpH], [bHbn], out=Hbn[:], in0=pH[:, 0:384].rearrange("p (a b) -> p a b", b=64), in1=tmpH[:], op=ALU.add)
            OP("dve", "tensor_tensor", [bH, b_tmpH], [bHn], out=Hn[:], in0=pH[:, 0:384].rearrange("p (a b) -> p a b", b=64), in1=tmpH[:], op=ALU.add)
            ph_rowtiled(lambda pr, mb: FMr[pb][mb, pr, :], [b_FMr[pb]], UY, b_UY, Ylocs, b_Yloc)
            per_head_768(lambda h: XC[:, h, 1, :], lambda h: BU[:, h * 64:(h + 1) * 64], lambda h: [b_nC[h // 4], b_BU], UY, b_UY, UY, b_UY)
            ysb, b_y = UY, b_UY
            g = gidx(seq, sc)
            if d == 0:
                P.dma("sp", C.yf_tok[g * SC:(g + 1) * SC, :], ysb[:], r=[b_y], w=[ybuf[g]])
                return
            OP("dve", "tensor_tensor", [b_y, b_yf[pb]], [b_y], out=ysb[:], in0=ysb[:], in1=yft[pb][:], op=ALU.add)
            OP("dve", "tensor_reduce", [b_y], [b_st[1]], out=stt_[:, 1, :], in_=v3(ysb[:]), axis=AX.X, op=ALU.add)
            OP("dve", "tensor_scalar", [b_st[1]], [b_st[1]], out=stt_[:, 1, :], in0=stt_[:, 1, :], scalar1=-1.0 / 64, scalar2=None, op0=ALU.mult)
            OP("dve", "tensor_tensor", [b_y, b_st[1]], [b_y], out=v3(ysb[:]), in0=v3(ysb[:]), in1=bc(1), op=ALU.add)
            OP("dve", "tensor_tensor", [b_y], [b_Yloc], out=Ylocs[:], in0=ysb[:], in1=ysb[:], op=ALU.mult)
            OP("dve", "tensor_reduce", [b_Yloc], [b_st[2]], out=stt_[:, 2, :], in_=v3(Ylocs[:]), axis=AX.X, op=ALU.add)
            OP("act", "activation", [b_st[2], b_k], [b_st[2]], out=stt_[:, 2, :], in_=stt_[:, 2, :], func=AF.Sqrt, bias=gneps[:, 0:1], scale=1.0 / 64)
            OP("dve", "reciprocal", [b_st[2]], [b_st[2]], out=stt_[:, 2, :], in_=stt_[:, 2, :])
            OP("dve", "tensor_tensor", [b_y, b_st[2]], [b_y], out=v3(ysb[:]), in0=v3(ysb[:]), in1=bc(2), op=ALU.mult)
            OP("dve", "tensor_tensor", [b_y, b_k], [b_y], out=ysb[:], in0=ysb[:], in1=tab[:, T_GNW, :], op=ALU.mult)
            OP("dve", "tensor_tensor", [b_y, b_k], [b_y], out=ysb[:], in0=ysb[:], in1=tab[:, T_GNB, :], op=ALU.add)
            OP("dve", "tensor_tensor", [b_y, b_BON[pb]], [b_y], out=ysb[:], in0=ysb[:], in1=BON[pb][:], op=ALU.add)
            OP("dve", "tensor_tensor", [b_y, b_Gt[pb]], [b_y], out=ysb[:], in0=ysb[:], in1=Gt[pb][:], op=ALU.mult)
            runs, _ = row_runs(seq, sc, 0)
            for (p0, p1, spec) in runs:
                P.dma("sp", dram_rows(C.o_tok, spec, 0, DR), ysb[p0:p1, :], r=[b_y], w=[P.buf()])

        for d in range(2):
            OP("dve", "memset", [], [b_H[0]], Hs[0][:], 0.0)
            OP("dve", "memset", [], [b_Hb[0]], Hb[0][:], 0.0)
            order = [(0, 0), (0, 1)] + [(1, s) for s in range(SEQ // SC)]
            if d == 1:
                order = [(0, 1), (0, 0)] + [(1, s) for s in range(SEQ // SC - 1, -1, -1)]
            if RWKV_LIMIT:
                order = [(0, 0), (0, 1), (1, 0)] if d == 0 else [(0, 1), (0, 0), (1, 0)]
            seq0, sc0 = order[0]
            P0c(d, seq0, sc0, 0)
            P0y(d, seq0, sc0, 0)
            P0pn(d, seq0, sc0, 0)
            if len(order) > 1:
                P0c(d, order[1][0], order[1][1], 1)
                P0y(d, order[1][0], order[1][1], 1)
            P1(d, seq0, sc0, 0)
            if len(order) > 1:
                P0pn(d, order[1][0], order[1][1], 1)
            for piece in (P2, P3, P4):
                piece(d, seq0, sc0, 0)
            cur = 0
            for n, (seq, sc) in enumerate(order):
                pb = n % 2
                nxt = order[n + 1] if n + 1 < len(order) else None
                nx2 = order[n + 2] if n + 2 < len(order) else None
                nb = (n + 1) % 2
                if nx2:
                    P0c(d, nx2[0], nx2[1], pb)
                E(d, seq, sc, pb)
                I_level(2)
                if nxt:
                    P1(d, nxt[0], nxt[1], nb)
                if nx2:
                    P0pn(d, nx2[0], nx2[1], pb)
                I_level(4)
                if nxt:
                    P2(d, nxt[0], nxt[1], nb)
                I_level(8)
                I_level(16)
                if nxt:
                    P3(d, nxt[0], nxt[1], nb)
                I_level(32)
                I_level(64)
                Lphase(d, seq, sc, pb, cur)
                if nx2:
                    P0y(d, nx2[0], nx2[1], pb)
                if nxt:
                    P4(d, nxt[0], nxt[1], nb)
                cur = 1 - cur
        P.barrier()


def stage_even_C(C, P, L, tiles):
    nc = C.nc
    j = L // 2
    T = TT
    with ExitStack() as st:
        sb = mk_sb(nc, st)
        wout = sb("c_wout", [128, 8, D], BF16)
        b_wout = P.bufs(8, "cwout")
        for kc in range(8):
            P.dma("pool", wout[:, kc, :], C.e_w_out[j, kc * 128:(kc + 1) * 128, :], w=[b_wout[kc]])
        hs = [sb(f"ch{i}", [128, 8, T]) for i in range(2)]
        ot = [sb(f"cot{i}", [128, DR]) for i in range(2)]
        cat = [sb(f"ccat{i}", [128, 8, T], BF16) for i in range(2)]
        b_hs = [P.bufs(8, f"ch{i}_") for i in range(2)]
        b_ot = P.bufs(2, "cot")
        b_cat = [P.bufs(3, f"ccat{i}_") for i in range(2)]
        no = 0
        npb = 0
        for i, ti in enumerate(tiles):
            k = i % 2
            v = 1 if ti == 0 else 0
            h, bh, ct, bct = hs[k], b_hs[k], cat[k], b_cat[k]
            P.dma("sp", h[:], fm(C.hT, ti * T, T), r=[C.hbuf[ti]], w=bh)
            P.dma("pool", ct[:, 6:8, :], C.fT.rearrange("(c p) t -> p c t", p=128)[:, :, ti * T:(ti + 1) * T], w=[bct[2]])
            for tb in range(T // 128):
                o, bo = ot[no % 2], b_ot[no % 2]
                no += 1
                r0 = ti * T + tb * 128
                P.dma("sp", o[:], C.o_tok[r0:r0 + 128, :], w=[bo])
                pab, pbb = 1 + (npb % 3) * 2, 2 + (npb % 3) * 2
                npb += 1
                pa, pb = C.ps[pab], C.ps[pbb]
                for c in range(4):
                    P.op("pe", lambda e, c=c, pa=pa, o=o: e.transpose(pa[:, c * 128:(c + 1) * 128], o[:, c * 128:(c + 1) * 128], C.ident_sb[:]),
                         r=[bo, C.b_const], w=[C.bps[pab]])
                for c in range(4, 6):
                    P.op("pe", lambda e, c=c, pb=pb, o=o: e.transpose(pb[:, (c - 4) * 128:(c - 3) * 128], o[:, c * 128:(c + 1) * 128], C.ident_sb[:]),
                         r=[bo, C.b_const], w=[C.bps[pbb]])
                P.op("act", lambda e, pa=pa, ct=ct, tb=tb: e.activation(
                    out=ct[:, 0:4, tb * 128:(tb + 1) * 128], in_=pa[:, 0:512].rearrange("p (a b) -> p a b", b=128), func=AF.Copy),
                    r=[C.bps[pab]], w=[bct[0]])
                P.op("dve", lambda e, pb=pb, ct=ct, tb=tb: e.tensor_copy(
                    out=ct[:, 4:6, tb * 128:(tb + 1) * 128], in_=pb[:, 0:256].rearrange("p (a b) -> p a b", b=128)),
                    r=[C.bps[pbb]], w=[bct[1]])
            for dc in range(8):
                pob = 7 if dc % 2 == 0 else 0
                po = C.ps[pob]
                for kc in range(8):
                    P.op("pe", lambda e, dc=dc, kc=kc, po=po, ct=ct: e.matmul(
                        po[:, :T], lhsT=wout[:, kc, dc * 128:(dc + 1) * 128], rhs=ct[:, kc, :],
                        start=(kc == 0), stop=(kc == 7)), r=[b_wout[kc]] + bct, w=[C.bps[pob]])
                P.op("dve", lambda e, dc=dc, po=po, h=h, v=v: e.scalar_tensor_tensor(
                    out=h[:, dc, :], in0=po[:, :T], scalar=C.mG[:, L, 1, v, dc:dc + 1], in1=h[:, dc, :],
                    op0=ALU.mult, op1=ALU.add), r=[C.bps[pob], bh[dc], C.b_mod], w=[bh[dc]])
            P.dma("pool", fm(C.hT, ti * T, T), h[:], r=bh, w=[C.hbuf[ti]])
        P.barrier()


def prep_even(inp, shared):
    f = lambda a: np.ascontiguousarray(np.asarray(a, np.float32))
    shared["e_w_in"] = f(inp["e_w_in"])
    shared["e_w_out"] = f(inp["e_w_out"])
    rep = lambda v: np.broadcast_to(np.asarray(v, np.float32)[None, :], (128, np.asarray(v).shape[-1]))
    shared["e_mu"] = np.ascontiguousarray(np.concatenate([rep(inp["e_mu"][j]) for j in range(2)], axis=1))
    tabs = []
    for j in range(2):
        for v in (inp["e_w0"][j, 0], inp["e_w0"][j, 1], inp["e_a0"][j, 0], inp["e_a0"][j, 1], inp["e_k_k"][j], inp["e_k_a"][j],
                  inp["e_r_k"][j], inp["e_gn_w"][j], inp["e_gn_b"][j]):
            tabs.append(rep(v))
    shared["e_tab"] = np.ascontiguousarray(np.concatenate(tabs, axis=1))
    shared["e_w_up"] = f(inp["e_w_up"])
    shared["e_a_up"] = f(inp["e_a_up"])
    shared["e_g_up"] = f(inp["e_g_up"])
    idx = np.arange(SC)
    mk = []
    for d in range(2):
        be = (idx[:, None] <= idx[None, :]) if d == 0 else (idx[:, None] >= idx[None, :])
        strict = be & (idx[:, None] != idx[None, :])
        be = be.astype(np.float32)
        strict = strict.astype(np.float32)
        mk += [-strict, -be, strict, be, -strict.T, be]
    shared["e_masks"] = np.ascontiguousarray(np.concatenate(mk, axis=1))
    def dft(N):
        n = np.arange(N, dtype=np.int64)
        m = (n[:, None] * n[None, :]) % N
        ang = (2.0 * np.pi / N) * m.astype(np.float64)
        return np.cos(ang).astype(np.float32), np.sin(ang).astype(np.float32)
    cN, sN = dft(SEQ)
    shared["cosT"] = cN.astype(ml_dtypes.bfloat16)
    shared["sinT"] = sN.astype(ml_dtypes.bfloat16)
    cc, sc_ = dft(CTX)
    shared["cosc"] = cc.astype(ml_dtypes.bfloat16)
    shared["sinc"] = sc_.astype(ml_dtypes.bfloat16)
    c64, s64 = dft(64)
    cd = np.zeros((128, 4, 128), np.float32)
    for qi, (N, ) in enumerate(((SEQ,), (CTX,))):
        scale = 1.0 / np.sqrt(N * 64.0)
        for g in range(2):
            cd[g * 64:(g + 1) * 64, 2 * qi, g * 64:(g + 1) * 64] = c64 * scale
            cd[g * 64:(g + 1) * 64, 2 * qi + 1, g * 64:(g + 1) * 64] = -s64 * scale
    shared["cdft"] = np.ascontiguousarray(cd.reshape(128, -1))


def full_stages():
    stages = [("ada",)]
    for i in range(DEPTH):
        with_ctx = not (i == DEPTH - 1 and i % 2 == 1)
        stages.append(("ffn", i, 0, with_ctx))
        if i % 2 == 0:
            stages.append(("even", i))
        else:
            stages.append(("odd", i, with_ctx))
        stages.append(("ffn", i, 2, with_ctx))
    stages.append(("final",))
    return stages


def kernel(**inputs):
    ncores = 8
    nc = build_program(full_stages(), debug=False)
    maps = prep_inputs(inputs, ncores=ncores)
    res = run_bass_kernel_spmd(nc, maps, core_ids=list(range(ncores)))
    out = np.stack([np.ascontiguousarray(res.results[b]["outT"].T) for b in range(ncores)], axis=0)
    return out.astype(np.float32)
```
